# Optimizing a Trainium2 kernel written in Bass

```python
import jax, jax.numpy as jnp
from jax import lax
import numpy as np

D_MODEL = 1024
BATCH = 8
SEQ = 2048
DEPTH = 1
DEC_BATCH = 128
DEC_SEQ = 8
PAST_LEN = 16384
PAGE_SIZE = 128

MIX_WIDTH = D_MODEL
R_WIDTH = MIX_WIDTH // 2
G_WIDTH = MIX_WIDTH - R_WIDTH
N_R = 64
H_R = R_WIDTH // N_R
H_C = 8
N_C = G_WIDTH // H_C
CHUNK = 128
W_RANK = 64
A_RANK = 64
G_RANK = 128
R_COLS = 3 * R_WIDTH + W_RANK + A_RANK + G_RANK
P_COLS = R_COLS + 2 * G_WIDTH
R_SPLITS = (R_WIDTH, 2 * R_WIDTH, 3 * R_WIDTH, 3 * R_WIDTH + W_RANK, 3 * R_WIDTH + W_RANK + A_RANK)
D_FF = ((8 * D_MODEL // 3 + 127) // 128) * 128
N_MOD = 9
RMS_EPS = 1e-6
LN_EPS = 1e-5
GN_EPS = 64e-5

kernel_name = 'rwkv7_gmlp_hymba_macaron_adaln_step'


def _rms(x):
    xf = x.astype(jnp.float32)
    return (xf * lax.rsqrt(jnp.mean(xf * xf, -1, keepdims=True) + RMS_EPS)).astype(x.dtype)


def _swiglu(h, w_gu, w_dn):
    g, u = jnp.split(h @ w_gu, 2, axis=-1)
    return (jax.nn.silu(g) * u) @ w_dn


def _wkv_scan(r, w, k, v, aa, bb, s0):
    def step(s, inp):
        r_t, w_t, k_t, v_t, a_t, b_t = inp
        sa = jnp.einsum('bhvk,bhk->bhv', s, a_t)
        s = s * w_t[:, :, None, :] + sa[..., None] * b_t[:, :, None, :] + v_t[..., None] * k_t[:, :, None, :]
        return s, jnp.einsum('bhvk,bhk->bhv', s, r_t)
    xs = tuple(jnp.moveaxis(t, 1, 0) for t in (r, w, k, v, aa, bb))
    s, ys = lax.scan(step, s0, xs)
    return jnp.moveaxis(ys, 0, 1), s


def _chunk_mix(vn, w_s, b_s):
    B, T = vn.shape[0], vn.shape[1]
    n_chunks = -(-T // CHUNK)
    pad = n_chunks * CHUNK - T
    vp = jnp.pad(vn, ((0, 0), (0, pad), (0, 0), (0, 0))).reshape(B, n_chunks, CHUNK, H_C, N_C)
    ws = jnp.where(jnp.tril(jnp.ones((CHUNK, CHUNK), bool)), w_s, jnp.zeros_like(w_s))
    mixed = jnp.einsum('hij,bcjhd->bcihd', ws, vp) + b_s.T[None, None, :, :, None]
    return mixed.reshape(B, n_chunks * CHUNK, H_C, N_C)[:, :T]


def _layer(x, c, shift0, wkv0, w_ada, b_ada, ffn1_gu, ffn1_dn, w_in, mu_shift, w0, w_lora_up, a0,
           a_lora_up, g_lora_up, k_k, k_a, r_k, gn_w, gn_b, ln_v_g, ln_v_b, w_s, b_s, w_out,
           ffn2_gu, ffn2_dn):
    B, T, _ = x.shape
    dt = x.dtype
    mod = jax.nn.silu(c) @ w_ada + b_ada
    sh1, sc1, g1, sh2, sc2, g2, sh3, sc3, g3 = [m[:, None, :] for m in jnp.split(mod, N_MOD, axis=-1)]

    h = x + 0.5 * g1 * _swiglu(_rms(x) * (1 + sc1) + sh1, ffn1_gu, ffn1_dn)

    n = _rms(h) * (1 + sc2) + sh2
    p = n @ w_in
    p_r = p[..., :R_COLS]
    p_u = p[..., R_COLS:R_COLS + G_WIDTH]
    p_v = p[..., R_COLS + G_WIDTH:]

    prev = jnp.concatenate([shift0[:, None, :].astype(dt), p_r[:, :-1]], axis=1)
    xm = p_r + (prev - p_r) * mu_shift
    r, k, v, wd, ad, gd = jnp.split(xm, R_SPLITS, axis=-1)
    w_log = -jax.nn.softplus(-(w0 + jnp.tanh(wd) @ w_lora_up)) - 0.5
    decay = jnp.exp(-jnp.exp(w_log.astype(jnp.float32)))
    a = jax.nn.sigmoid(a0 + ad @ a_lora_up)
    g = jax.nn.sigmoid(gd) @ g_lora_up
    heads = lambda t: t.reshape(B, T, H_R, N_R)
    kk = heads(k * k_k).astype(jnp.float32)
    kk = kk / jnp.maximum(jnp.sqrt(jnp.sum(kk * kk, -1, keepdims=True)), 1e-12)
    k = k * (1 + (a - 1) * k_a)
    r_h, k_h, v_h, a_h = heads(r), heads(k), heads(v), heads(a)
    f32 = lambda t: t.astype(jnp.float32)
    y, s_new = _wkv_scan(f32(r_h), heads(decay), f32(k_h), f32(v_h), -kk, kk * f32(a_h),
                         wkv0.astype(jnp.float32))
    mu = jnp.mean(y, -1, keepdims=True)
    var = jnp.mean(jnp.square(y - mu), -1, keepdims=True)
    y = ((y - mu) * lax.rsqrt(var + GN_EPS)).reshape(B, T, R_WIDTH).astype(dt) * gn_w + gn_b
    y = y + (jnp.sum(r_h * k_h * r_k, -1, keepdims=True) * v_h).reshape(B, T, R_WIDTH)
    y_r = y * g

    pv = p_v.astype(jnp.float32)
    pm = jnp.mean(pv, -1, keepdims=True)
    pvar = jnp.mean(jnp.square(pv - pm), -1, keepdims=True)
    vn = ((pv - pm) * lax.rsqrt(pvar + LN_EPS)).astype(dt) * ln_v_g + ln_v_b
    mixed = _chunk_mix(vn.reshape(B, T, H_C, N_C), w_s, b_s).reshape(B, T, G_WIDTH)
    y_c = p_u * mixed

    h = h + g2 * (jnp.concatenate([y_r, y_c], axis=-1) @ w_out)

    h = h + 0.5 * g3 * _swiglu(_rms(h) * (1 + sc3) + sh3, ffn2_gu, ffn2_dn)
    return h, p_r[:, -1], s_new.astype(wkv0.dtype), vn


def setup_inputs(seed: int = 0) -> dict:
    key = jax.random.key(seed)
    ks = iter(jax.random.split(key, 40))
    nrm = lambda shape, s: jax.random.normal(next(ks), shape, jnp.float32) * s
    L, D = DEPTH, D_MODEL
    return {
        'x_prompt': nrm((BATCH, SEQ, D), 1.0),
        'x_sample': nrm((DEC_BATCH, DEC_SEQ, D), 1.0),
        'state_shift': nrm((L, DEC_BATCH, R_COLS), 1.0),
        'state_wkv': nrm((L, DEC_BATCH, H_R, N_R, N_R), 0.5),
        'c_prompt': nrm((BATCH, D), 1.0),
        'c_sample': nrm((DEC_BATCH, D), 1.0),
        'w_ada': nrm((L, D, N_MOD * D), D ** -0.5),
        'b_ada': nrm((L, N_MOD * D), 0.01),
        'ffn1_gu': nrm((L, D, 2 * D_FF), D ** -0.5),
        'ffn1_dn': nrm((L, D_FF, D), D_FF ** -0.5),
        'w_in': nrm((L, D, P_COLS), D ** -0.5),
        'mu_shift': jax.random.uniform(next(ks), (L, R_COLS), jnp.float32),
        'w0': jax.random.uniform(next(ks), (L, R_WIDTH), jnp.float32, minval=-6.0, maxval=1.0),
        'w_lora_up': nrm((L, W_RANK, R_WIDTH), 0.1),
        'a0': nrm((L, R_WIDTH), 0.1),
        'a_lora_up': nrm((L, A_RANK, R_WIDTH), 0.1),
        'g_lora_up': nrm((L, G_RANK, R_WIDTH), G_RANK ** -0.5),
        'k_k': 0.85 + nrm((L, R_WIDTH), 0.02),
        'k_a': 1.0 + nrm((L, R_WIDTH), 0.02),
        'r_k': nrm((L, H_R, N_R), 0.1),
        'gn_w': 1.0 + nrm((L, R_WIDTH), 0.02),
        'gn_b': nrm((L, R_WIDTH), 0.01),
        'ln_v_g': 1.0 + nrm((L, G_WIDTH), 0.02),
        'ln_v_b': nrm((L, G_WIDTH), 0.01),
        'w_s': nrm((L, H_C, CHUNK, CHUNK), CHUNK ** -0.5),
        'b_s': 1.0 + nrm((L, H_C, CHUNK), 0.02),
        'w_out': nrm((L, MIX_WIDTH, D), MIX_WIDTH ** -0.5),
        'ffn2_gu': nrm((L, D, 2 * D_FF), D ** -0.5),
        'ffn2_dn': nrm((L, D_FF, D), D_FF ** -0.5),
        'final_g': 1.0 + nrm((D,), 0.02),
    }


def reference(x_prompt, x_sample, state_shift, state_wkv, c_prompt, c_sample, w_ada, b_ada, ffn1_gu,
              ffn1_dn, w_in, mu_shift, w0, w_lora_up, a0, a_lora_up, g_lora_up, k_k, k_a, r_k, gn_w,
              gn_b, ln_v_g, ln_v_b, w_s, b_s, w_out, ffn2_gu, ffn2_dn, final_g):
    xp, xs = x_prompt, x_sample
    shp, wkp, shs, wks, cvs = [], [], [], [], []
    for l in range(DEPTH):
        lw = (w_ada[l], b_ada[l], ffn1_gu[l], ffn1_dn[l], w_in[l], mu_shift[l], w0[l], w_lora_up[l],
              a0[l], a_lora_up[l], g_lora_up[l], k_k[l], k_a[l], r_k[l], gn_w[l], gn_b[l], ln_v_g[l],
              ln_v_b[l], w_s[l], b_s[l], w_out[l], ffn2_gu[l], ffn2_dn[l])
        zero_shift = jnp.zeros((xp.shape[0], R_COLS), xp.dtype)
        zero_wkv = jnp.zeros((xp.shape[0], H_R, N_R, N_R), state_wkv.dtype)
        xp, sh_p, wk_p, _ = _layer(xp, c_prompt, zero_shift, zero_wkv, *lw)
        xs, sh_s, wk_s, v_s = _layer(xs, c_sample, state_shift[l], state_wkv[l], *lw)
        shp.append(sh_p); wkp.append(wk_p); shs.append(sh_s); wks.append(wk_s); cvs.append(v_s)
    y_prompt = _rms(xp) * final_g
    y_sample = _rms(xs) * final_g
    return (y_prompt, y_sample, jnp.stack(shp), jnp.stack(wkp), jnp.stack(shs), jnp.stack(wks), jnp.stack(cvs))
```

```python
import contextlib
import numpy as np
import concourse.bass as bass
import concourse.mybir as mybir
from concourse.bass_utils import run_bass_kernel_spmd

F32 = mybir.dt.float32
BF16 = mybir.dt.bfloat16
AF = mybir.ActivationFunctionType
ALU = mybir.AluOpType

NCORES = 8
D = 1024
DFF = 2816
FC = 22
SEQ = 2048
NSEQ_S = 16
TL_S = 8
TS = NSEQ_S * TL_S
NTOK = SEQ + TS
TP = 512
R_COLS = 1792
N_MOD = 9
SCAN_SUB = [None]
DBGF = set()
W_C = float(np.exp(-0.5))

ENGS = ("pe", "act", "dve", "pool", "sp")
SAME_ENGINE_SYNC = {"pe": False, "act": False, "dve": False, "pool": False, "sp": False}


class Op:
    __slots__ = ("eng", "fn", "deps", "dma_key", "dma_val", "needs_inc", "seq", "idx")

    def __init__(self, eng, fn, dma_key=None):
        self.eng = eng
        self.fn = fn
        self.deps = []
        self.dma_key = dma_key
        self.dma_val = 0
        self.needs_inc = False
        self.seq = 0


class Sched:
    def __init__(self, nc):
        self.nc = nc
        self.ops = {e: [] for e in ENGS}
        self.last_w = {}
        self.readers = {}
        self.dma_cnt = {}
        self.n = 0

    def op(self, eng, fn, reads=(), writes=(), dma_key=None):
        o = Op(eng, fn, dma_key)
        o.idx = self.n
        self.n += 1
        deps = {}
        for r in reads:
            w = self.last_w.get(r)
            if w is not None:
                deps[id(w)] = w
        for r in writes:
            w = self.last_w.get(r)
            if w is not None:
                deps[id(w)] = w
            for rd in self.readers.get(r, ()):
                deps[id(rd)] = rd
        o.deps = list(deps.values())
        for r in reads:
            lst = self.readers.setdefault(r, [])
            if dma_key is None:
                for k_ in range(len(lst)):
                    if lst[k_].dma_key is None and lst[k_].eng == eng:
                        lst[k_] = o
                        break
                else:
                    lst.append(o)
            else:
                lst.append(o)
        for r in writes:
            self.last_w[r] = o
            self.readers[r] = []
        if dma_key is not None:
            self.dma_cnt[dma_key] = self.dma_cnt.get(dma_key, 0) + 16
            o.dma_val = self.dma_cnt[dma_key]
        self.ops[eng].append(o)
        return o

    def dma(self, eng, out, in_, reads=(), writes=(), key=None):
        assert key is not None
        return self.op(eng, lambda e: e.dma_start(out=out, in_=in_), reads, writes, dma_key=key)

    def emit(self, final_keys=()):
        nc = self.nc
        for e in ENGS:
            for o in self.ops[e]:
                for d in o.deps:
                    if d.dma_key is not None:
                        continue
                    if d.eng == o.eng and not SAME_ENGINE_SYNC[o.eng]:
                        continue
                    d.needs_inc = True
        for e in ENGS:
            c = 0
            for o in self.ops[e]:
                if o.dma_key is None and o.needs_inc:
                    c += 1
                    o.seq = c
        with contextlib.ExitStack() as st:
            esem = {e: st.enter_context(nc.semaphore("s_" + e)) for e in ENGS}
            dsem = {k: st.enter_context(nc.semaphore("d_" + str(k))) for k in self.dma_cnt}
            block = st.enter_context(nc.Block())
            engobj = {"pe": "tensor", "act": "scalar", "dve": "vector", "pool": "gpsimd", "sp": "sync"}

            def make(e):
                def body(eng):
                    waited = {}
                    for o in self.ops[e]:
                        need = {}
                        for d in o.deps:
                            if d.dma_key is not None:
                                key, val, sem = ("d", d.dma_key), d.dma_val, dsem[d.dma_key]
                            else:
                                if d.eng == e and not SAME_ENGINE_SYNC[e]:
                                    continue
                                key, val, sem = ("e", d.eng), d.seq, esem[d.eng]
                            if key not in need or need[key][0] < val:
                                need[key] = (val, sem)
                        for key, (val, sem) in need.items():
                            if waited.get(key, 0) >= val:
                                continue
                            waited[key] = val
                            eng.wait_ge(sem, val)
                        ins = o.fn(eng)
                        if o.dma_key is not None:
                            ins.then_inc(dsem[o.dma_key], 16)
                        elif o.needs_inc:
                            ins.then_inc(esem[e], 1)
                    if e == "sp":
                        for k in final_keys:
                            if k in self.dma_cnt:
                                eng.wait_ge(dsem[k], self.dma_cnt[k])
                return body

            for e in ENGS:
                getattr(block, engobj[e])(make(e))


VEC_SPECS = [("b_ada", 72), ("mu", 14), ("w0", 4), ("a0", 4), ("k_k", 4), ("k_a", 4), ("r_k", 4),
             ("gn_w", 4), ("gn_b", 4), ("ln_g", 4), ("ln_b", 4), ("final_g", 8)]
VOFF = {}
_o = 0
for _n, _k in VEC_SPECS:
    VOFF[_n] = _o
    _o += _k
NV = _o
VD = {"onem_mu": NV, "half_w0": NV + 14, "half_a0": NV + 18, "onem_ka": NV + 22}
NVT = NV + 26


class TileCtx:
    def __init__(self, idx, col0, nseq, tl, C, sample, last_prompt):
        self.idx = idx
        self.col0 = col0
        self.nseq = nseq
        self.tl = tl
        self.T = nseq * tl
        self.C = C
        self.sample = sample
        self.last_prompt = last_prompt


def build_program(stop=None, dbg=False):
    nc = bass.Bass("TRN2", target_bir_lowering=False)

    def din(name, shape, dt=F32):
        return nc.dram_tensor(name, list(shape), dt, kind="ExternalInput").ap()

    def dout(name, shape, dt=F32):
        return nc.dram_tensor(name, list(shape), dt, kind="ExternalOutput").ap()

    def dscr(name, shape, dt=BF16):
        return nc.dram_tensor(name, list(shape), dt, kind="Internal").ap()

    xT_d = din("xT", [D, NTOK])
    cT_d = din("cT", [128, 8, 17])
    vec_d = din("vecs", [128, NV])
    wada_d = din("w_ada", [D, N_MOD * D])
    gu_d = din("gu_h", [2, FC, 128, 2048])
    dn_d = din("dn_h", [2, 8, 128, DFF])
    win_d = din("win_h", [FC, 128, D])
    wout_d = din("wout_h", [8, 128, D])
    lora_d = din("lora_h", [128, 1536])
    wsT_d = din("wsT_h", [128, 8, 128])
    rep_d = din("rep_h", [128, 8, 8])
    bsb_d = din("bsb_h", [128, 4, 128])
    ssh_d = din("sshT_h", [128, 14, NSEQ_S])
    wkv_d = din("wkv_h", [128, NSEQ_S, 4, 64])

    yT_o = dout("yT", [D, NTOK])
    sh_o = dout("shT_o", [128, 14, 17])
    wkv_o = dout("wkv_o", [128, 17, 4, 64])
    cv_o = dout("cvT_o", [128, 4, TS])

    if dbg:
        dbg_y = dout("dbg_y", [128, 4, NTOK])
        dbg_xm = dout("dbg_xm", [128, 14, NTOK])
        dbg_lw = dout("dbg_lw", [128, 4, NTOK])
        dbg_a = dout("dbg_a", [128, 4, NTOK])
        dbg_kkn = dout("dbg_kkn", [128, 4, NTOK])
    gu_s = dscr("gu_s", [2, FC, 128, 2048])
    dn_s = dscr("dn_s", [2, 8, 128, DFF])
    win_s = dscr("win_s", [FC, 128, D])
    wout_s = dscr("wout_s", [8, 128, D])

    st = contextlib.ExitStack()
    with st:
        def sb(name, shape, dt):
            return st.enter_context(nc.sbuf_tensor(name, list(shape), dt))

        xt = sb("xt", [128, 8, TP], F32)
        n_ = sb("n_", [128, 8, TP], BF16)
        big = sb("big", [128, 14 * TP], F32)
        slots = [sb("slot%d" % i, [128, 3072], BF16) for i in range(3)]
        P1 = sb("P1", [128, 4, TP], F32)
        P2 = sb("P2", [128, 4, TP], F32)
        P3 = sb("P3", [128, 4, TP], F32)
        P4 = sb("P4", [128, 4, TP], F32)
        G_ = sb("G_", [128, 4, TP], F32)
        ar = sb("ar", [128, 4, 2 * TP], BF16)
        bt = sb("bt", [128, 4, TP], BF16)
        kt = sb("kt", [128, 4, TP], BF16)
        vb = sb("vb", [128, 4, TP], BF16)
        bkT_sb = sb("bkT_sb", [64, 2, 4, 128], BF16)
        vT_sb = sb("vT_sb", [64, 4, 128], BF16)
        vnT_sb = sb("vnT_sb", [128, 4, 128], BF16)
        Pbk_sb = sb("Pbk_sb", [64, 8, 2, 2, 64], BF16)
        A_sb = [sb("A_sb%d" % i, [64, 8, 64], BF16) for i in range(2)]
        N_sb = [sb("N_sb%d" % i, [64, 8, 64], BF16) for i in range(2)]
        Q_sb = [sb("Q_sb%d" % i, [64, 8, 64], BF16) for i in range(2)]
        A_all = sb("A_all", [64, 4, 8, 64], BF16)
        N_all = sb("N_all", [64, 4, 8, 64], BF16)
        D_sb = [sb("D_sb%d" % i, [64, 8, 64], BF16) for i in range(2)]
        msk4_A = sb("msk4_A", [64, 4, 64], F32)
        msk4_N = sb("msk4_N", [64, 4, 64], BF16)
        msk4_Nf = sb("msk4_Nf", [64, 4, 64], F32)
        Xs = sb("Xs", [64, 4, 128], BF16)
        Us = sb("Us", [64, 4, 128], BF16)
        NMB = 2
        M_pad = [sb("M_pad%d" % i, [128, 4, 128], F32) for i in range(NMB)]
        Mb_pad = [sb("Mb_pad%d" % i, [128, 4, 128], BF16) for i in range(NMB)]
        onesf = sb("onesf", [128, 128], F32)
        ident_f = sb("ident_f", [128, 128], F32)
        ident_bf = sb("ident_bf", [128, 128], BF16)
        onesD_bf = sb("onesD_bf", [128, 128], BF16)
        ones512_f = sb("ones512_f", [128, 128], F32)
        blk64m_f = sb("blk64m_f", [128, 128], F32)
        blk64m_bf = sb("blk64m_bf", [128, 128], BF16)
        blk64o_bf = sb("blk64o_bf", [128, 128], BF16)
        msk_ar = sb("msk_ar", [64, 2, 64], F32)
        m_A = sb("m_A", [64, 64], F32)
        rm64 = sb("rm64", [128, TP], F32)
        rm8 = sb("rm8", [128, TS], F32)
        cmk = sb("cmk", [128, 16, 8], F32)
        cst = sb("cst", [128, 4], F32)
        WsT = sb("WsT", [128, 8, 128], BF16)
        BsT = sb("BsT", [128, 8, 128], BF16)
        rep_f = sb("rep_f", [128, 8, 8], F32)
        bsb = sb("bsb", [128, 4, 128], F32)
        lora_bf = sb("lora_bf", [128, 1536], BF16)
        vec = sb("vec_sb", [128, NVT], F32)
        mod = sb("mod", [128, 72, 17], F32)
        cT = sb("cT_sb", [128, 8, 17], F32)
        cs_bf = sb("cs_bf", [128, 8, 17], BF16)
        carry = sb("carry", [128, 14, 1], F32)
        sshT = sb("sshT", [128, 14, NSEQ_S], F32)
        shout = sb("shout", [128, 14, 17], F32)
        th = [sb("th%d" % i, [128, TP], F32) for i in range(2)]
        sq = [sb("sq%d" % i, [128, TP], BF16) for i in range(2)]
        lin_bf, sg_bf = sq[0], sq[1]
        rstd = sb("rstd", [128, TP], F32)
        t1 = [sb("t1_%d" % i, [128, TP], F32) for i in range(2)]
        tA = sb("tA", [128, TP], F32)
        tB = sb("tB", [128, TP], F32)
        ps = st.enter_context(nc.psum_tensor("ps", [128, 8, 512], F32))

        S = Sched(nc)
        bank_ctr = [0]

        def nb():
            b = bank_ctr[0] % 8
            bank_ctr[0] += 1
            return b

        def PB(b):
            return "ps%d" % b

        def mm(out, lhsT, rhs, start, stop, r, w):
            S.op("pe", lambda e: e.matmul(out, lhsT=lhsT, rhs=rhs, start=start, stop=stop), r, w)

        def tr(out, in_, r, w):
            S.op("pe", lambda e: e.transpose(out=out, in_=in_, identity=ident_bf[:]), list(r) + ["ident_bf"], w)

        def act(out, in_, func, r, w, bias=None, scale=1.0):
            if bias is None:
                S.op("act", lambda e: e.activation(out=out, in_=in_, func=func, scale=scale), r, w)
            else:
                S.op("act", lambda e: e.activation(out=out, in_=in_, func=func, bias=bias, scale=scale), r, w)

        def cp(eng, out, in_, r, w):
            if eng == "act":
                S.op("act", lambda e: e.copy(out=out, in_=in_), r, w)
            else:
                S.op(eng, lambda e: e.tensor_copy(out=out, in_=in_), r, w)

        def tt(eng, out, in0, in1, op, r, w):
            S.op(eng, lambda e: e.tensor_tensor(out=out, in0=in0, in1=in1, op=op), r, w)

        def ts(eng, out, in0, s1, s2, op0, op1, r, w):
            if s2 is None:
                S.op(eng, lambda e: e.tensor_scalar(out=out, in0=in0, scalar1=s1, scalar2=None, op0=op0), r, w)
            else:
                S.op(eng, lambda e: e.tensor_scalar(out=out, in0=in0, scalar1=s1, scalar2=s2, op0=op0, op1=op1), r, w)

        def stt(eng, out, in0, scalar, in1, op0, op1, r, w):
            S.op(eng, lambda e: e.scalar_tensor_tensor(out=out, in0=in0, scalar=scalar, in1=in1, op0=op0, op1=op1), r, w)

        def memset(eng, ap, val, w):
            S.op(eng, lambda e: e.memset(ap, val), (), w)

        def asel(out, in_, pattern, op, base, cm, r, w):
            S.op("pool", lambda e: e.affine_select(out=out, in_=in_, pattern=pattern, compare_op=op, fill=0.0,
                                                   base=base, channel_multiplier=cm), r, w)

        def V(name, c=0, n=1):
            o = VOFF[name] if name in VOFF else VD[name]
            return vec[:, o + c:o + c + n]

        S.dma("sp", vec[:, 0:NV], vec_d, writes=["vec"], key="k_vec")
        S.dma("sp", cT[:], cT_d, writes=["cT"], key="k_cT")
        memset("pool", onesf[:], 1.0, ["onesf"])
        asel(ident_f[:], onesf[:], [[1, 128]], ALU.is_equal, 0, -1, ["onesf"], ["ident_f"])
        cp("pool", ident_bf[:], ident_f[:], ["ident_f"], ["ident_bf"])
        memset("pool", onesD_bf[:], 1.0 / D, ["onesD_bf"])
        memset("pool", ones512_f[:], 1.0 / 512, ["ones512_f"])
        for (t_, v_) in ((blk64m_f, 1.0 / 64), (blk64m_bf, 1.0 / 64), (blk64o_bf, 1.0)):
            memset("pool", t_[:], 0.0, [t_.name])
            memset("pool", t_[0:64, 0:64], v_, [t_.name])
            memset("pool", t_[64:128, 64:128], v_, [t_.name])
        asel(msk_ar[:, 0, :], onesf[0:64, 0:64], [[1, 64]], ALU.is_gt, 0, -1, ["onesf"], ["msk_ar"])
        asel(msk_ar[:, 1, :], onesf[0:64, 0:64], [[1, 64]], ALU.is_ge, 0, -1, ["onesf"], ["msk_ar"])
        asel(m_A[:], onesf[0:64, 0:64], [[-1, 64]], ALU.is_gt, 0, 1, ["onesf"], ["m_A"])
        o64 = onesf[0:64, 0:64]
        vA = msk4_A[:, 0, :].rearrange("p (a b) -> p a b", b=8)
        asel(msk4_A[:, 0, :], o64, [[-1, 64]], ALU.is_gt, 0, 1, ["onesf"], ["msk4_A"])
        asel(vA, vA, [[-8, 8], [0, 8]], ALU.is_ge, 0, 1, ["msk4_A"], ["msk4_A"])
        asel(vA, vA, [[8, 8], [0, 8]], ALU.is_ge, 7, -1, ["msk4_A"], ["msk4_A"])
        vN = msk4_Nf[:, 0, :].rearrange("p (a b) -> p a b", b=8)
        asel(msk4_Nf[:, 0, :], o64, [[1, 64]], ALU.is_gt, 0, -1, ["onesf"], ["msk4_Nf"])
        asel(vN, vN, [[-8, 8], [0, 8]], ALU.is_ge, 0, 1, ["msk4_Nf"], ["msk4_Nf"])
        asel(vN, vN, [[8, 8], [0, 8]], ALU.is_ge, 7, -1, ["msk4_Nf"], ["msk4_Nf"])
        for v_, m_ in ((1, 8), (2, 16), (3, 32)):
            nb_ = 64 // (2 * m_)
            wA = msk4_A[:, v_, :].rearrange("p (a b) -> p a b", b=2 * m_)
            memset("pool", msk4_A[:, v_, :], 1.0, ["msk4_A"])
            asel(wA, wA, [[-2 * m_, nb_], [0, 2 * m_]], ALU.is_ge, -m_, 1, ["msk4_A"], ["msk4_A"])
            asel(wA, wA, [[2 * m_, nb_], [0, 2 * m_]], ALU.is_ge, 2 * m_ - 1, -1, ["msk4_A"], ["msk4_A"])
            asel(wA, wA, [[0, nb_], [-1, 2 * m_]], ALU.is_ge, m_ - 1, 0, ["msk4_A"], ["msk4_A"])
            wN = msk4_Nf[:, v_, :].rearrange("p (a b) -> p a b", b=2 * m_)
            memset("pool", msk4_Nf[:, v_, :], 1.0, ["msk4_Nf"])
            asel(wN, wN, [[-2 * m_, nb_], [0, 2 * m_]], ALU.is_ge, 0, 1, ["msk4_Nf"], ["msk4_Nf"])
            asel(wN, wN, [[2 * m_, nb_], [0, 2 * m_]], ALU.is_ge, m_ - 1, -1, ["msk4_Nf"], ["msk4_Nf"])
            asel(wN, wN, [[0, nb_], [1, 2 * m_]], ALU.is_ge, -m_, 0, ["msk4_Nf"], ["msk4_Nf"])
        cp("pool", msk4_N[:], msk4_Nf[:], ["msk4_Nf"], ["msk4_N"])
        memset("pool", rm64[:], 1.0, ["rm64"])
        memset("pool", rm64[:].rearrange("p (a b) -> p a b", b=64)[:, :, 0:1], 0.0, ["rm64"])
        memset("pool", rm8[:], 1.0, ["rm8"])
        memset("pool", rm8[:].rearrange("p (a b) -> p a b", b=8)[:, :, 0:1], 0.0, ["rm8"])
        memset("pool", cmk[:], 1.0, ["cmk"])
        asel(cmk[:], cmk[:], [[8, 16], [1, 8]], ALU.is_ge, 0, -1, ["cmk"], ["cmk"])
        asel(cmk[:], cmk[:], [[-8, 16], [0, 8]], ALU.is_ge, 0, 1, ["cmk"], ["cmk"])
        memset("pool", cst[:, 0:1], 1e-6, ["cst"])
        memset("pool", cst[:, 1:2], 1e-5, ["cst"])
        memset("pool", cst[:, 2:3], 64e-5, ["cst"])
        memset("pool", cst[:, 3:4], 1e-24, ["cst"])
        for i in range(NMB):
            memset("pool", M_pad[i][:], 0.0, ["M_pad%d_g0" % i, "M_pad%d_g1" % i])
            memset("pool", Mb_pad[i][:], 0.0, ["Mb_pad%d_g0" % i, "Mb_pad%d_g1" % i])
        memset("pool", carry[:], 0.0, ["carry"])
        ts("dve", V("onem_mu", 0, 14), V("mu", 0, 14), -1.0, 1.0, ALU.mult, ALU.add, ["vec"], ["vec"])
        ts("dve", V("half_w0", 0, 4), V("w0", 0, 4), 0.5, None, ALU.mult, None, ["vec"], ["vec"])
        ts("dve", V("half_a0", 0, 4), V("a0", 0, 4), 0.5, None, ALU.mult, None, ["vec"], ["vec"])
        ts("dve", V("onem_ka", 0, 4), V("k_a", 0, 4), -1.0, 1.0, ALU.mult, ALU.add, ["vec"], ["vec"])
        wsT_f = P3[:, 0:2, :].rearrange("p a (b n) -> p (a b) n", n=128)
        S.dma("sp", wsT_f, wsT_d, writes=["P3"], key="k_ws")
        S.dma("sp", rep_f[:], rep_d, writes=["rep_f"], key="k_rep")
        S.dma("sp", bsb[:], bsb_d, writes=["bsb"], key="k_bsb")
        S.dma("sp", sshT[:], ssh_d, writes=["sshT"], key="k_ssh")
        S.dma("pool", lora_bf[:], lora_d, writes=["lora_bf"], key="k_lora")
        asel(wsT_f, wsT_f, [[0, 8], [1, 128]], ALU.is_ge, 0, -1, ["P3"], ["P3"])
        cp("pool", WsT[:], wsT_f, ["P3"], ["WsT"])
        tt("pool", BsT[:].rearrange("p h (s i) -> p h s i", i=8),
           rep_f[:].unsqueeze(2).broadcast_to([128, 8, 16, 8]),
           cmk[:].unsqueeze(1).broadcast_to([128, 8, 16, 8]), ALU.mult, ["rep_f", "cmk"], ["BsT"])

        act(tA[:, 0:136], cT[:].rearrange("p a b -> p (a b)"), AF.Tanh, ["cT"], ["tA"], scale=0.5)
        stt("dve", tA[:, 0:136], tA[:, 0:136], 1.0, cT[:].rearrange("p a b -> p (a b)"), ALU.add, ALU.mult, ["tA", "cT"], ["tA"])
        ts("dve", cs_bf[:].rearrange("p a b -> p (a b)"), tA[:, 0:136], 0.5, None, ALU.mult, None, ["tA"], ["cs_bf"])
        ada_buf = [P1, P2]
        wada_v = wada_d.rearrange("(c p) n -> p c n", p=128)
        for pc in range(18):
            buf = ada_buf[pc % 2]
            bv = buf[:].rearrange("p a b -> p (a b)").bitcast(BF16).rearrange("p (c n) -> p c n", n=512)
            S.dma("pool", bv, wada_v[:, :, pc * 512:(pc + 1) * 512], writes=[buf.name], key="ad%d" % (pc % 2))
            b = nb()
            for jc in range(4):
                for c in range(8):
                    mm(ps[:, b, jc * 17:(jc + 1) * 17], bv[:, c, jc * 128:(jc + 1) * 128], cs_bf[:, c, :],
                       c == 0, c == 7, [buf.name, "cs_bf"], [PB(b)])
            tt("dve", mod[:, pc * 4:(pc + 1) * 4, :], ps[:, b, 0:68].rearrange("p (a b) -> p a b", b=17),
               V("b_ada", pc * 4, 4).unsqueeze(2).broadcast_to([128, 4, 17]), ALU.add, [PB(b), "vec"], ["mod"])
        for m in (1, 4, 7):
            ts("dve", mod[:, m * 8:(m + 1) * 8, :], mod[:, m * 8:(m + 1) * 8, :], 1.0, None, ALU.add, None, ["mod"], ["mod"])
        for m in (2, 8):
            ts("dve", mod[:, m * 8:(m + 1) * 8, :], mod[:, m * 8:(m + 1) * 8, :], 0.25, None, ALU.mult, None, ["mod"], ["mod"])

        cvn = [0]

        def conv(out, in_, tok):
            k = "cv%d" % (cvn[0] % 8)
            cvn[0] += 1
            S.dma("pool", out, in_, reads=[("cvkey", k)], writes=[tok, ("cvkey", k)], key=k)

        def conv_ffn(f):
            for j in range(FC):
                conv(gu_s[f, j], gu_d[f, j], ("gu", f, j))
            for dc in range(8):
                conv(dn_s[f, dc], dn_d[f, dc], ("dn", f, dc))

        conv_ffn(0)
        for l in range(8):
            lo, hi = 3 * l, min(3 * l + 3, FC)
            conv(win_s[lo:hi], win_d[lo:hi], ("win", l))

        def conv_batch_b():
            nB = [0]

            def convb(out, in_, tok):
                S.dma("pool", out, in_, writes=[tok], key="cvB%d" % nB[0])
                nB[0] += 1
            for l in range(3):
                lo, hi = 3 * l, min(3 * l + 3, 8)
                convb(wout_s[lo:hi], wout_d[lo:hi], ("wout", l))
            for j in range(FC):
                convb(gu_s[1, j], gu_d[1, j], ("gu", 1, j))
            for dc in range(8):
                convb(dn_s[1, dc], dn_d[1, dc], ("dn", 1, dc))

        tiles = []
        for i in range(4):
            tiles.append(TileCtx(i, i * TP, 1, TP, 64, False, i == 3))
        tiles.append(TileCtx(4, SEQ, NSEQ_S, TL_S, 8, True, False))
        tiles = [tiles[4]] + tiles[:4]
        if "pesync" in DBGF:
            SAME_ENGINE_SYNC["pe"] = True

        loads = []
        for tcx in tiles:
            loads += [("gu", 0, j) for j in range(FC)] + [("dn", 0, dc) for dc in range(8)]
            loads += [("win", l) for l in range(8)] + [("wout", l) for l in range(3)]
            loads += [("gu", 1, j) for j in range(FC)] + [("dn", 1, dc) for dc in range(8)]
        ld_state = {"issued": 0, "next": 0, "hold": None}

        def issue_load(i):
            kind = loads[i]
            s = slots[i % 3]
            tok = "slot%d" % (i % 3)
            if kind[0] == "gu":
                S.dma("sp", s[:, 0:2048], gu_s[kind[1], kind[2]], reads=[kind], writes=[tok], key="w%d" % (i % 3))
            elif kind[0] == "dn":
                S.dma("sp", s[:, 0:DFF], dn_s[kind[1], kind[2]], reads=[kind], writes=[tok], key="w%d" % (i % 3))
            else:
                src = win_s if kind[0] == "win" else wout_s
                tot = FC if kind[0] == "win" else 8
                lo, hi = 3 * kind[1], min(3 * kind[1] + 3, tot)
                S.dma("sp", s[:, 0:(hi - lo) * D].rearrange("p (j n) -> p j n", n=D),
                      src[lo:hi].rearrange("j p n -> p j n"), reads=[kind], writes=[tok], key="w%d" % (i % 3))

        def next_slot(expect):
            i = ld_state["next"]
            assert loads[i][:len(expect)] == expect, (loads[i], expect)
            base = i if ld_state["hold"] is None else ld_state["hold"]
            while ld_state["issued"] < min(base + 3, len(loads)):
                issue_load(ld_state["issued"])
                ld_state["issued"] += 1
            ld_state["next"] += 1
            return slots[i % 3], "slot%d" % (i % 3)

        def hid_view(tc):
            return big[:, 0:FC * TP // 2].bitcast(BF16).rearrange("p (j t) -> p j t", t=TP)

        def hid_tok(j):
            return ["big%d" % j]

        def xm_tok(jc):
            return ["big%d" % (2 * jc), "big%d" % (2 * jc + 1)]

        xm = big[:].rearrange("p (j t) -> p j t", t=TP)

        def s3(tc, ap2):
            return ap2.rearrange("p (s t) -> p s t", t=tc.tl)

        def modap(tc, m, c):
            if not tc.sample:
                return mod[:, m * 8 + c, 0:1]
            return mod[:, m * 8 + c, 1:17].unsqueeze(2).broadcast_to([128, NSEQ_S, TL_S])

        def rms_stats(tc, xs, nchunks, lhs, toks_in, eps_col):
            T = tc.T
            b = nb()
            for c in range(nchunks):
                k = c % 2
                act(sq[k][:, 0:T], xs(c), AF.Square, toks_in(c), ["sq%d" % k])
                mm(ps[:, b, 0:T], lhs, sq[k][:, 0:T], c == 0, c == nchunks - 1, ["sq%d" % k], [PB(b)])
            act(rstd[:, 0:T], ps[:, b, 0:T], AF.Sqrt, [PB(b), "cst"], ["rstd"], bias=cst[:, eps_col:eps_col + 1])
            S.op("dve", lambda e: e.reciprocal(out=rstd[:, 0:T], in_=rstd[:, 0:T]), ["rstd"], ["rstd"])

        def norm_mod(tc, m_sh, m_sc):
            T = tc.T
            rms_stats(tc, lambda c: xt[:, c, 0:T], 8, onesD_bf[:], lambda c: ["xt%d" % c], 0)
            for c in range(8):
                k = c % 2
                tt("dve", t1[k][:, 0:T], xt[:, c, 0:T], rstd[:, 0:T], ALU.mult, ["xt%d" % c, "rstd"], ["t1_%d" % k])
                if not tc.sample:
                    act(n_[:, c, 0:T], t1[k][:, 0:T], AF.Identity, ["t1_%d" % k, "mod"], ["n%d" % c],
                        bias=modap(tc, m_sh, c), scale=modap(tc, m_sc, c))
                else:
                    tt("dve", s3(tc, t1[k][:, 0:T]), s3(tc, t1[k][:, 0:T]), modap(tc, m_sc, c), ALU.mult, ["t1_%d" % k, "mod"], ["t1_%d" % k])
                    tt("dve", s3(tc, n_[:, c, 0:T]), s3(tc, t1[k][:, 0:T]), modap(tc, m_sh, c), ALU.add, ["t1_%d" % k, "mod"], ["n%d" % c])

        def gate_add(tc, c, psap, m_g, ptoks):
            T = tc.T
            if not tc.sample:
                stt("dve", xt[:, c, 0:T], psap, modap(tc, m_g, c), xt[:, c, 0:T], ALU.mult, ALU.add,
                    list(ptoks) + ["xt%d" % c, "mod"], ["xt%d" % c])
            else:
                tt("dve", s3(tc, tA[:, 0:T]), s3(tc, psap), modap(tc, m_g, c), ALU.mult, list(ptoks) + ["mod"], ["tA"])
                tt("dve", xt[:, c, 0:T], xt[:, c, 0:T], tA[:, 0:T], ALU.add, ["tA", "xt%d" % c], ["xt%d" % c])

        def ffn(tc, f, m_g):
            T = tc.T
            hid = hid_view(tc)
            for j in range(FC):
                sl, stok = next_slot(("gu", f, j))
                w = sl[:, 0:2048].rearrange("p (g c n) -> p g c n", g=2, c=8)
                bg, bu = nb(), nb()
                for g, b in ((0, bg), (1, bu)):
                    for c in range(8):
                        mm(ps[:, b, 0:T], w[:, g, c, :], n_[:, c, 0:T], c == 0, c == 7, [stok, "n%d" % c], [PB(b)])
                k = j % 2
                act(th[k][:, 0:T], ps[:, bg, 0:T], AF.Tanh, [PB(bg)], ["th%d" % k], scale=0.5)
                stt("dve", th[k][:, 0:T], th[k][:, 0:T], 1.0, ps[:, bg, 0:T], ALU.add, ALU.mult, ["th%d" % k, PB(bg)], ["th%d" % k])
                tt("dve", hid[:, j, 0:T], th[k][:, 0:T], ps[:, bu, 0:T], ALU.mult, ["th%d" % k, PB(bu)], hid_tok(j))
            for dc in range(8):
                sl, stok = next_slot(("dn", f, dc))
                w = sl[:, 0:DFF].rearrange("p (j n) -> p j n", n=128)
                b = nb()
                for j in range(FC):
                    mm(ps[:, b, 0:T], w[:, j, :], hid[:, j, 0:T], j == 0, j == FC - 1, [stok] + hid_tok(j), [PB(b)])
                gate_add(tc, dc, ps[:, b, 0:T], m_g, [PB(b)])

        def w_in_phase(tc):
            T = tc.T
            if tc.sample:
                conv_batch_b()
            nsq, tl = tc.nseq, tc.tl
            for l in range(8):
                sl, stok = next_slot(("win", l))
                lo, hi = 3 * l, min(3 * l + 3, FC)
                w = sl[:, 0:(hi - lo) * D].rearrange("p (j c n) -> p j c n", c=8, n=128)
                for jj in range(hi - lo):
                    jc = lo + jj
                    b = nb()
                    for c in range(8):
                        mm(ps[:, b, 0:T], w[:, jj, c, :], n_[:, c, 0:T], c == 0, c == 7, [stok, "n%d" % c], [PB(b)])
                    p3 = s3(tc, ps[:, b, 0:T])
                    if jc < 14:
                        k = jc % 2
                        tk = "t1_%d" % k
                        tmp3 = s3(tc, t1[k][:, 0:T])
                        ts("dve", tmp3[:, :, 1:tl], p3[:, :, 0:tl - 1], V("mu", jc), None, ALU.mult, None, [PB(b), "vec"], [tk])
                        prev0 = sshT[:, jc, :] if tc.sample else carry[:, jc, :]
                        ts("dve", tmp3[:, :, 0:1], prev0.unsqueeze(2), V("mu", jc), None, ALU.mult, None,
                           ["sshT", "carry", "vec"], [tk])
                        stt("dve", xm[:, jc, 0:T], ps[:, b, 0:T], V("onem_mu", jc), t1[k][:, 0:T], ALU.mult, ALU.add,
                            [PB(b), tk, "vec"], xm_tok(jc))
                        if not tc.sample:
                            cp("dve", carry[:, jc, :], ps[:, b, T - 1:T], [PB(b)], ["carry"])
                            if tc.last_prompt and "noshout" not in DBGF:
                                cp("dve", shout[:, jc, 0:1], ps[:, b, T - 1:T], [PB(b)], ["shout"])
                        else:
                            cp("dve", shout[:, jc, 1:17], p3[:, :, tl - 1], [PB(b)], ["shout"])
                        if jc == 13:
                            for _ in prep_gen(tc, 0):
                                pass
                    elif jc < 18:
                        cp("act", P1[:, jc - 14, 0:T], ps[:, b, 0:T], [PB(b)], ["P1"])
                    else:
                        cp("act", P2[:, jc - 18, 0:T], ps[:, b, 0:T], [PB(b)], ["P2"])

        SBK = 128
        pbf = sb("pbf", [128, 4, SBK], BF16)
        gCs = sb("gCs", [128, 4, 16], F32)

        def bc4(name, w=SBK):
            return V(name, 0, 4).unsqueeze(2).broadcast_to([128, 4, w])

        def v4(t2):
            return t2[:, 0:4 * SBK].rearrange("p (c n) -> p c n", n=SBK)

        def gmlp_gen(tc, qb):
            cols = slice(qb * SBK, (qb + 1) * SBK)
            pu, pv = P1, P2
            pvc, pvt = v4(t1[0]), "t1_0"
            tmix, tmt = v4(t1[1]), "t1_1"
            rstd_g = t1[1][:, 0:SBK]
            sqb = th[1][:, 0:256].bitcast(BF16).rearrange("p (c n) -> p c n", n=SBK)
            vn_bf = sq[1][:, 0:4 * SBK].rearrange("p (c n) -> p c n", n=SBK)
            b = nb()
            for c in range(4):
                mm(ps[:, b, 0:SBK], ones512_f[:], pv[:, c, cols], c == 0, c == 3, ["P2", "ones512_f"], [PB(b)])
            tt("dve", pvc, pv[:, :, cols], ps[:, b, 0:SBK].unsqueeze(1).broadcast_to([128, 4, SBK]), ALU.subtract, ["P2", PB(b)], [pvt])
            act(sqb, pvc, AF.Square, [pvt, "th1"], ["th1a"])
            yield
            b2 = nb()
            for c in range(4):
                mm(ps[:, b2, 0:SBK], ones512_bf[:], sqb[:, c, :], c == 0, c == 3, ["th1a", "ones512_bf"], [PB(b2)])
            act(rstd_g, ps[:, b2, 0:SBK], AF.Sqrt, [PB(b2), "cst"], [tmt], bias=cst[:, 1:2])
            S.op("dve", lambda e: e.reciprocal(out=rstd_g, in_=rstd_g), [tmt], [tmt])
            tt("pool", pvc, pvc, rstd_g.unsqueeze(1).broadcast_to([128, 4, SBK]), ALU.mult, [pvt, tmt], [pvt])
            yield
            for c in range(4):
                act(pvc[:, c, :], pvc[:, c, :], AF.Identity, [pvt, "vec"], [pvt], bias=V("ln_b", c), scale=V("ln_g", c))
            cp("act", vn_bf, pvc, [pvt], ["sq1"])
            if tc.sample:
                S.dma("sp", cv_o, pvc, reads=[pvt], key="cvo")
            yield
            bt_ = nb()
            pT = ps[:, bt_, 0:256].bitcast(BF16).rearrange("p (c n) -> p c n", n=128)
            for c in range(4):
                tr(pT[:, c, :], vn_bf[:, c, :], ["sq1"], [PB(bt_)])
            cp("act", vnT_sb[:], pT, [PB(bt_)], ["vnT_sb"])
            yield
            bm = nb()
            Wm = BsT if tc.sample else WsT
            Wtok = "BsT" if tc.sample else "WsT"
            for c in range(4):
                for hh in range(2):
                    mm(ps[hh * 64:(hh + 1) * 64, bm, c * 128:(c + 1) * 128], vnT_sb[:, c, hh * 64:(hh + 1) * 64],
                       Wm[:, 2 * c + hh, :], True, True, ["vnT_sb", Wtok], [PB(bm)])
            mixp = ps[:, bm, :].rearrange("p (c n) -> p c n", n=128)
            if tc.sample:
                bias_ap = bsb[:, :, 0:8].unsqueeze(2).broadcast_to([128, 4, 16, 8])
                tt("dve", tmix.rearrange("p c (s i) -> p c s i", i=8),
                   mixp.rearrange("p c (s i) -> p c s i", i=8), bias_ap, ALU.add, [PB(bm), "bsb"], [tmt])
            else:
                tt("dve", tmix, mixp, bsb[:], ALU.add, [PB(bm), "bsb"], [tmt])
            tt("pool", n_[:, 4:8, cols], tmix, pu[:, :, cols], ALU.mult, [tmt, "P1"], ["n4", "n5", "n6", "n7"])
            yield

        def prep_gen(tc, sbi):
            T, C = tc.T, tc.C
            cols = slice(sbi * SBK, (sbi + 1) * SBK)
            SB = "_%d" % sbi
            r4, k4, v4_ = xm[:, 0:4, cols], xm[:, 4:8, cols], xm[:, 8:12, cols]
            rtok = [t for c in range(4) for t in xm_tok(c)]
            ktok = [t for c in range(4) for t in xm_tok(4 + c)]
            vtok = [t for c in range(4) for t in xm_tok(8 + c)]
            wdad = xm[:, 12, cols]
            gd = xm[:, 13, cols]
            L, Aa, K, Gi = P4[:, :, 0:128], P4[:, :, 128:256], P4[:, :, 256:384], P4[:, :, 384:512]
            tX, tY = v4(tA), v4(tB)
            lin_bf, sg_bf = sq[0][:, 0:128], sq[0][:, 128:256]
            sqs = [sq[0][:, 256:384], sq[0][:, 384:512]]
            act(lin_bf[0:64, :], wdad[0:64, :], AF.Tanh, xm_tok(12) + ["sq0"], ["sq0a"])
            cp("act", lin_bf[64:128, :], wdad[64:128, :], xm_tok(12), ["sq0a"])
            act(tY[:, 0, :], gd, AF.Tanh, xm_tok(13), ["tB"], scale=0.5)
            ts("pool", sg_bf, tY[:, 0, :], 0.5, 0.5, ALU.mult, ALU.add, ["tB", "sq0"], ["sq0b"])
            yield
            bw, ba, bg = nb(), nb(), nb()
            for c in range(4):
                mm(ps[:, bw, c * 128:(c + 1) * 128], lora_bf[:, c * 128:(c + 1) * 128], lin_bf, True, True, ["lora_bf", "sq0a"], [PB(bw)])
            for c in range(4):
                mm(ps[:, ba, c * 128:(c + 1) * 128], lora_bf[:, 512 + c * 128:512 + (c + 1) * 128], lin_bf, True, True, ["lora_bf", "sq0a"], [PB(ba)])
            for c in range(4):
                mm(ps[:, bg, c * 128:(c + 1) * 128], lora_bf[:, 1024 + c * 128:1024 + (c + 1) * 128], sg_bf, True, True, ["lora_bf", "sq0b"], [PB(bg)])
            for c in range(4):
                act(tX[:, c, :], ps[:, bw, c * 128:(c + 1) * 128], AF.Tanh, [PB(bw), "vec"], ["tA"], bias=V("half_w0", c), scale=0.5)
            ts("pool", L, tX, -0.5 * W_C, -0.5 * W_C, ALU.mult, ALU.add, ["tA"], ["P4a"])
            for c in range(4):
                act(tY[:, c, :], ps[:, ba, c * 128:(c + 1) * 128], AF.Tanh, [PB(ba), "vec"], ["tB"], bias=V("half_a0", c), scale=0.5)
            ts("pool", Aa, tY, 0.5, 0.5, ALU.mult, ALU.add, ["tB"], ["P4b"])
            cp("act", G_[:, :, cols], ps[:, bg, :].rearrange("p (c n) -> p c n", n=128), [PB(bg)], ["G_" + SB])
            yield
            tt("pool", K, k4, bc4("k_k"), ALU.mult, ktok + ["vec"], ["P4c"])
            bk = nb()
            for c in range(4):
                act(sqs[c % 2], K[:, c, :], AF.Square, ["P4c", "sq0"], ["sq0%s" % "cd"[c % 2]])
                mm(ps[:, bk, c * 128:(c + 1) * 128], blk64o_bf[:], sqs[c % 2], True, True, ["sq0%s" % "cd"[c % 2], "blk64o_bf"], [PB(bk)])
            act(tA[:, 0:512], ps[:, bk, :], AF.Sqrt, [PB(bk), "cst"], ["tA"], bias=cst[:, 3:4])
            S.op("dve", lambda e: e.reciprocal(out=tA[:, 0:512], in_=tA[:, 0:512]), ["tA"], ["tA"])
            tt("pool", K, K, tX, ALU.mult, ["P4c", "tA"], ["P4c"])
            yield
            tt("pool", tY, Aa, bc4("k_a"), ALU.mult, ["P4b", "vec"], ["tB"])
            tt("pool", tY, tY, bc4("onem_ka"), ALU.add, ["tB", "vec"], ["tB"])
            tt("dve", k4, k4, tY, ALU.mult, ktok + ["tB"], ktok)
            yield
            rm = rm8 if tc.sample else rm64
            rmt = "rm8" if tc.sample else "rm64"
            for c in range(4):
                S.op("dve", lambda e, c=c: e.tensor_tensor_scan(out=tX[:, c, :], data0=rm[:, 0:SBK], data1=L[:, c, :],
                                                                 initial=0.0, op0=ALU.mult, op1=ALU.add),
                     [rmt, "P4a"], ["tA"])
            act(Gi, tX, AF.Exp, ["tA"], ["P4d"])
            tt("pool", tY, tX, L, ALU.subtract, ["tA", "P4a"], ["tB"])
            act(tY, tY, AF.Exp, ["tB"], ["tB"])
            act(tX, tX, AF.Exp, ["tA"], ["tA"], scale=-1.0)
            nqs = SBK // C
            q0 = sbi * nqs
            cp("pool", gCs[:, :, q0:q0 + nqs], Gi.rearrange("p c (q t) -> p c q t", t=C)[:, :, :, C - 1],
               ["P4d"], ["gCs" + SB])
            yield
            arv = ar[:, :, 0:2 * T].rearrange("p c (q x t) -> p c q x t", x=2, t=C)
            q4 = lambda ap3: ap3.rearrange("p c (q t) -> p c q t", t=C)
            for c in range(4):
                stt("dve", arv[:, c, q0:q0 + nqs, 0, :], K[:, c, :].rearrange("p (q t) -> p q t", t=C), -1.0,
                    tY[:, c, :].rearrange("p (q t) -> p q t", t=C), ALU.mult, ALU.mult, ["P4c", "tB"], ["ar" + SB])
                tt("pool", arv[:, c, q0:q0 + nqs, 1, :], xm[:, c, cols].rearrange("p (q t) -> p q t", t=C),
                   Gi[:, c, :].rearrange("p (q t) -> p q t", t=C), ALU.mult, xm_tok(c) + ["P4d"], ["ar" + SB])
            yield
            tt("pool", tY, K, Aa, ALU.mult, ["P4c", "P4b"], ["tB"])
            tt("dve", bt[:, :, cols], tY, tX, ALU.mult, ["tB", "tA"], ["bt" + SB])
            tt("pool", kt[:, :, cols], k4, tX, ALU.mult, ktok + ["tA"], ["kt" + SB])
            cp("act", vb[:, :, cols], v4_, vtok, ["vb" + SB])
            yield

        def post_gen(tc, sbi):
            cols = slice(sbi * SBK, (sbi + 1) * SBK)
            SB = "_%d" % sbi
            y = P3[:, :, cols]
            yt = "P3" + SB
            pa = v4(th[0])
            sqb = th[1][:, 256:512].bitcast(BF16).rearrange("p (c n) -> p c n", n=SBK)
            rs = v4(rstd)
            rtok = [t for c in range(4) for t in xm_tok(c)]
            ktok = [t for c in range(4) for t in xm_tok(4 + c)]
            vtok = [t for c in range(4) for t in xm_tok(8 + c)]
            b = nb()
            for c in range(4):
                mm(ps[:, b, c * 128:(c + 1) * 128], blk64m_f[:], P3[:, c, cols], True, True, [yt, "blk64m_f"], [PB(b)])
            tt("dve", y, y, ps[:, b, :].rearrange("p (c n) -> p c n", n=128), ALU.subtract, [yt, PB(b)], [yt])
            act(sqb, y, AF.Square, [yt, "th1"], ["th1d"])
            yield
            b2 = nb()
            for c in range(4):
                mm(ps[:, b2, c * 128:(c + 1) * 128], blk64m_bf[:], sqb[:, c, :], True, True, ["th1d", "blk64m_bf"], [PB(b2)])
            act(rstd[:, 0:512], ps[:, b2, :], AF.Sqrt, [PB(b2), "cst"], ["rstd"], bias=cst[:, 2:3])
            S.op("dve", lambda e: e.reciprocal(out=rstd[:, 0:512], in_=rstd[:, 0:512]), ["rstd"], ["rstd"])
            tt("pool", y, y, rs, ALU.mult, [yt, "rstd"], [yt])
            yield
            for c in range(4):
                act(P3[:, c, cols], P3[:, c, cols], AF.Identity, [yt, "vec"], [yt], bias=V("gn_b", c), scale=V("gn_w", c))
            tt("pool", pa, xm[:, 0:4, cols], bc4("r_k"), ALU.mult, rtok + ["vec"], ["th0"])
            tt("pool", pbf[:], pa, xm[:, 4:8, cols], ALU.mult, ["th0"] + ktok, ["pbf"])
            yield
            b3 = nb()
            for c in range(4):
                mm(ps[:, b3, c * 128:(c + 1) * 128], blk64o_bf[:], pbf[:, c, :], True, True, ["pbf", "blk64o_bf"], [PB(b3)])
            tt("dve", pa, ps[:, b3, :].rearrange("p (c n) -> p c n", n=128), xm[:, 8:12, cols], ALU.mult, [PB(b3)] + vtok, ["th0"])
            tt("pool", y, y, pa, ALU.add, [yt, "th0"], [yt])
            tt("pool", n_[:, 0:4, cols], y, G_[:, :, cols], ALU.mult, [yt, "G_" + SB], ["n0", "n1", "n2", "n3"])
            yield

        def scan_gen(tc, sbi):
            T, C = tc.T, tc.C
            nq = T // C
            y_sb = P3
            SB = "_%d" % sbi
            arv = ar[:, :, 0:2 * T].rearrange("p c (q x t) -> p c q x t", x=2, t=C)
            nv = 4 if C == 64 else 1

            def chunk_gen(q, hg, mi, pb=0):
                G = "_g%d_s%d" % (hg, pb // 32)
                PS = slice(pb, pb + C)
                FS = slice(pb, pb + C)
                Mp, Mb = M_pad[mi], Mb_pad[mi]
                Mt, Mbt = "M_pad%d_g%d" % (mi, hg), "Mb_pad%d_g%d" % (mi, hg)
                cs_ = slice(q * C, (q + 1) * C)
                c0 = 2 * hg
                h0 = 4 * hg
                cl = (c0, c0 + 1)
                b1 = nb()
                pT = ps[PS, b1, 0:256].bitcast(BF16).rearrange("p (x c n) -> p x c n", x=2, n=128)
                pV = ps[PS, b1, 256:384].bitcast(BF16).rearrange("p (c n) -> p c n", n=128)
                for x, src, stok in ((0, bt, "bt" + SB), (1, kt, "kt" + SB)):
                    for ci, c in enumerate(cl):
                        tr(pT[:, x, ci, :], src[:, c, cs_], [stok], [PB(b1)])
                for ci, c in enumerate(cl):
                    tr(pV[:, ci, :], vb[:, c, cs_], ["vb" + SB], [PB(b1)])
                cp("act", bkT_sb[PS, :, c0:c0 + 2, :], pT, [PB(b1)], ["bkT_sb" + G])
                cp("act", vT_sb[PS, c0:c0 + 2, :], pV, [PB(b1)], ["vT_sb" + G])
                yield
                for hh in range(2):
                    hs = slice(hh * 64, (hh + 1) * 64)
                    b = nb()
                    pv5 = ps[PS, b, 0:2 * 4 * C].rearrange("p (h x y t) -> p h x y t", x=2, y=2, t=C)
                    for ci, c in enumerate(cl):
                        rhs = arv[hs, c, q].rearrange("p x t -> p (x t)")
                        for x, src, stok in ((0, bt, "bt" + SB), (1, kt, "kt" + SB)):
                            mm(pv5[:, ci, x].rearrange("p y t -> p (y t)"), src[hs, c, cs_], rhs, True, True, [stok, "ar" + SB], [PB(b)])
                    hsel = slice(h0 + hh, h0 + hh + 3, 2)
                    if C == 64:
                        tt("dve", Pbk_sb[PS, hsel, :, :, 0:C], pv5,
                           msk_ar[PS, :, FS].unsqueeze(1).unsqueeze(1).broadcast_to([C, 2, 2, 2, C]), ALU.mult,
                           [PB(b), "msk_ar"], ["Pbk_sb" + G])
                    else:
                        for x in range(2):
                            tt("dve", Pbk_sb[PS, hsel, x, :, 0:C], pv5[:, :, x],
                               msk_ar[PS, :, FS].unsqueeze(1).broadcast_to([C, 2, 2, C]), ALU.mult,
                               [PB(b), "msk_ar"], ["Pbk_sb" + G])
                    yield
                for hh in range(2):
                    hs = slice(hh * 64, (hh + 1) * 64)
                    b = nb()
                    pA = ps[PS, b, 0:2 * C].rearrange("p (h t) -> p h t", t=C)
                    for ci, c in enumerate(cl):
                        mm(pA[:, ci, :], arv[hs, c, q, 0, :], bt[hs, c, cs_], True, True, ["ar" + SB, "bt" + SB], [PB(b)])
                    tt("dve", A_all[PS, 0:nv, h0 + hh:h0 + hh + 3:2, 0:C], pA.unsqueeze(1).broadcast_to([C, nv, 2, C]),
                       msk4_A[PS, 0:nv, FS].unsqueeze(2).broadcast_to([C, nv, 2, C]), ALU.mult,
                       [PB(b), "msk4_A"], ["A_all" + G])
                    yield
                hr = slice(h0, h0 + 4)
                tt("pool", N_all[PS, 0:nv, hr, 0:C], Pbk_sb[PS, hr, 0, 0, 0:C].unsqueeze(1).broadcast_to([C, nv, 4, C]),
                   msk4_N[PS, 0:nv, FS].unsqueeze(2).broadcast_to([C, nv, 4, C]), ALU.mult,
                   ["Pbk_sb" + G, "msk4_N"], ["N_all" + G])
                idb = ident_bf[PS, FS].unsqueeze(1).broadcast_to([C, 4, C])
                tt("pool", Q_sb[0][PS, hr, 0:C], N_all[PS, 0, hr, 0:C], idb, ALU.add, ["N_all" + G, "ident_bf"], ["Q_sb0" + G])
                if nv > 1:
                    tt("pool", D_sb[0][PS, hr, 0:C], A_all[PS, 0, hr, 0:C], idb, ALU.add, ["A_all" + G, "ident_bf"], ["D_sb0" + G])
                yield

                def hmm(psv, lhs, ltok, rhs, rtok, bnk):
                    for hl in range(4):
                        mm(psv[:, hl, :], lhs[PS, h0 + hl, 0:C], rhs[PS, h0 + hl, 0:C], True, True, [ltok + G, rtok + G], [PB(bnk)])

                def pbank():
                    bnk = nb()
                    return bnk, ps[PS, bnk, 0:4 * C].rearrange("p (h t) -> p h t", t=C)

                Acur, Atok = A_all[:, 0], "A_all"
                Ncur, Ntok = N_all[:, 0], "N_all"
                cur = 0
                for lv in range(2):
                    nxt = 1 - cur
                    an, nn = A_sb[lv % 2], N_sb[lv % 2]
                    ant, nnt = "A_sb%d" % (lv % 2), "N_sb%d" % (lv % 2)
                    bA, pA2 = pbank()
                    hmm(pA2, Ncur, Ntok, Acur, Atok, bA)
                    needN = (nv > 1) or lv == 0
                    if needN:
                        bN, pN2 = pbank()
                        hmm(pN2, Acur, Atok, Ncur, Ntok, bN)
                    cp("act", an[PS, hr, 0:C], pA2, [PB(bA)], [ant + G])
                    if needN:
                        cp("dve", nn[PS, hr, 0:C], pN2, [PB(bN)], [nnt + G])
                    yield
                    bQ, pQ = pbank()
                    hmm(pQ, an, ant, Q_sb[cur], "Q_sb%d" % cur, bQ)
                    if nv > 1:
                        bD, pD = pbank()
                        hmm(pD, nn, nnt, D_sb[cur], "D_sb%d" % cur, bD)
                    tt("dve", Q_sb[nxt][PS, hr, 0:C], pQ, Q_sb[cur][PS, hr, 0:C], ALU.add, [PB(bQ), "Q_sb%d" % cur + G], ["Q_sb%d" % nxt + G])
                    if nv > 1:
                        tt("dve", D_sb[nxt][PS, hr, 0:C], pD, D_sb[cur][PS, hr, 0:C], ALU.add, [PB(bD), "D_sb%d" % cur + G], ["D_sb%d" % nxt + G])
                    yield
                    Acur, Atok, Ncur, Ntok = an, ant, nn, nnt
                    cur = nxt
                for v in range(1, nv):
                    nxt = 1 - cur
                    lastm = (v == nv - 1)
                    Qc, Qct, Dc, Dct = Q_sb[cur], "Q_sb%d" % cur, D_sb[cur], "D_sb%d" % cur
                    b2_, pX2 = pbank()
                    hmm(pX2, A_all[:, v], "A_all", Qc, Qct, b2_)
                    if not lastm:
                        b1_, pX1 = pbank()
                        hmm(pX1, N_all[:, v], "N_all", Dc, Dct, b1_)
                    cp("act", N_sb[0][PS, hr, 0:C], pX2, [PB(b2_)], ["N_sb0" + G])
                    if not lastm:
                        cp("dve", A_sb[0][PS, hr, 0:C], pX1, [PB(b1_)], ["A_sb0" + G])
                    yield
                    b3_, pY2 = pbank()
                    hmm(pY2, Dc, Dct, N_sb[0], "N_sb0", b3_)
                    if not lastm:
                        b4_, pY1 = pbank()
                        hmm(pY1, Qc, Qct, A_sb[0], "A_sb0", b4_)
                    tt("dve", Q_sb[nxt][PS, hr, 0:C], pY2, Qc[PS, hr, 0:C], ALU.add, [PB(b3_), Qct + G], ["Q_sb%d" % nxt + G])
                    if not lastm:
                        tt("dve", D_sb[nxt][PS, hr, 0:C], pY1, Dc[PS, hr, 0:C], ALU.add, [PB(b4_), Dct + G], ["D_sb%d" % nxt + G])
                    yield
                    cur = nxt
                Qf, Qt = Q_sb[cur], "Q_sb%d" % cur + G
                bX = nb()
                pX = ps[PS, bX, 0:256].rearrange("p (c n) -> p c n", n=128)
                for ci, c in enumerate(cl):
                    mm(pX[:, ci, :], arv[:, c, q, 0, :], Mb[:, c, :], True, False, ["ar" + SB, Mbt], [PB(bX)])
                    for hh in range(2):
                        hs = slice(hh * 64, (hh + 1) * 64)
                        mm(pX[:, ci, hs], Pbk_sb[PS, 2 * c + hh, 1, 0, 0:C], vT_sb[PS, c, hs], False, hh == 1,
                           ["Pbk_sb" + G, "vT_sb" + G], [PB(bX)])
                cp("act", Xs[PS, c0:c0 + 2, :], pX, [PB(bX)], ["Xs" + G])
                yield
                bU = nb()
                pU = ps[PS, bU, 0:256].rearrange("p (c n) -> p c n", n=128)
                for ci, c in enumerate(cl):
                    for hh in range(2):
                        hs = slice(hh * 64, (hh + 1) * 64)
                        mm(pU[:, ci, hs], Qf[PS, 2 * c + hh, 0:C], Xs[PS, c, hs], True, True, [Qt, "Xs" + G], [PB(bU)])
                cp("dve", Us[PS, c0:c0 + 2, :], pU, [PB(bU)], ["Us" + G])
                yield
                bY = nb()
                pY = ps[:, bY, 0:2 * C].rearrange("p (c t) -> p c t", t=C)
                for ci, c in enumerate(cl):
                    for hh in range(2):
                        h = 2 * c + hh
                        hs = slice(hh * 64, (hh + 1) * 64)
                        mm(pY[hs, ci, :], Mb[:, c, hs], arv[:, c, q, 1, :], True, False, [Mbt, "ar" + SB], [PB(bY)])
                        mm(pY[hs, ci, :], Us[PS, c, hs], Pbk_sb[PS, h, 0, 1, 0:C], False, False, ["Us" + G, "Pbk_sb" + G], [PB(bY)])
                        mm(pY[hs, ci, :], vT_sb[PS, c, hs], Pbk_sb[PS, h, 1, 1, 0:C], False, True, ["vT_sb" + G, "Pbk_sb" + G], [PB(bY)])
                bM = nb()
                pM = ps[:, bM, 0:256].rearrange("p (c n) -> p c n", n=128)
                for ci, c in enumerate(cl):
                    mm(pM[:, ci, :], bkT_sb[PS, 0, c, :], Us[PS, c, :], True, False, ["bkT_sb" + G, "Us" + G], [PB(bM)])
                    mm(pM[:, ci, :], bkT_sb[PS, 1, c, :], vT_sb[PS, c, :], False, True, ["bkT_sb" + G, "vT_sb" + G], [PB(bM)])
                cp("act", y_sb[:, c0:c0 + 2, cs_], pY, [PB(bY)], ["P3" + SB])
                tt("dve", Mp[:, c0:c0 + 2, :], Mp[:, c0:c0 + 2, :], pM, ALU.add, [Mt, PB(bM)], [Mt])
                gC = gCs[:, c0:c0 + 2, q:q + 1].broadcast_to([128, 2, 128])
                tt("dve", Mp[:, c0:c0 + 2, :], Mp[:, c0:c0 + 2, :], gC, ALU.mult, [Mt, "gCs" + SB], [Mt])
                last = tc.sample or (tc.last_prompt and q == nq - 1)
                if not last:
                    for hh in range(2):
                        hs = slice(hh * 64, (hh + 1) * 64)
                        cp("act", Mb[hs, c0:c0 + 2, hs], Mp[hs, c0:c0 + 2, hs], [Mt], [Mbt])
                else:
                    seq = (1 + q) if tc.sample else 0
                    for hh in range(2):
                        hs = slice(hh * 64, (hh + 1) * 64)
                        S.dma("sp", wkv_o[hs, seq, c0:c0 + 2], Mp[hs, c0:c0 + 2, hs], reads=[Mt], key="wk%d_%d" % (mi, hg))
                yield

            nqs = SBK // C
            step = 2 if tc.sample else 1
            for q0_ in range(sbi * nqs, (sbi + 1) * nqs, step):
                gens = []
                for q in range(q0_, q0_ + step):
                    mi = (q % NMB) if tc.sample else 0
                    pb = 32 * (q - q0_)
                    if tc.sample:
                        Mp, Mb = M_pad[mi], Mb_pad[mi]
                        mts = ["M_pad%d_g0" % mi, "M_pad%d_g1" % mi]
                        mbts = ["Mb_pad%d_g0" % mi, "Mb_pad%d_g1" % mi]
                        for hh in range(2):
                            S.dma("sp", Mp[hh * 64:(hh + 1) * 64, :, hh * 64:(hh + 1) * 64], wkv_d[hh * 64:(hh + 1) * 64, q],
                                  writes=mts, key="sl%d" % mi)
                        for hh in range(2):
                            cp("act", Mb[hh * 64:(hh + 1) * 64, :, hh * 64:(hh + 1) * 64],
                               Mp[hh * 64:(hh + 1) * 64, :, hh * 64:(hh + 1) * 64], mts, mbts)
                    gens += [chunk_gen(q, 0, mi, pb), chunk_gen(q, 1, mi, pb)]
                while gens:
                    for g in list(gens):
                        try:
                            next(g)
                        except StopIteration:
                            gens.remove(g)
                    yield

        def mixer_phase(tc):
            nsb = tc.T // SBK
            if (not tc.sample) and tc.idx == 0:
                memset("pool", M_pad[0][:], 0.0, ["M_pad0_g0", "M_pad0_g1"])
                memset("pool", Mb_pad[0][:], 0.0, ["Mb_pad0_g0", "Mb_pad0_g1"])

            def run(main, aux, ratio=3):
                aux = list(aux)
                alive = True
                while alive or aux:
                    if main is not None and alive:
                        for _ in range(ratio):
                            try:
                                next(main)
                            except StopIteration:
                                alive = False
                                break
                    else:
                        alive = False
                    for g in list(aux):
                        try:
                            next(g)
                        except StopIteration:
                            aux.remove(g)

            for k in range(nsb):
                aux = []
                if k + 1 < nsb:
                    aux.append(prep_gen(tc, k + 1))
                aux.append(gmlp_gen(tc, k))
                if k >= 1:
                    aux.append(post_gen(tc, k - 1))
                run(scan_gen(tc, k), aux)
            run(None, [post_gen(tc, nsb - 1)])

        def w_out_phase(tc):
            T = tc.T
            wv = []
            ld_state["hold"] = ld_state["next"]
            for l in range(3):
                sl, stok = next_slot(("wout", l))
                lo, hi = 3 * l, min(3 * l + 3, 8)
                w = sl[:, 0:(hi - lo) * D].rearrange("p (j n) -> p j n", n=D)
                for jj in range(hi - lo):
                    wv.append((w[:, jj, :], stok))
            for dc in range(8):
                b = nb()
                for kc in range(8):
                    mm(ps[:, b, 0:T], wv[kc][0][:, dc * 128:(dc + 1) * 128], n_[:, kc, 0:T], kc == 0, kc == 7,
                       [wv[kc][1], "n%d" % kc], [PB(b)])
                gate_add(tc, dc, ps[:, b, 0:T], 5, [PB(b)])
            ld_state["hold"] = None

        def final_norm(tc):
            T = tc.T
            rms_stats(tc, lambda c: xt[:, c, 0:T], 8, onesD_bf[:], lambda c: ["xt%d" % c], 0)
            yv = yT_o.rearrange("(c p) t -> p c t", p=128)
            for c in range(8):
                k = c % 2
                stt("dve", th[k][:, 0:T], xt[:, c, 0:T], V("final_g", c), rstd[:, 0:T], ALU.mult, ALU.mult,
                    ["xt%d" % c, "rstd", "vec"], ["th%d" % k])
                S.dma("sp", yv[:, c, tc.col0:tc.col0 + T], th[k][:, 0:T], reads=["th%d" % k], key="yo%d" % k)

        ones512_bf = sb("ones512_bf", [128, 128], BF16)
        memset("pool", ones512_bf[:], 1.0 / 512, ["ones512_bf"])

        xv = xT_d.rearrange("(c p) t -> p c t", p=128)
        phases = [lambda tc: norm_mod(tc, 0, 1), lambda tc: ffn(tc, 0, 2), lambda tc: norm_mod(tc, 3, 4), w_in_phase,
                  mixer_phase, w_out_phase, lambda tc: norm_mod(tc, 6, 7),
                  lambda tc: ffn(tc, 1, 8), final_norm]
        pcount = 0
        for tc in (tiles if stop is None or stop >= 0 else []):
            T = tc.T
            S.dma("sp", xt[:, :, 0:T], xv[:, :, tc.col0:tc.col0 + T], writes=["xt%d" % c for c in range(8)], key="xin")
            for ph in phases:
                if stop is not None and pcount >= stop:
                    break
                ph(tc)
                pcount += 1
            if stop is not None and pcount >= stop:
                break
        S.dma("sp", sh_o, shout[:], reads=["shout"], key="sho")
        fk = ["yo0", "yo1", "sho", "cvo", "dbg"] + ["wk%d_%d" % (i, g) for i in range(NMB) for g in range(2)]
        S.emit(final_keys=fk)
    return nc


def _chunkvec(v):
    v = np.asarray(v, np.float32).reshape(-1)
    return np.ascontiguousarray(v.reshape(-1, 128).T)


def kernel(x_prompt, x_sample, state_shift, state_wkv, c_prompt, c_sample, w_ada, b_ada, ffn1_gu,
           ffn1_dn, w_in, mu_shift, w0, w_lora_up, a0, a_lora_up, g_lora_up, k_k, k_a, r_k, gn_w,
           gn_b, ln_v_g, ln_v_b, w_s, b_s, w_out, ffn2_gu, ffn2_dn, final_g):
    f = lambda a: np.asarray(a, np.float32)
    x_prompt, x_sample, state_shift, state_wkv = f(x_prompt), f(x_sample), f(state_shift), f(state_wkv)
    c_prompt, c_sample = f(c_prompt), f(c_sample)
    vecs = np.concatenate([_chunkvec(v) for v in (b_ada[0], mu_shift[0], w0[0], a0[0], k_k[0], k_a[0], f(r_k[0]).reshape(-1),
                                                   gn_w[0], gn_b[0], ln_v_g[0], ln_v_b[0], final_g)], axis=1)
    assert vecs.shape == (128, NV)

    def gu_layout(w):
        w = f(w).reshape(8, 128, 2, FC, 128)
        return np.ascontiguousarray(w.transpose(3, 1, 2, 0, 4)).reshape(FC, 128, 2048)

    def dn_layout(w):
        w = f(w).reshape(FC, 128, 8, 128)
        return np.ascontiguousarray(w.transpose(2, 1, 0, 3)).reshape(8, 128, DFF)

    gu_h = np.stack([gu_layout(ffn1_gu[0]), gu_layout(ffn2_gu[0])])
    dn_h = np.stack([dn_layout(ffn1_dn[0]), dn_layout(ffn2_dn[0])])
    win_h = np.ascontiguousarray(f(w_in[0]).reshape(8, 128, FC, 128).transpose(2, 1, 0, 3)).reshape(FC, 128, D)
    wout_h = np.ascontiguousarray(f(w_out[0]).reshape(8, 128, D))
    lora_h = np.zeros((128, 1536), np.float32)
    lora_h[0:64, 0:512] = f(w_lora_up[0])
    lora_h[64:128, 512:1024] = f(a_lora_up[0])
    lora_h[:, 1024:1536] = f(g_lora_up[0])
    ws = f(w_s[0])
    wsT_h = np.ascontiguousarray(ws.transpose(2, 0, 1))
    rep_h = np.ascontiguousarray(np.tile(ws[:, 0:8, 0:8].transpose(2, 0, 1), (16, 1, 1)))
    bs = f(b_s[0])
    bsb_h = np.ascontiguousarray(np.repeat(bs.reshape(4, 2, 1, 128), 64, axis=2).transpose(1, 2, 0, 3)).reshape(128, 4, 128)
    wada = np.ascontiguousarray(f(w_ada[0]))

    in_maps = []
    for i in range(NCORES):
        ss = slice(NSEQ_S * i, NSEQ_S * (i + 1))
        xT = np.ascontiguousarray(np.concatenate([x_prompt[i].T, x_sample[ss].reshape(TS, D).T], axis=1))
        call = np.concatenate([c_prompt[i:i + 1], c_sample[ss]], axis=0)
        cT_h = np.ascontiguousarray(call.reshape(17, 8, 128).transpose(2, 1, 0))
        sshT_h = np.ascontiguousarray(state_shift[0, ss].reshape(NSEQ_S, 14, 128).transpose(2, 1, 0))
        wk = state_wkv[0, ss].reshape(NSEQ_S, 4, 2, 64, 64)
        wkv_h = np.ascontiguousarray(wk.transpose(2, 4, 0, 1, 3)).reshape(128, NSEQ_S, 4, 64)
        in_maps.append({"xT": xT, "cT": cT_h, "vecs": vecs, "w_ada": wada, "gu_h": gu_h, "dn_h": dn_h,
                        "win_h": win_h, "wout_h": wout_h, "lora_h": lora_h, "wsT_h": wsT_h, "rep_h": rep_h,
                        "bsb_h": bsb_h, "sshT_h": sshT_h, "wkv_h": wkv_h})
    nc = build_program()
    res = run_bass_kernel_spmd(nc, in_maps, core_ids=list(range(NCORES)))
    R = res.results
    y_prompt = np.zeros((8, SEQ, D), np.float32)
    y_sample = np.zeros((128, TL_S, D), np.float32)
    nsp = np.zeros((1, 8, R_COLS), np.float32)
    nwp = np.zeros((1, 8, 8, 64, 64), np.float32)
    nss = np.zeros((1, 128, R_COLS), np.float32)
    nws = np.zeros((1, 128, 8, 64, 64), np.float32)
    ncv = np.zeros((1, 128, TL_S, 512), np.float32)
    for i in range(NCORES):
        r = R[i]
        ss = slice(NSEQ_S * i, NSEQ_S * (i + 1))
        yT = np.asarray(r["yT"], np.float32)
        y_prompt[i] = yT[:, :SEQ].T
        y_sample[ss] = yT[:, SEQ:].T.reshape(NSEQ_S, TL_S, D)
        sh = np.asarray(r["shT_o"], np.float32)
        shf = sh.transpose(2, 1, 0).reshape(17, R_COLS)
        nsp[0, i] = shf[0]
        nss[0, ss] = shf[1:]
        wk = np.asarray(r["wkv_o"], np.float32).reshape(2, 64, 17, 4, 64)
        wk = wk.transpose(2, 3, 0, 4, 1).reshape(17, 8, 64, 64)
        nwp[0, i] = wk[0]
        nws[0, ss] = wk[1:]
        cv = np.asarray(r["cvT_o"], np.float32)
        ncv[0, ss] = cv.transpose(2, 1, 0).reshape(NSEQ_S, TL_S, 512)
    return (y_prompt, y_sample, nsp, nwp, nss, nws, ncv)
```

```python
import contextlib
import numpy as np
import concourse.bass as bass
import concourse.mybir as mybir
from concourse.bass_utils import run_bass_kernel_spmd

F32 = mybir.dt.float32
BF16 = mybir.dt.bfloat16
AF = mybir.ActivationFunctionType
ALU = mybir.AluOpType

NCORES = 8
D = 1024
DFF = 2816
FC = 22
SEQ = 2048
NSEQ_S = 16
TL_S = 8
TS = NSEQ_S * TL_S
NTOK = SEQ + TS
TP = 512
R_COLS = 1792
N_MOD = 9
SCAN_SUB = [None]
DBGF = set()
W_C = float(np.exp(-0.5))

ENGS = ("pe", "act", "dve", "pool", "sp")
SAME_ENGINE_SYNC = {"pe": False, "act": True, "dve": True, "pool": True, "sp": False}


class Op:
    __slots__ = ("eng", "fn", "deps", "dma_key", "dma_val", "needs_inc", "seq", "idx")

    def __init__(self, eng, fn, dma_key=None):
        self.eng = eng
        self.fn = fn
        self.deps = []
        self.dma_key = dma_key
        self.dma_val = 0
        self.needs_inc = False
        self.seq = 0


class Sched:
    def __init__(self, nc):
        self.nc = nc
        self.ops = {e: [] for e in ENGS}
        self.last_w = {}
        self.readers = {}
        self.dma_cnt = {}
        self.n = 0

    def op(self, eng, fn, reads=(), writes=(), dma_key=None):
        o = Op(eng, fn, dma_key)
        o.idx = self.n
        self.n += 1
        deps = {}
        for r in reads:
            w = self.last_w.get(r)
            if w is not None:
                deps[id(w)] = w
        for r in writes:
            w = self.last_w.get(r)
            if w is not None:
                deps[id(w)] = w
            for rd in self.readers.get(r, ()):
                deps[id(rd)] = rd
        o.deps = list(deps.values())
        for r in reads:
            lst = self.readers.setdefault(r, [])
            if dma_key is None:
                for k_ in range(len(lst)):
                    if lst[k_].dma_key is None and lst[k_].eng == eng:
                        lst[k_] = o
                        break
                else:
                    lst.append(o)
            else:
                lst.append(o)
        for r in writes:
            self.last_w[r] = o
            self.readers[r] = []
        if dma_key is not None:
            self.dma_cnt[dma_key] = self.dma_cnt.get(dma_key, 0) + 16
            o.dma_val = self.dma_cnt[dma_key]
        self.ops[eng].append(o)
        return o

    def dma(self, eng, out, in_, reads=(), writes=(), key=None):
        assert key is not None
        return self.op(eng, lambda e: e.dma_start(out=out, in_=in_), reads, writes, dma_key=key)

    def emit(self, final_keys=()):
        nc = self.nc
        for e in ENGS:
            for o in self.ops[e]:
                for d in o.deps:
                    if d.dma_key is not None:
                        continue
                    if d.eng == o.eng and not SAME_ENGINE_SYNC[o.eng]:
                        continue
                    d.needs_inc = True
        for e in ENGS:
            c = 0
            for o in self.ops[e]:
                if o.dma_key is None and o.needs_inc:
                    c += 1
                    o.seq = c
        with contextlib.ExitStack() as st:
            esem = {e: st.enter_context(nc.semaphore("s_" + e)) for e in ENGS}
            dsem = {k: st.enter_context(nc.semaphore("d_" + str(k))) for k in self.dma_cnt}
            block = st.enter_context(nc.Block())
            engobj = {"pe": "tensor", "act": "scalar", "dve": "vector", "pool": "gpsimd", "sp": "sync"}

            def make(e):
                def body(eng):
                    waited = {}
                    for o in self.ops[e]:
                        need = {}
                        for d in o.deps:
                            if d.dma_key is not None:
                                key, val, sem = ("d", d.dma_key), d.dma_val, dsem[d.dma_key]
                            else:
                                if d.eng == e and not SAME_ENGINE_SYNC[e]:
                                    continue
                                key, val, sem = ("e", d.eng), d.seq, esem[d.eng]
                            if key not in need or need[key][0] < val:
                                need[key] = (val, sem)
                        for key, (val, sem) in need.items():
                            if waited.get(key, 0) >= val:
                                continue
                            waited[key] = val
                            eng.wait_ge(sem, val)
                        ins = o.fn(eng)
                        if o.dma_key is not None:
                            ins.then_inc(dsem[o.dma_key], 16)
                        elif o.needs_inc:
                            ins.then_inc(esem[e], 1)
                    if e == "sp":
                        for k in final_keys:
                            if k in self.dma_cnt:
                                eng.wait_ge(dsem[k], self.dma_cnt[k])
                return body

            for e in ENGS:
                getattr(block, engobj[e])(make(e))


VEC_SPECS = [("b_ada", 72), ("mu", 14), ("w0", 4), ("a0", 4), ("k_k", 4), ("k_a", 4), ("r_k", 4),
             ("gn_w", 4), ("gn_b", 4), ("ln_g", 4), ("ln_b", 4), ("final_g", 8)]
VOFF = {}
_o = 0
for _n, _k in VEC_SPECS:
    VOFF[_n] = _o
    _o += _k
NV = _o
VD = {"onem_mu": NV, "half_w0": NV + 14, "half_a0": NV + 18, "onem_ka": NV + 22}
NVT = NV + 26


class TileCtx:
    def __init__(self, idx, col0, nseq, tl, C, sample, last_prompt):
        self.idx = idx
        self.col0 = col0
        self.nseq = nseq
        self.tl = tl
        self.T = nseq * tl
        self.C = C
        self.sample = sample
        self.last_prompt = last_prompt


def build_program(stop=None, dbg=False):
    nc = bass.Bass("TRN2", target_bir_lowering=False)

    def din(name, shape, dt=F32):
        return nc.dram_tensor(name, list(shape), dt, kind="ExternalInput").ap()

    def dout(name, shape, dt=F32):
        return nc.dram_tensor(name, list(shape), dt, kind="ExternalOutput").ap()

    def dscr(name, shape, dt=BF16):
        return nc.dram_tensor(name, list(shape), dt, kind="Internal").ap()

    xT_d = din("xT", [D, NTOK])
    cT_d = din("cT", [128, 8, 17])
    vec_d = din("vecs", [128, NV])
    wada_d = din("w_ada", [D, N_MOD * D])
    gu_d = din("gu_h", [2, FC, 128, 2048])
    dn_d = din("dn_h", [2, 8, 128, DFF])
    win_d = din("win_h", [FC, 128, D])
    wout_d = din("wout_h", [8, 128, D])
    lora_d = din("lora_h", [128, 1536])
    wsT_d = din("wsT_h", [128, 8, 128])
    rep_d = din("rep_h", [128, 8, 8])
    bsb_d = din("bsb_h", [128, 4, 128])
    ssh_d = din("sshT_h", [128, 14, NSEQ_S])
    wkv_d = din("wkv_h", [128, NSEQ_S, 4, 64])

    yT_o = dout("yT", [D, NTOK])
    sh_o = dout("shT_o", [128, 14, 17])
    wkv_o = dout("wkv_o", [128, 17, 4, 64])
    cv_o = dout("cvT_o", [128, 4, TS])

    if dbg:
        dbg_y = dout("dbg_y", [128, 4, NTOK])
        dbg_xm = dout("dbg_xm", [128, 14, NTOK])
        dbg_lw = dout("dbg_lw", [128, 4, NTOK])
        dbg_a = dout("dbg_a", [128, 4, NTOK])
        dbg_kkn = dout("dbg_kkn", [128, 4, NTOK])
    gu_s = dscr("gu_s", [2, FC, 128, 2048])
    dn_s = dscr("dn_s", [2, 8, 128, DFF])
    win_s = dscr("win_s", [FC, 128, D])
    wout_s = dscr("wout_s", [8, 128, D])

    st = contextlib.ExitStack()
    with st:
        def sb(name, shape, dt):
            return st.enter_context(nc.sbuf_tensor(name, list(shape), dt))

        xt = sb("xt", [128, 8, TP], F32)
        n_ = sb("n_", [128, 8, TP], BF16)
        big = sb("big", [128, 14 * TP], F32)
        slots = [sb("slot%d" % i, [128, 3072], BF16) for i in range(3)]
        P1 = sb("P1", [128, 4, TP], F32)
        P2 = sb("P2", [128, 4, TP], F32)
        P3 = sb("P3", [128, 4, TP], F32)
        P4 = sb("P4", [128, 4, TP], F32)
        G_ = sb("G_", [128, 4, TP], F32)
        ar = sb("ar", [128, 4, 2 * TP], BF16)
        bt = sb("bt", [128, 4, TP], BF16)
        kt = sb("kt", [128, 4, TP], BF16)
        vb = sb("vb", [128, 4, TP], BF16)
        bkT_sb = sb("bkT_sb", [64, 2, 4, 128], BF16)
        vT_sb = sb("vT_sb", [64, 4, 128], BF16)
        vnT_sb = sb("vnT_sb", [128, 4, 128], BF16)
        Pbk_sb = sb("Pbk_sb", [64, 8, 2, 2, 64], BF16)
        A_sb = [sb("A_sb%d" % i, [64, 8, 64], BF16) for i in range(2)]
        N_sb = [sb("N_sb%d" % i, [64, 8, 64], BF16) for i in range(2)]
        Q_sb = [sb("Q_sb%d" % i, [64, 8, 64], BF16) for i in range(2)]
        A_all = sb("A_all", [64, 4, 8, 64], BF16)
        N_all = sb("N_all", [64, 4, 8, 64], BF16)
        D_sb = [sb("D_sb%d" % i, [64, 8, 64], BF16) for i in range(2)]
        msk4_A = sb("msk4_A", [64, 4, 64], F32)
        msk4_N = sb("msk4_N", [64, 4, 64], BF16)
        msk4_Nf = sb("msk4_Nf", [64, 4, 64], F32)
        Xs = sb("Xs", [64, 4, 128], BF16)
        Us = sb("Us", [64, 4, 128], BF16)
        NMB = 2
        M_pad = [sb("M_pad%d" % i, [128, 4, 128], F32) for i in range(NMB)]
        Mb_pad = [sb("Mb_pad%d" % i, [128, 4, 128], BF16) for i in range(NMB)]
        onesf = sb("onesf", [128, 128], F32)
        ident_f = sb("ident_f", [128, 128], F32)
        ident_bf = sb("ident_bf", [128, 128], BF16)
        onesD_bf = sb("onesD_bf", [128, 128], BF16)
        ones512_f = sb("ones512_f", [128, 128], F32)
        blk64m_f = sb("blk64m_f", [128, 128], F32)
        blk64m_bf = sb("blk64m_bf", [128, 128], BF16)
        blk64o_bf = sb("blk64o_bf", [128, 128], BF16)
        msk_ar = sb("msk_ar", [64, 2, 64], F32)
        m_A = sb("m_A", [64, 64], F32)
        rm64 = sb("rm64", [128, TP], F32)
        rm8 = sb("rm8", [128, TS], F32)
        cmk = sb("cmk", [128, 16, 8], F32)
        cst = sb("cst", [128, 4], F32)
        WsT = sb("WsT", [128, 8, 128], BF16)
        BsT = sb("BsT", [128, 8, 128], BF16)
        rep_f = sb("rep_f", [128, 8, 8], F32)
        bsb = sb("bsb", [128, 4, 128], F32)
        lora_bf = sb("lora_bf", [128, 1536], BF16)
        vec = sb("vec_sb", [128, NVT], F32)
        mod = sb("mod", [128, 72, 17], F32)
        cT = sb("cT_sb", [128, 8, 17], F32)
        cs_bf = sb("cs_bf", [128, 8, 17], BF16)
        carry = sb("carry", [128, 14, 1], F32)
        sshT = sb("sshT", [128, 14, NSEQ_S], F32)
        shout = sb("shout", [128, 14, 17], F32)
        th = [sb("th%d" % i, [128, TP], F32) for i in range(2)]
        sq = [sb("sq%d" % i, [128, TP], BF16) for i in range(2)]
        lin_bf, sg_bf = sq[0], sq[1]
        rstd = sb("rstd", [128, TP], F32)
        t1 = [sb("t1_%d" % i, [128, TP], F32) for i in range(2)]
        tA = sb("tA", [128, TP], F32)
        tB = sb("tB", [128, TP], F32)
        ps = st.enter_context(nc.psum_tensor("ps", [128, 8, 512], F32))

        S = Sched(nc)
        bank_ctr = [0]

        def nb():
            b = bank_ctr[0] % 8
            bank_ctr[0] += 1
            return b

        def PB(b):
            return "ps%d" % b

        scan_ctr = [0]
        aux_ctr = [0]

        def nbs():
            b = scan_ctr[0] % 5
            scan_ctr[0] += 1
            return b

        def nba():
            b = 5 + aux_ctr[0] % 3
            aux_ctr[0] += 1
            return b

        def mm(out, lhsT, rhs, start, stop, r, w):
            S.op("pe", lambda e: e.matmul(out, lhsT=lhsT, rhs=rhs, start=start, stop=stop), r, w)

        def tr(out, in_, r, w):
            S.op("pe", lambda e: e.transpose(out=out, in_=in_, identity=ident_bf[:]), list(r) + ["ident_bf"], w)

        def act(out, in_, func, r, w, bias=None, scale=1.0):
            if bias is None:
                S.op("act", lambda e: e.activation(out=out, in_=in_, func=func, scale=scale), r, w)
            else:
                S.op("act", lambda e: e.activation(out=out, in_=in_, func=func, bias=bias, scale=scale), r, w)

        def cp(eng, out, in_, r, w):
            if eng == "act":
                S.op("act", lambda e: e.copy(out=out, in_=in_), r, w)
            else:
                S.op(eng, lambda e: e.tensor_copy(out=out, in_=in_), r, w)

        def tt(eng, out, in0, in1, op, r, w):
            S.op(eng, lambda e: e.tensor_tensor(out=out, in0=in0, in1=in1, op=op), r, w)

        def ts(eng, out, in0, s1, s2, op0, op1, r, w):
            if s2 is None:
                S.op(eng, lambda e: e.tensor_scalar(out=out, in0=in0, scalar1=s1, scalar2=None, op0=op0), r, w)
            else:
                S.op(eng, lambda e: e.tensor_scalar(out=out, in0=in0, scalar1=s1, scalar2=s2, op0=op0, op1=op1), r, w)

        def stt(eng, out, in0, scalar, in1, op0, op1, r, w):
            S.op(eng, lambda e: e.scalar_tensor_tensor(out=out, in0=in0, scalar=scalar, in1=in1, op0=op0, op1=op1), r, w)

        def memset(eng, ap, val, w):
            S.op(eng, lambda e: e.memset(ap, val), (), w)

        def asel(out, in_, pattern, op, base, cm, r, w):
            S.op("pool", lambda e: e.affine_select(out=out, in_=in_, pattern=pattern, compare_op=op, fill=0.0,
                                                   base=base, channel_multiplier=cm), r, w)

        def V(name, c=0, n=1):
            o = VOFF[name] if name in VOFF else VD[name]
            return vec[:, o + c:o + c + n]

        S.dma("sp", vec[:, 0:NV], vec_d, writes=["vec"], key="k_vec")
        S.dma("sp", cT[:], cT_d, writes=["cT"], key="k_cT")
        memset("pool", onesf[:], 1.0, ["onesf"])
        asel(ident_f[:], onesf[:], [[1, 128]], ALU.is_equal, 0, -1, ["onesf"], ["ident_f"])
        cp("pool", ident_bf[:], ident_f[:], ["ident_f"], ["ident_bf"])
        memset("pool", onesD_bf[:], 1.0 / D, ["onesD_bf"])
        memset("pool", ones512_f[:], 1.0 / 512, ["ones512_f"])
        for (t_, v_) in ((blk64m_f, 1.0 / 64), (blk64m_bf, 1.0 / 64), (blk64o_bf, 1.0)):
            memset("pool", t_[:], 0.0, [t_.name])
            memset("pool", t_[0:64, 0:64], v_, [t_.name])
            memset("pool", t_[64:128, 64:128], v_, [t_.name])
        asel(msk_ar[:, 0, :], onesf[0:64, 0:64], [[1, 64]], ALU.is_gt, 0, -1, ["onesf"], ["msk_ar"])
        asel(msk_ar[:, 1, :], onesf[0:64, 0:64], [[1, 64]], ALU.is_ge, 0, -1, ["onesf"], ["msk_ar"])
        asel(m_A[:], onesf[0:64, 0:64], [[-1, 64]], ALU.is_gt, 0, 1, ["onesf"], ["m_A"])
        o64 = onesf[0:64, 0:64]
        vA = msk4_A[:, 0, :].rearrange("p (a b) -> p a b", b=8)
        asel(msk4_A[:, 0, :], o64, [[-1, 64]], ALU.is_gt, 0, 1, ["onesf"], ["msk4_A"])
        asel(vA, vA, [[-8, 8], [0, 8]], ALU.is_ge, 0, 1, ["msk4_A"], ["msk4_A"])
        asel(vA, vA, [[8, 8], [0, 8]], ALU.is_ge, 7, -1, ["msk4_A"], ["msk4_A"])
        vN = msk4_Nf[:, 0, :].rearrange("p (a b) -> p a b", b=8)
        asel(msk4_Nf[:, 0, :], o64, [[1, 64]], ALU.is_gt, 0, -1, ["onesf"], ["msk4_Nf"])
        asel(vN, vN, [[-8, 8], [0, 8]], ALU.is_ge, 0, 1, ["msk4_Nf"], ["msk4_Nf"])
        asel(vN, vN, [[8, 8], [0, 8]], ALU.is_ge, 7, -1, ["msk4_Nf"], ["msk4_Nf"])
        for v_, m_ in ((1, 8), (2, 16), (3, 32)):
            nb_ = 64 // (2 * m_)
            wA = msk4_A[:, v_, :].rearrange("p (a b) -> p a b", b=2 * m_)
            memset("pool", msk4_A[:, v_, :], 1.0, ["msk4_A"])
            asel(wA, wA, [[-2 * m_, nb_], [0, 2 * m_]], ALU.is_ge, -m_, 1, ["msk4_A"], ["msk4_A"])
            asel(wA, wA, [[2 * m_, nb_], [0, 2 * m_]], ALU.is_ge, 2 * m_ - 1, -1, ["msk4_A"], ["msk4_A"])
            asel(wA, wA, [[0, nb_], [-1, 2 * m_]], ALU.is_ge, m_ - 1, 0, ["msk4_A"], ["msk4_A"])
            wN = msk4_Nf[:, v_, :].rearrange("p (a b) -> p a b", b=2 * m_)
            memset("pool", msk4_Nf[:, v_, :], 1.0, ["msk4_Nf"])
            asel(wN, wN, [[-2 * m_, nb_], [0, 2 * m_]], ALU.is_ge, 0, 1, ["msk4_Nf"], ["msk4_Nf"])
            asel(wN, wN, [[2 * m_, nb_], [0, 2 * m_]], ALU.is_ge, m_ - 1, -1, ["msk4_Nf"], ["msk4_Nf"])
            asel(wN, wN, [[0, nb_], [1, 2 * m_]], ALU.is_ge, -m_, 0, ["msk4_Nf"], ["msk4_Nf"])
        cp("pool", msk4_N[:], msk4_Nf[:], ["msk4_Nf"], ["msk4_N"])
        memset("pool", rm64[:], 1.0, ["rm64"])
        memset("pool", rm64[:].rearrange("p (a b) -> p a b", b=64)[:, :, 0:1], 0.0, ["rm64"])
        memset("pool", rm8[:], 1.0, ["rm8"])
        memset("pool", rm8[:].rearrange("p (a b) -> p a b", b=8)[:, :, 0:1], 0.0, ["rm8"])
        memset("pool", cmk[:], 1.0, ["cmk"])
        asel(cmk[:], cmk[:], [[8, 16], [1, 8]], ALU.is_ge, 0, -1, ["cmk"], ["cmk"])
        asel(cmk[:], cmk[:], [[-8, 16], [0, 8]], ALU.is_ge, 0, 1, ["cmk"], ["cmk"])
        memset("pool", cst[:, 0:1], 1e-6, ["cst"])
        memset("pool", cst[:, 1:2], 1e-5, ["cst"])
        memset("pool", cst[:, 2:3], 64e-5, ["cst"])
        memset("pool", cst[:, 3:4], 1e-24, ["cst"])
        for i in range(NMB):
            memset("pool", M_pad[i][:], 0.0, ["M_pad%d_g0" % i, "M_pad%d_g1" % i])
            memset("pool", Mb_pad[i][:], 0.0, ["Mb_pad%d_g0" % i, "Mb_pad%d_g1" % i])
        memset("pool", carry[:], 0.0, ["carry"])
        ts("dve", V("onem_mu", 0, 14), V("mu", 0, 14), -1.0, 1.0, ALU.mult, ALU.add, ["vec"], ["vec"])
        ts("dve", V("half_w0", 0, 4), V("w0", 0, 4), 0.5, None, ALU.mult, None, ["vec"], ["vec"])
        ts("dve", V("half_a0", 0, 4), V("a0", 0, 4), 0.5, None, ALU.mult, None, ["vec"], ["vec"])
        ts("dve", V("onem_ka", 0, 4), V("k_a", 0, 4), -1.0, 1.0, ALU.mult, ALU.add, ["vec"], ["vec"])
        wsT_f = P3[:, 0:2, :].rearrange("p a (b n) -> p (a b) n", n=128)
        S.dma("sp", wsT_f, wsT_d, writes=["P3"], key="k_ws")
        S.dma("sp", rep_f[:], rep_d, writes=["rep_f"], key="k_rep")
        S.dma("sp", bsb[:], bsb_d, writes=["bsb"], key="k_bsb")
        S.dma("sp", sshT[:], ssh_d, writes=["sshT"], key="k_ssh")
        S.dma("pool", lora_bf[:], lora_d, writes=["lora_bf"], key="k_lora")
        asel(wsT_f, wsT_f, [[0, 8], [1, 128]], ALU.is_ge, 0, -1, ["P3"], ["P3"])
        cp("pool", WsT[:], wsT_f, ["P3"], ["WsT"])
        tt("pool", BsT[:].rearrange("p h (s i) -> p h s i", i=8),
           rep_f[:].unsqueeze(2).broadcast_to([128, 8, 16, 8]),
           cmk[:].unsqueeze(1).broadcast_to([128, 8, 16, 8]), ALU.mult, ["rep_f", "cmk"], ["BsT"])

        act(tA[:, 0:136], cT[:].rearrange("p a b -> p (a b)"), AF.Tanh, ["cT"], ["tA"], scale=0.5)
        stt("dve", tA[:, 0:136], tA[:, 0:136], 1.0, cT[:].rearrange("p a b -> p (a b)"), ALU.add, ALU.mult, ["tA", "cT"], ["tA"])
        ts("dve", cs_bf[:].rearrange("p a b -> p (a b)"), tA[:, 0:136], 0.5, None, ALU.mult, None, ["tA"], ["cs_bf"])
        ada_buf = [P1, P2]
        wada_v = wada_d.rearrange("(c p) n -> p c n", p=128)
        for pc in range(18):
            buf = ada_buf[pc % 2]
            bv = buf[:].rearrange("p a b -> p (a b)").bitcast(BF16).rearrange("p (c n) -> p c n", n=512)
            S.dma("pool", bv, wada_v[:, :, pc * 512:(pc + 1) * 512], writes=[buf.name], key="ad%d" % (pc % 2))
            b = nb()
            for jc in range(4):
                for c in range(8):
                    mm(ps[:, b, jc * 17:(jc + 1) * 17], bv[:, c, jc * 128:(jc + 1) * 128], cs_bf[:, c, :],
                       c == 0, c == 7, [buf.name, "cs_bf"], [PB(b)])
            tt("dve", mod[:, pc * 4:(pc + 1) * 4, :], ps[:, b, 0:68].rearrange("p (a b) -> p a b", b=17),
               V("b_ada", pc * 4, 4).unsqueeze(2).broadcast_to([128, 4, 17]), ALU.add, [PB(b), "vec"], ["mod"])
        for m in (1, 4, 7):
            ts("dve", mod[:, m * 8:(m + 1) * 8, :], mod[:, m * 8:(m + 1) * 8, :], 1.0, None, ALU.add, None, ["mod"], ["mod"])
        for m in (2, 8):
            ts("dve", mod[:, m * 8:(m + 1) * 8, :], mod[:, m * 8:(m + 1) * 8, :], 0.25, None, ALU.mult, None, ["mod"], ["mod"])

        cvn = [0]

        def conv(out, in_, tok):
            k = "cv%d" % (cvn[0] % 8)
            cvn[0] += 1
            S.dma("pool", out, in_, reads=[("cvkey", k)], writes=[tok, ("cvkey", k)], key=k)

        def conv_ffn(f):
            for j in range(FC):
                conv(gu_s[f, j], gu_d[f, j], ("gu", f, j))
            for dc in range(8):
                conv(dn_s[f, dc], dn_d[f, dc], ("dn", f, dc))

        conv_ffn(0)
        for l in range(8):
            lo, hi = 3 * l, min(3 * l + 3, FC)
            conv(win_s[lo:hi], win_d[lo:hi], ("win", l))

        def conv_batch_b():
            nB = [0]

            def convb(out, in_, tok):
                S.dma("pool", out, in_, writes=[tok], key="cvB%d" % nB[0])
                nB[0] += 1
            for l in range(3):
                lo, hi = 3 * l, min(3 * l + 3, 8)
                convb(wout_s[lo:hi], wout_d[lo:hi], ("wout", l))
            for j in range(FC):
                convb(gu_s[1, j], gu_d[1, j], ("gu", 1, j))
            for dc in range(8):
                convb(dn_s[1, dc], dn_d[1, dc], ("dn", 1, dc))

        tiles = []
        for i in range(4):
            tiles.append(TileCtx(i, i * TP, 1, TP, 64, False, i == 3))
        tiles.append(TileCtx(4, SEQ, NSEQ_S, TL_S, 8, True, False))
        tiles = [tiles[4]] + tiles[:4]
        if "pesync" in DBGF:
            SAME_ENGINE_SYNC["pe"] = True

        loads = []
        for tcx in tiles:
            loads += [("gu", 0, j) for j in range(FC)] + [("dn", 0, dc) for dc in range(8)]
            loads += [("win", l) for l in range(8)] + [("wout", l) for l in range(3)]
            loads += [("gu", 1, j) for j in range(FC)] + [("dn", 1, dc) for dc in range(8)]
        ld_state = {"issued": 0, "next": 0, "hold": None}

        def issue_load(i):
            kind = loads[i]
            s = slots[i % 3]
            tok = "slot%d" % (i % 3)
            if kind[0] == "gu":
                S.dma("sp", s[:, 0:2048], gu_s[kind[1], kind[2]], reads=[kind], writes=[tok], key="w%d" % (i % 3))
            elif kind[0] == "dn":
                S.dma("sp", s[:, 0:DFF], dn_s[kind[1], kind[2]], reads=[kind], writes=[tok], key="w%d" % (i % 3))
            else:
                src = win_s if kind[0] == "win" else wout_s
                tot = FC if kind[0] == "win" else 8
                lo, hi = 3 * kind[1], min(3 * kind[1] + 3, tot)
                S.dma("sp", s[:, 0:(hi - lo) * D].rearrange("p (j n) -> p j n", n=D),
                      src[lo:hi].rearrange("j p n -> p j n"), reads=[kind], writes=[tok], key="w%d" % (i % 3))

        def next_slot(expect):
            i = ld_state["next"]
            assert loads[i][:len(expect)] == expect, (loads[i], expect)
            base = i if ld_state["hold"] is None else ld_state["hold"]
            while ld_state["issued"] < min(base + 3, len(loads)):
                issue_load(ld_state["issued"])
                ld_state["issued"] += 1
            ld_state["next"] += 1
            return slots[i % 3], "slot%d" % (i % 3)

        def hid_view(tc):
            return big[:, 0:FC * TP // 2].bitcast(BF16).rearrange("p (j t) -> p j t", t=TP)

        def hid_tok(j):
            return ["big%d" % j]

        def xm_tok(jc):
            return ["big%d" % (2 * jc), "big%d" % (2 * jc + 1)]

        xm = big[:].rearrange("p (j t) -> p j t", t=TP)

        def s3(tc, ap2):
            return ap2.rearrange("p (s t) -> p s t", t=tc.tl)

        def modap(tc, m, c):
            if not tc.sample:
                return mod[:, m * 8 + c, 0:1]
            return mod[:, m * 8 + c, 1:17].unsqueeze(2).broadcast_to([128, NSEQ_S, TL_S])

        def rms_stats(tc, xs, nchunks, lhs, toks_in, eps_col):
            T = tc.T
            b = nb()
            for c in range(nchunks):
                k = c % 2
                act(sq[k][:, 0:T], xs(c), AF.Square, toks_in(c), ["sq%d" % k])
                mm(ps[:, b, 0:T], lhs, sq[k][:, 0:T], c == 0, c == nchunks - 1, ["sq%d" % k], [PB(b)])
            act(rstd[:, 0:T], ps[:, b, 0:T], AF.Sqrt, [PB(b), "cst"], ["rstd"], bias=cst[:, eps_col:eps_col + 1])
            S.op("dve", lambda e: e.reciprocal(out=rstd[:, 0:T], in_=rstd[:, 0:T]), ["rstd"], ["rstd"])

        def norm_mod(tc, m_sh, m_sc):
            T = tc.T
            rms_stats(tc, lambda c: xt[:, c, 0:T], 8, onesD_bf[:], lambda c: ["xt%d" % c], 0)
            for c in range(8):
                k = c % 2
                tt("dve", t1[k][:, 0:T], xt[:, c, 0:T], rstd[:, 0:T], ALU.mult, ["xt%d" % c, "rstd"], ["t1_%d" % k])
                if not tc.sample:
                    act(n_[:, c, 0:T], t1[k][:, 0:T], AF.Identity, ["t1_%d" % k, "mod"], ["n%d" % c],
                        bias=modap(tc, m_sh, c), scale=modap(tc, m_sc, c))
                else:
                    tt("dve", s3(tc, t1[k][:, 0:T]), s3(tc, t1[k][:, 0:T]), modap(tc, m_sc, c), ALU.mult, ["t1_%d" % k, "mod"], ["t1_%d" % k])
                    tt("dve", s3(tc, n_[:, c, 0:T]), s3(tc, t1[k][:, 0:T]), modap(tc, m_sh, c), ALU.add, ["t1_%d" % k, "mod"], ["n%d" % c])

        def gate_add(tc, c, psap, m_g, ptoks):
            T = tc.T
            if not tc.sample:
                stt("dve", xt[:, c, 0:T], psap, modap(tc, m_g, c), xt[:, c, 0:T], ALU.mult, ALU.add,
                    list(ptoks) + ["xt%d" % c, "mod"], ["xt%d" % c])
            else:
                tt("dve", s3(tc, tA[:, 0:T]), s3(tc, psap), modap(tc, m_g, c), ALU.mult, list(ptoks) + ["mod"], ["tA"])
                tt("dve", xt[:, c, 0:T], xt[:, c, 0:T], tA[:, 0:T], ALU.add, ["tA", "xt%d" % c], ["xt%d" % c])

        def ffn(tc, f, m_g):
            T = tc.T
            hid = hid_view(tc)
            for j in range(FC):
                sl, stok = next_slot(("gu", f, j))
                w = sl[:, 0:2048].rearrange("p (g c n) -> p g c n", g=2, c=8)
                bg, bu = nb(), nb()
                for g, b in ((0, bg), (1, bu)):
                    for c in range(8):
                        mm(ps[:, b, 0:T], w[:, g, c, :], n_[:, c, 0:T], c == 0, c == 7, [stok, "n%d" % c], [PB(b)])
                k = j % 2
                act(th[k][:, 0:T], ps[:, bg, 0:T], AF.Tanh, [PB(bg)], ["th%d" % k], scale=0.5)
                stt("dve", th[k][:, 0:T], th[k][:, 0:T], 1.0, ps[:, bg, 0:T], ALU.add, ALU.mult, ["th%d" % k, PB(bg)], ["th%d" % k])
                tt("dve", hid[:, j, 0:T], th[k][:, 0:T], ps[:, bu, 0:T], ALU.mult, ["th%d" % k, PB(bu)], hid_tok(j))
            for dc in range(8):
                sl, stok = next_slot(("dn", f, dc))
                w = sl[:, 0:DFF].rearrange("p (j n) -> p j n", n=128)
                b = nb()
                for j in range(FC):
                    mm(ps[:, b, 0:T], w[:, j, :], hid[:, j, 0:T], j == 0, j == FC - 1, [stok] + hid_tok(j), [PB(b)])
                gate_add(tc, dc, ps[:, b, 0:T], m_g, [PB(b)])

        def w_in_phase(tc):
            T = tc.T
            if tc.sample:
                conv_batch_b()
            nsq, tl = tc.nseq, tc.tl
            for l in range(8):
                sl, stok = next_slot(("win", l))
                lo, hi = 3 * l, min(3 * l + 3, FC)
                w = sl[:, 0:(hi - lo) * D].rearrange("p (j c n) -> p j c n", c=8, n=128)
                for jj in range(hi - lo):
                    jc = lo + jj
                    b = nb()
                    for c in range(8):
                        mm(ps[:, b, 0:T], w[:, jj, c, :], n_[:, c, 0:T], c == 0, c == 7, [stok, "n%d" % c], [PB(b)])
                    p3 = s3(tc, ps[:, b, 0:T])
                    if jc < 14:
                        k = jc % 2
                        tk = "t1_%d" % k
                        tmp3 = s3(tc, t1[k][:, 0:T])
                        ts("dve", tmp3[:, :, 1:tl], p3[:, :, 0:tl - 1], V("mu", jc), None, ALU.mult, None, [PB(b), "vec"], [tk])
                        prev0 = sshT[:, jc, :] if tc.sample else carry[:, jc, :]
                        ts("dve", tmp3[:, :, 0:1], prev0.unsqueeze(2), V("mu", jc), None, ALU.mult, None,
                           ["sshT", "carry", "vec"], [tk])
                        stt("dve", xm[:, jc, 0:T], ps[:, b, 0:T], V("onem_mu", jc), t1[k][:, 0:T], ALU.mult, ALU.add,
                            [PB(b), tk, "vec"], xm_tok(jc))
                        if not tc.sample:
                            cp("dve", carry[:, jc, :], ps[:, b, T - 1:T], [PB(b)], ["carry"])
                            if tc.last_prompt and "noshout" not in DBGF:
                                cp("dve", shout[:, jc, 0:1], ps[:, b, T - 1:T], [PB(b)], ["shout"])
                        else:
                            cp("dve", shout[:, jc, 1:17], p3[:, :, tl - 1], [PB(b)], ["shout"])
                        if jc == 13:
                            for _ in prep_gen(tc, 0):
                                pass
                    elif jc < 18:
                        cp("act", P1[:, jc - 14, 0:T], ps[:, b, 0:T], [PB(b)], ["P1"])
                    else:
                        cp("act", P2[:, jc - 18, 0:T], ps[:, b, 0:T], [PB(b)], ["P2"])

        SBK = 128
        pbf = sb("pbf", [128, 4, SBK], BF16)
        gCs = sb("gCs", [128, 4, 16], F32)

        def bc4(name, w=SBK):
            return V(name, 0, 4).unsqueeze(2).broadcast_to([128, 4, w])

        def v4(t2):
            return t2[:, 0:4 * SBK].rearrange("p (c n) -> p c n", n=SBK)

        def gmlp_gen(tc, qb):
            cols = slice(qb * SBK, (qb + 1) * SBK)
            pu, pv = P1, P2
            pvc, pvt = v4(t1[0]), "t1_0"
            tmix, tmt = v4(t1[1]), "t1_1"
            rstd_g = t1[1][:, 0:SBK]
            sqb = th[1][:, 0:256].bitcast(BF16).rearrange("p (c n) -> p c n", n=SBK)
            vn_bf = sq[1][:, 0:4 * SBK].rearrange("p (c n) -> p c n", n=SBK)
            b = nba()
            for c in range(4):
                mm(ps[:, b, 0:SBK], ones512_f[:], pv[:, c, cols], c == 0, c == 3, ["P2", "ones512_f"], [PB(b)])
            tt("dve", pvc, pv[:, :, cols], ps[:, b, 0:SBK].unsqueeze(1).broadcast_to([128, 4, SBK]), ALU.subtract, ["P2", PB(b)], [pvt])
            act(sqb, pvc, AF.Square, [pvt, "th1"], ["th1a"])
            yield
            b2 = nba()
            for c in range(4):
                mm(ps[:, b2, 0:SBK], ones512_bf[:], sqb[:, c, :], c == 0, c == 3, ["th1a", "ones512_bf"], [PB(b2)])
            act(rstd_g, ps[:, b2, 0:SBK], AF.Sqrt, [PB(b2), "cst"], [tmt], bias=cst[:, 1:2])
            S.op("dve", lambda e: e.reciprocal(out=rstd_g, in_=rstd_g), [tmt], [tmt])
            tt("pool", pvc, pvc, rstd_g.unsqueeze(1).broadcast_to([128, 4, SBK]), ALU.mult, [pvt, tmt], [pvt])
            yield
            for c in range(4):
                act(pvc[:, c, :], pvc[:, c, :], AF.Identity, [pvt, "vec"], [pvt], bias=V("ln_b", c), scale=V("ln_g", c))
            cp("act", vn_bf, pvc, [pvt], ["sq1"])
            if tc.sample:
                S.dma("sp", cv_o, pvc, reads=[pvt], key="cvo")
            yield
            bt_ = nba()
            pT = ps[:, bt_, 0:256].bitcast(BF16).rearrange("p (c n) -> p c n", n=128)
            for c in range(4):
                tr(pT[:, c, :], vn_bf[:, c, :], ["sq1"], [PB(bt_)])
            cp("act", vnT_sb[:], pT, [PB(bt_)], ["vnT_sb"])
            yield
            bm = nba()
            Wm = BsT if tc.sample else WsT
            Wtok = "BsT" if tc.sample else "WsT"
            for c in range(4):
                for hh in range(2):
                    mm(ps[hh * 64:(hh + 1) * 64, bm, c * 128:(c + 1) * 128], vnT_sb[:, c, hh * 64:(hh + 1) * 64],
                       Wm[:, 2 * c + hh, :], True, True, ["vnT_sb", Wtok], [PB(bm)])
            mixp = ps[:, bm, :].rearrange("p (c n) -> p c n", n=128)
            if tc.sample:
                bias_ap = bsb[:, :, 0:8].unsqueeze(2).broadcast_to([128, 4, 16, 8])
                tt("dve", tmix.rearrange("p c (s i) -> p c s i", i=8),
                   mixp.rearrange("p c (s i) -> p c s i", i=8), bias_ap, ALU.add, [PB(bm), "bsb"], [tmt])
            else:
                tt("dve", tmix, mixp, bsb[:], ALU.add, [PB(bm), "bsb"], [tmt])
            tt("pool", n_[:, 4:8, cols], tmix, pu[:, :, cols], ALU.mult, [tmt, "P1"], ["n4", "n5", "n6", "n7"])
            yield

        def prep_gen(tc, sbi):
            T, C = tc.T, tc.C
            cols = slice(sbi * SBK, (sbi + 1) * SBK)
            SB = "_%d" % sbi
            r4, k4, v4_ = xm[:, 0:4, cols], xm[:, 4:8, cols], xm[:, 8:12, cols]
            rtok = [t for c in range(4) for t in xm_tok(c)]
            ktok = [t for c in range(4) for t in xm_tok(4 + c)]
            vtok = [t for c in range(4) for t in xm_tok(8 + c)]
            wdad = xm[:, 12, cols]
            gd = xm[:, 13, cols]
            L, Aa, K, Gi = P4[:, :, 0:128], P4[:, :, 128:256], P4[:, :, 256:384], P4[:, :, 384:512]
            tX, tY = v4(tA), v4(tB)
            lin_bf, sg_bf = sq[0][:, 0:128], sq[0][:, 128:256]
            sqs = [sq[0][:, 256:384], sq[0][:, 384:512]]
            act(lin_bf[0:64, :], wdad[0:64, :], AF.Tanh, xm_tok(12) + ["sq0"], ["sq0a"])
            cp("act", lin_bf[64:128, :], wdad[64:128, :], xm_tok(12), ["sq0a"])
            act(tY[:, 0, :], gd, AF.Tanh, xm_tok(13), ["tB"], scale=0.5)
            ts("pool", sg_bf, tY[:, 0, :], 0.5, 0.5, ALU.mult, ALU.add, ["tB", "sq0"], ["sq0b"])
            yield
            bw, ba, bg = nba(), nba(), nba()
            for c in range(4):
                mm(ps[:, bw, c * 128:(c + 1) * 128], lora_bf[:, c * 128:(c + 1) * 128], lin_bf, True, True, ["lora_bf", "sq0a"], [PB(bw)])
            for c in range(4):
                mm(ps[:, ba, c * 128:(c + 1) * 128], lora_bf[:, 512 + c * 128:512 + (c + 1) * 128], lin_bf, True, True, ["lora_bf", "sq0a"], [PB(ba)])
            for c in range(4):
                mm(ps[:, bg, c * 128:(c + 1) * 128], lora_bf[:, 1024 + c * 128:1024 + (c + 1) * 128], sg_bf, True, True, ["lora_bf", "sq0b"], [PB(bg)])
            for c in range(4):
                act(tX[:, c, :], ps[:, bw, c * 128:(c + 1) * 128], AF.Tanh, [PB(bw), "vec"], ["tA"], bias=V("half_w0", c), scale=0.5)
            ts("pool", L, tX, -0.5 * W_C, -0.5 * W_C, ALU.mult, ALU.add, ["tA"], ["P4a"])
            for c in range(4):
                act(tY[:, c, :], ps[:, ba, c * 128:(c + 1) * 128], AF.Tanh, [PB(ba), "vec"], ["tB"], bias=V("half_a0", c), scale=0.5)
            ts("pool", Aa, tY, 0.5, 0.5, ALU.mult, ALU.add, ["tB"], ["P4b"])
            cp("act", G_[:, :, cols], ps[:, bg, :].rearrange("p (c n) -> p c n", n=128), [PB(bg)], ["G_" + SB])
            yield
            tt("pool", K, k4, bc4("k_k"), ALU.mult, ktok + ["vec"], ["P4c"])
            bk = nba()
            for c in range(4):
                act(sqs[c % 2], K[:, c, :], AF.Square, ["P4c", "sq0"], ["sq0%s" % "cd"[c % 2]])
                mm(ps[:, bk, c * 128:(c + 1) * 128], blk64o_bf[:], sqs[c % 2], True, True, ["sq0%s" % "cd"[c % 2], "blk64o_bf"], [PB(bk)])
            act(tA[:, 0:512], ps[:, bk, :], AF.Sqrt, [PB(bk), "cst"], ["tA"], bias=cst[:, 3:4])
            S.op("dve", lambda e: e.reciprocal(out=tA[:, 0:512], in_=tA[:, 0:512]), ["tA"], ["tA"])
            tt("pool", K, K, tX, ALU.mult, ["P4c", "tA"], ["P4c"])
            yield
            tt("pool", tY, Aa, bc4("k_a"), ALU.mult, ["P4b", "vec"], ["tB"])
            tt("pool", tY, tY, bc4("onem_ka"), ALU.add, ["tB", "vec"], ["tB"])
            tt("dve", k4, k4, tY, ALU.mult, ktok + ["tB"], ktok)
            yield
            rm = rm8 if tc.sample else rm64
            rmt = "rm8" if tc.sample else "rm64"
            for c in range(4):
                S.op("dve", lambda e, c=c: e.tensor_tensor_scan(out=tX[:, c, :], data0=rm[:, 0:SBK], data1=L[:, c, :],
                                                                 initial=0.0, op0=ALU.mult, op1=ALU.add),
                     [rmt, "P4a"], ["tA"])
            act(Gi, tX, AF.Exp, ["tA"], ["P4d"])
            tt("pool", tY, tX, L, ALU.subtract, ["tA", "P4a"], ["tB"])
            act(tY, tY, AF.Exp, ["tB"], ["tB"])
            act(tX, tX, AF.Exp, ["tA"], ["tA"], scale=-1.0)
            nqs = SBK // C
            q0 = sbi * nqs
            cp("pool", gCs[:, :, q0:q0 + nqs], Gi.rearrange("p c (q t) -> p c q t", t=C)[:, :, :, C - 1],
               ["P4d"], ["gCs" + SB])
            yield
            arv = ar[:, :, 0:2 * T].rearrange("p c (q x t) -> p c q x t", x=2, t=C)
            q4 = lambda ap3: ap3.rearrange("p c (q t) -> p c q t", t=C)
            for c in range(4):
                stt("dve", arv[:, c, q0:q0 + nqs, 0, :], K[:, c, :].rearrange("p (q t) -> p q t", t=C), -1.0,
                    tY[:, c, :].rearrange("p (q t) -> p q t", t=C), ALU.mult, ALU.mult, ["P4c", "tB"], ["ar" + SB])
                tt("pool", arv[:, c, q0:q0 + nqs, 1, :], xm[:, c, cols].rearrange("p (q t) -> p q t", t=C),
                   Gi[:, c, :].rearrange("p (q t) -> p q t", t=C), ALU.mult, xm_tok(c) + ["P4d"], ["ar" + SB])
            yield
            tt("pool", tY, K, Aa, ALU.mult, ["P4c", "P4b"], ["tB"])
            tt("dve", bt[:, :, cols], tY, tX, ALU.mult, ["tB", "tA"], ["bt" + SB])
            tt("pool", kt[:, :, cols], k4, tX, ALU.mult, ktok + ["tA"], ["kt" + SB])
            cp("act", vb[:, :, cols], v4_, vtok, ["vb" + SB])
            yield

        def post_gen(tc, sbi):
            cols = slice(sbi * SBK, (sbi + 1) * SBK)
            SB = "_%d" % sbi
            y = P3[:, :, cols]
            yt = "P3" + SB
            pa = v4(th[0])
            sqb = th[1][:, 256:512].bitcast(BF16).rearrange("p (c n) -> p c n", n=SBK)
            rs = v4(rstd)
            rtok = [t for c in range(4) for t in xm_tok(c)]
            ktok = [t for c in range(4) for t in xm_tok(4 + c)]
            vtok = [t for c in range(4) for t in xm_tok(8 + c)]
            b = nba()
            for c in range(4):
                mm(ps[:, b, c * 128:(c + 1) * 128], blk64m_f[:], P3[:, c, cols], True, True, [yt, "blk64m_f"], [PB(b)])
            tt("dve", y, y, ps[:, b, :].rearrange("p (c n) -> p c n", n=128), ALU.subtract, [yt, PB(b)], [yt])
            act(sqb, y, AF.Square, [yt, "th1"], ["th1d"])
            yield
            b2 = nba()
            for c in range(4):
                mm(ps[:, b2, c * 128:(c + 1) * 128], blk64m_bf[:], sqb[:, c, :], True, True, ["th1d", "blk64m_bf"], [PB(b2)])
            act(rstd[:, 0:512], ps[:, b2, :], AF.Sqrt, [PB(b2), "cst"], ["rstd"], bias=cst[:, 2:3])
            S.op("dve", lambda e: e.reciprocal(out=rstd[:, 0:512], in_=rstd[:, 0:512]), ["rstd"], ["rstd"])
            tt("pool", y, y, rs, ALU.mult, [yt, "rstd"], [yt])
            yield
            for c in range(4):
                act(P3[:, c, cols], P3[:, c, cols], AF.Identity, [yt, "vec"], [yt], bias=V("gn_b", c), scale=V("gn_w", c))
            tt("pool", pa, xm[:, 0:4, cols], bc4("r_k"), ALU.mult, rtok + ["vec"], ["th0"])
            tt("pool", pbf[:], pa, xm[:, 4:8, cols], ALU.mult, ["th0"] + ktok, ["pbf"])
            yield
            b3 = nba()
            for c in range(4):
                mm(ps[:, b3, c * 128:(c + 1) * 128], blk64o_bf[:], pbf[:, c, :], True, True, ["pbf", "blk64o_bf"], [PB(b3)])
            tt("dve", pa, ps[:, b3, :].rearrange("p (c n) -> p c n", n=128), xm[:, 8:12, cols], ALU.mult, [PB(b3)] + vtok, ["th0"])
            tt("pool", y, y, pa, ALU.add, [yt, "th0"], [yt])
            tt("pool", n_[:, 0:4, cols], y, G_[:, :, cols], ALU.mult, [yt, "G_" + SB], ["n0", "n1", "n2", "n3"])
            yield

        def scan_gen(tc, sbi):
            T, C = tc.T, tc.C
            nq = T // C
            y_sb = P3
            SB = "_%d" % sbi
            arv = ar[:, :, 0:2 * T].rearrange("p c (q x t) -> p c q x t", x=2, t=C)
            nv = 4 if C == 64 else 1

            def chunk_gen(q, hg, mi, pb=0):
                G = "_g%d_s%d" % (hg, pb // 32)
                PS = slice(pb, pb + C)
                FS = slice(pb, pb + C)
                Mp, Mb = M_pad[mi], Mb_pad[mi]
                Mt, Mbt = "M_pad%d_g%d" % (mi, hg), "Mb_pad%d_g%d" % (mi, hg)
                cs_ = slice(q * C, (q + 1) * C)
                c0 = 2 * hg
                h0 = 4 * hg
                cl = (c0, c0 + 1)
                b1 = nbs()
                pT = ps[PS, b1, 0:256].bitcast(BF16).rearrange("p (x c n) -> p x c n", x=2, n=128)
                pV = ps[PS, b1, 256:384].bitcast(BF16).rearrange("p (c n) -> p c n", n=128)
                for x, src, stok in ((0, bt, "bt" + SB), (1, kt, "kt" + SB)):
                    for ci, c in enumerate(cl):
                        tr(pT[:, x, ci, :], src[:, c, cs_], [stok], [PB(b1)])
                for ci, c in enumerate(cl):
                    tr(pV[:, ci, :], vb[:, c, cs_], ["vb" + SB], [PB(b1)])
                cp("act", bkT_sb[PS, :, c0:c0 + 2, :], pT, [PB(b1)], ["bkT_sb" + G])
                cp("act", vT_sb[PS, c0:c0 + 2, :], pV, [PB(b1)], ["vT_sb" + G])
                yield
                for hh in range(2):
                    hs = slice(hh * 64, (hh + 1) * 64)
                    b = nbs()
                    pv5 = ps[PS, b, 0:2 * 4 * C].rearrange("p (h x y t) -> p h x y t", x=2, y=2, t=C)
                    for ci, c in enumerate(cl):
                        rhs = arv[hs, c, q].rearrange("p x t -> p (x t)")
                        for x, src, stok in ((0, bt, "bt" + SB), (1, kt, "kt" + SB)):
                            mm(pv5[:, ci, x].rearrange("p y t -> p (y t)"), src[hs, c, cs_], rhs, True, True, [stok, "ar" + SB], [PB(b)])
                    hsel = slice(h0 + hh, h0 + hh + 3, 2)
                    if C == 64:
                        tt("dve", Pbk_sb[PS, hsel, :, :, 0:C], pv5,
                           msk_ar[PS, :, FS].unsqueeze(1).unsqueeze(1).broadcast_to([C, 2, 2, 2, C]), ALU.mult,
                           [PB(b), "msk_ar"], ["Pbk_sb" + G])
                    else:
                        for x in range(2):
                            tt("dve", Pbk_sb[PS, hsel, x, :, 0:C], pv5[:, :, x],
                               msk_ar[PS, :, FS].unsqueeze(1).broadcast_to([C, 2, 2, C]), ALU.mult,
                               [PB(b), "msk_ar"], ["Pbk_sb" + G])
                    yield
                for hh in range(2):
                    hs = slice(hh * 64, (hh + 1) * 64)
                    b = nbs()
                    pA = ps[PS, b, 0:2 * C].rearrange("p (h t) -> p h t", t=C)
                    for ci, c in enumerate(cl):
                        mm(pA[:, ci, :], arv[hs, c, q, 0, :], bt[hs, c, cs_], True, True, ["ar" + SB, "bt" + SB], [PB(b)])
                    tt("dve", A_all[PS, 0:nv, h0 + hh:h0 + hh + 3:2, 0:C], pA.unsqueeze(1).broadcast_to([C, nv, 2, C]),
                       msk4_A[PS, 0:nv, FS].unsqueeze(2).broadcast_to([C, nv, 2, C]), ALU.mult,
                       [PB(b), "msk4_A"], ["A_all" + G])
                    yield
                hr = slice(h0, h0 + 4)
                tt("pool", N_all[PS, 0:nv, hr, 0:C], Pbk_sb[PS, hr, 0, 0, 0:C].unsqueeze(1).broadcast_to([C, nv, 4, C]),
                   msk4_N[PS, 0:nv, FS].unsqueeze(2).broadcast_to([C, nv, 4, C]), ALU.mult,
                   ["Pbk_sb" + G, "msk4_N"], ["N_all" + G])
                idb = ident_bf[PS, FS].unsqueeze(1).broadcast_to([C, 4, C])
                tt("pool", Q_sb[0][PS, hr, 0:C], N_all[PS, 0, hr, 0:C], idb, ALU.add, ["N_all" + G, "ident_bf"], ["Q_sb0" + G])
                if nv > 1:
                    tt("pool", D_sb[0][PS, hr, 0:C], A_all[PS, 0, hr, 0:C], idb, ALU.add, ["A_all" + G, "ident_bf"], ["D_sb0" + G])
                yield

                def hmm(psv, lhs, ltok, rhs, rtok, bnk):
                    for hl in range(4):
                        mm(psv[:, hl, :], lhs[PS, h0 + hl, 0:C], rhs[PS, h0 + hl, 0:C], True, True, [ltok + G, rtok + G], [PB(bnk)])

                def pbank():
                    bnk = nbs()
                    return bnk, ps[PS, bnk, 0:4 * C].rearrange("p (h t) -> p h t", t=C)

                Acur, Atok = A_all[:, 0], "A_all"
                Ncur, Ntok = N_all[:, 0], "N_all"
                cur = 0
                for lv in range(2):
                    nxt = 1 - cur
                    an, nn = A_sb[lv % 2], N_sb[lv % 2]
                    ant, nnt = "A_sb%d" % (lv % 2), "N_sb%d" % (lv % 2)
                    bA, pA2 = pbank()
                    hmm(pA2, Ncur, Ntok, Acur, Atok, bA)
                    needN = (nv > 1) or lv == 0
                    if needN:
                        bN, pN2 = pbank()
                        hmm(pN2, Acur, Atok, Ncur, Ntok, bN)
                    cp("act", an[PS, hr, 0:C], pA2, [PB(bA)], [ant + G])
                    if needN:
                        cp("dve", nn[PS, hr, 0:C], pN2, [PB(bN)], [nnt + G])
                    yield
                    bQ, pQ = pbank()
                    hmm(pQ, an, ant, Q_sb[cur], "Q_sb%d" % cur, bQ)
                    if nv > 1:
                        bD, pD = pbank()
                        hmm(pD, nn, nnt, D_sb[cur], "D_sb%d" % cur, bD)
                    tt("dve", Q_sb[nxt][PS, hr, 0:C], pQ, Q_sb[cur][PS, hr, 0:C], ALU.add, [PB(bQ), "Q_sb%d" % cur + G], ["Q_sb%d" % nxt + G])
                    if nv > 1:
                        tt("dve", D_sb[nxt][PS, hr, 0:C], pD, D_sb[cur][PS, hr, 0:C], ALU.add, [PB(bD), "D_sb%d" % cur + G], ["D_sb%d" % nxt + G])
                    yield
                    Acur, Atok, Ncur, Ntok = an, ant, nn, nnt
                    cur = nxt
                for v in range(1, nv):
                    nxt = 1 - cur
                    lastm = (v == nv - 1)
                    Qc, Qct, Dc, Dct = Q_sb[cur], "Q_sb%d" % cur, D_sb[cur], "D_sb%d" % cur
                    b2_, pX2 = pbank()
                    hmm(pX2, A_all[:, v], "A_all", Qc, Qct, b2_)
                    if not lastm:
                        b1_, pX1 = pbank()
                        hmm(pX1, N_all[:, v], "N_all", Dc, Dct, b1_)
                    cp("act", N_sb[0][PS, hr, 0:C], pX2, [PB(b2_)], ["N_sb0" + G])
                    if not lastm:
                        cp("dve", A_sb[0][PS, hr, 0:C], pX1, [PB(b1_)], ["A_sb0" + G])
                    yield
                    b3_, pY2 = pbank()
                    hmm(pY2, Dc, Dct, N_sb[0], "N_sb0", b3_)
                    if not lastm:
                        b4_, pY1 = pbank()
                        hmm(pY1, Qc, Qct, A_sb[0], "A_sb0", b4_)
                    tt("dve", Q_sb[nxt][PS, hr, 0:C], pY2, Qc[PS, hr, 0:C], ALU.add, [PB(b3_), Qct + G], ["Q_sb%d" % nxt + G])
                    if not lastm:
                        tt("dve", D_sb[nxt][PS, hr, 0:C], pY1, Dc[PS, hr, 0:C], ALU.add, [PB(b4_), Dct + G], ["D_sb%d" % nxt + G])
                    yield
                    cur = nxt
                Qf, Qt = Q_sb[cur], "Q_sb%d" % cur + G
                bX = nbs()
                pX = ps[PS, bX, 0:256].rearrange("p (c n) -> p c n", n=128)
                for ci, c in enumerate(cl):
                    mm(pX[:, ci, :], arv[:, c, q, 0, :], Mb[:, c, :], True, False, ["ar" + SB, Mbt], [PB(bX)])
                    for hh in range(2):
                        hs = slice(hh * 64, (hh + 1) * 64)
                        mm(pX[:, ci, hs], Pbk_sb[PS, 2 * c + hh, 1, 0, 0:C], vT_sb[PS, c, hs], False, hh == 1,
                           ["Pbk_sb" + G, "vT_sb" + G], [PB(bX)])
                cp("act", Xs[PS, c0:c0 + 2, :], pX, [PB(bX)], ["Xs" + G])
                yield
                bU = nbs()
                pU = ps[PS, bU, 0:256].rearrange("p (c n) -> p c n", n=128)
                for ci, c in enumerate(cl):
                    for hh in range(2):
                        hs = slice(hh * 64, (hh + 1) * 64)
                        mm(pU[:, ci, hs], Qf[PS, 2 * c + hh, 0:C], Xs[PS, c, hs], True, True, [Qt, "Xs" + G], [PB(bU)])
                cp("dve", Us[PS, c0:c0 + 2, :], pU, [PB(bU)], ["Us" + G])
                yield
                bY = nbs()
                pY = ps[:, bY, 0:2 * C].rearrange("p (c t) -> p c t", t=C)
                for ci, c in enumerate(cl):
                    for hh in range(2):
                        h = 2 * c + hh
                        hs = slice(hh * 64, (hh + 1) * 64)
                        mm(pY[hs, ci, :], Mb[:, c, hs], arv[:, c, q, 1, :], True, False, [Mbt, "ar" + SB], [PB(bY)])
                        mm(pY[hs, ci, :], Us[PS, c, hs], Pbk_sb[PS, h, 0, 1, 0:C], False, False, ["Us" + G, "Pbk_sb" + G], [PB(bY)])
                        mm(pY[hs, ci, :], vT_sb[PS, c, hs], Pbk_sb[PS, h, 1, 1, 0:C], False, True, ["vT_sb" + G, "Pbk_sb" + G], [PB(bY)])
                bM = nbs()
                pM = ps[:, bM, 0:256].rearrange("p (c n) -> p c n", n=128)
                for ci, c in enumerate(cl):
                    mm(pM[:, ci, :], bkT_sb[PS, 0, c, :], Us[PS, c, :], True, False, ["bkT_sb" + G, "Us" + G], [PB(bM)])
                    mm(pM[:, ci, :], bkT_sb[PS, 1, c, :], vT_sb[PS, c, :], False, True, ["bkT_sb" + G, "vT_sb" + G], [PB(bM)])
                cp("act", y_sb[:, c0:c0 + 2, cs_], pY, [PB(bY)], ["P3" + SB])
                tt("dve", Mp[:, c0:c0 + 2, :], Mp[:, c0:c0 + 2, :], pM, ALU.add, [Mt, PB(bM)], [Mt])
                gC = gCs[:, c0:c0 + 2, q:q + 1].broadcast_to([128, 2, 128])
                tt("dve", Mp[:, c0:c0 + 2, :], Mp[:, c0:c0 + 2, :], gC, ALU.mult, [Mt, "gCs" + SB], [Mt])
                last = tc.sample or (tc.last_prompt and q == nq - 1)
                if not last:
                    for hh in range(2):
                        hs = slice(hh * 64, (hh + 1) * 64)
                        cp("act", Mb[hs, c0:c0 + 2, hs], Mp[hs, c0:c0 + 2, hs], [Mt], [Mbt])
                else:
                    seq = (1 + q) if tc.sample else 0
                    for hh in range(2):
                        hs = slice(hh * 64, (hh + 1) * 64)
                        S.dma("sp", wkv_o[hs, seq, c0:c0 + 2], Mp[hs, c0:c0 + 2, hs], reads=[Mt], key="wk%d_%d" % (mi, hg))
                yield

            nqs = SBK // C
            step = 2 if tc.sample else 1
            for q0_ in range(sbi * nqs, (sbi + 1) * nqs, step):
                gens = []
                for q in range(q0_, q0_ + step):
                    mi = (q % NMB) if tc.sample else 0
                    pb = 32 * (q - q0_)
                    if tc.sample:
                        Mp, Mb = M_pad[mi], Mb_pad[mi]
                        mts = ["M_pad%d_g0" % mi, "M_pad%d_g1" % mi]
                        mbts = ["Mb_pad%d_g0" % mi, "Mb_pad%d_g1" % mi]
                        for hh in range(2):
                            S.dma("sp", Mp[hh * 64:(hh + 1) * 64, :, hh * 64:(hh + 1) * 64], wkv_d[hh * 64:(hh + 1) * 64, q],
                                  writes=mts, key="sl%d" % mi)
                        for hh in range(2):
                            cp("act", Mb[hh * 64:(hh + 1) * 64, :, hh * 64:(hh + 1) * 64],
                               Mp[hh * 64:(hh + 1) * 64, :, hh * 64:(hh + 1) * 64], mts, mbts)
                    gens += [chunk_gen(q, 0, mi, pb), chunk_gen(q, 1, mi, pb)]
                while gens:
                    for g in list(gens):
                        try:
                            next(g)
                        except StopIteration:
                            gens.remove(g)
                    yield

        def mixer_phase(tc):
            nsb = tc.T // SBK
            if (not tc.sample) and tc.idx == 0:
                memset("pool", M_pad[0][:], 0.0, ["M_pad0_g0", "M_pad0_g1"])
                memset("pool", Mb_pad[0][:], 0.0, ["Mb_pad0_g0", "Mb_pad0_g1"])

            def run(main, aux, ratio=3):
                aux = list(aux)
                alive = True
                while alive or aux:
                    if main is not None and alive:
                        for _ in range(ratio):
                            try:
                                next(main)
                            except StopIteration:
                                alive = False
                                break
                    else:
                        alive = False
                    for g in list(aux):
                        try:
                            next(g)
                        except StopIteration:
                            aux.remove(g)

            for k in range(nsb):
                aux = []
                if k + 1 < nsb:
                    aux.append(prep_gen(tc, k + 1))
                aux.append(gmlp_gen(tc, k))
                if k >= 1:
                    aux.append(post_gen(tc, k - 1))
                run(scan_gen(tc, k), aux)
            run(None, [post_gen(tc, nsb - 1)])

        def w_out_phase(tc):
            T = tc.T
            wv = []
            ld_state["hold"] = ld_state["next"]
            for l in range(3):
                sl, stok = next_slot(("wout", l))
                lo, hi = 3 * l, min(3 * l + 3, 8)
                w = sl[:, 0:(hi - lo) * D].rearrange("p (j n) -> p j n", n=D)
                for jj in range(hi - lo):
                    wv.append((w[:, jj, :], stok))
            for dc in range(8):
                b = nb()
                for kc in range(8):
                    mm(ps[:, b, 0:T], wv[kc][0][:, dc * 128:(dc + 1) * 128], n_[:, kc, 0:T], kc == 0, kc == 7,
                       [wv[kc][1], "n%d" % kc], [PB(b)])
                gate_add(tc, dc, ps[:, b, 0:T], 5, [PB(b)])
            ld_state["hold"] = None

        def final_norm(tc):
            T = tc.T
            rms_stats(tc, lambda c: xt[:, c, 0:T], 8, onesD_bf[:], lambda c: ["xt%d" % c], 0)
            yv = yT_o.rearrange("(c p) t -> p c t", p=128)
            for c in range(8):
                k = c % 2
                stt("dve", th[k][:, 0:T], xt[:, c, 0:T], V("final_g", c), rstd[:, 0:T], ALU.mult, ALU.mult,
                    ["xt%d" % c, "rstd", "vec"], ["th%d" % k])
                S.dma("sp", yv[:, c, tc.col0:tc.col0 + T], th[k][:, 0:T], reads=["th%d" % k], key="yo%d" % k)

        ones512_bf = sb("ones512_bf", [128, 128], BF16)
        memset("pool", ones512_bf[:], 1.0 / 512, ["ones512_bf"])

        xv = xT_d.rearrange("(c p) t -> p c t", p=128)
        phases = [lambda tc: norm_mod(tc, 0, 1), lambda tc: ffn(tc, 0, 2), lambda tc: norm_mod(tc, 3, 4), w_in_phase,
                  mixer_phase, w_out_phase, lambda tc: norm_mod(tc, 6, 7),
                  lambda tc: ffn(tc, 1, 8), final_norm]
        pcount = 0
        for tc in (tiles if stop is None or stop >= 0 else []):
            T = tc.T
            S.dma("sp", xt[:, :, 0:T], xv[:, :, tc.col0:tc.col0 + T], writes=["xt%d" % c for c in range(8)], key="xin")
            for ph in phases:
                if stop is not None and pcount >= stop:
                    break
                ph(tc)
                pcount += 1
            if stop is not None and pcount >= stop:
                break
        S.dma("sp", sh_o, shout[:], reads=["shout"], key="sho")
        fk = ["yo0", "yo1", "sho", "cvo", "dbg"] + ["wk%d_%d" % (i, g) for i in range(NMB) for g in range(2)]
        S.emit(final_keys=fk)
    return nc


def _chunkvec(v):
    v = np.asarray(v, np.float32).reshape(-1)
    return np.ascontiguousarray(v.reshape(-1, 128).T)


def kernel(x_prompt, x_sample, state_shift, state_wkv, c_prompt, c_sample, w_ada, b_ada, ffn1_gu,
           ffn1_dn, w_in, mu_shift, w0, w_lora_up, a0, a_lora_up, g_lora_up, k_k, k_a, r_k, gn_w,
           gn_b, ln_v_g, ln_v_b, w_s, b_s, w_out, ffn2_gu, ffn2_dn, final_g):
    f = lambda a: np.asarray(a, np.float32)
    x_prompt, x_sample, state_shift, state_wkv = f(x_prompt), f(x_sample), f(state_shift), f(state_wkv)
    c_prompt, c_sample = f(c_prompt), f(c_sample)
    vecs = np.concatenate([_chunkvec(v) for v in (b_ada[0], mu_shift[0], w0[0], a0[0], k_k[0], k_a[0], f(r_k[0]).reshape(-1),
                                                   gn_w[0], gn_b[0], ln_v_g[0], ln_v_b[0], final_g)], axis=1)
    assert vecs.shape == (128, NV)

    def gu_layout(w):
        w = f(w).reshape(8, 128, 2, FC, 128)
        return np.ascontiguousarray(w.transpose(3, 1, 2, 0, 4)).reshape(FC, 128, 2048)

    def dn_layout(w):
        w = f(w).reshape(FC, 128, 8, 128)
        return np.ascontiguousarray(w.transpose(2, 1, 0, 3)).reshape(8, 128, DFF)

    gu_h = np.stack([gu_layout(ffn1_gu[0]), gu_layout(ffn2_gu[0])])
    dn_h = np.stack([dn_layout(ffn1_dn[0]), dn_layout(ffn2_dn[0])])
    win_h = np.ascontiguousarray(f(w_in[0]).reshape(8, 128, FC, 128).transpose(2, 1, 0, 3)).reshape(FC, 128, D)
    wout_h = np.ascontiguousarray(f(w_out[0]).reshape(8, 128, D))
    lora_h = np.zeros((128, 1536), np.float32)
    lora_h[0:64, 0:512] = f(w_lora_up[0])
    lora_h[64:128, 512:1024] = f(a_lora_up[0])
    lora_h[:, 1024:1536] = f(g_lora_up[0])
    ws = f(w_s[0])
    wsT_h = np.ascontiguousarray(ws.transpose(2, 0, 1))
    rep_h = np.ascontiguousarray(np.tile(ws[:, 0:8, 0:8].transpose(2, 0, 1), (16, 1, 1)))
    bs = f(b_s[0])
    bsb_h = np.ascontiguousarray(np.repeat(bs.reshape(4, 2, 1, 128), 64, axis=2).transpose(1, 2, 0, 3)).reshape(128, 4, 128)
    wada = np.ascontiguousarray(f(w_ada[0]))

    in_maps = []
    for i in range(NCORES):
        ss = slice(NSEQ_S * i, NSEQ_S * (i + 1))
        xT = np.ascontiguousarray(np.concatenate([x_prompt[i].T, x_sample[ss].reshape(TS, D).T], axis=1))
        call = np.concatenate([c_prompt[i:i + 1], c_sample[ss]], axis=0)
        cT_h = np.ascontiguousarray(call.reshape(17, 8, 128).transpose(2, 1, 0))
        sshT_h = np.ascontiguousarray(state_shift[0, ss].reshape(NSEQ_S, 14, 128).transpose(2, 1, 0))
        wk = state_wkv[0, ss].reshape(NSEQ_S, 4, 2, 64, 64)
        wkv_h = np.ascontiguousarray(wk.transpose(2, 4, 0, 1, 3)).reshape(128, NSEQ_S, 4, 64)
        in_maps.append({"xT": xT, "cT": cT_h, "vecs": vecs, "w_ada": wada, "gu_h": gu_h, "dn_h": dn_h,
                        "win_h": win_h, "wout_h": wout_h, "lora_h": lora_h, "wsT_h": wsT_h, "rep_h": rep_h,
                        "bsb_h": bsb_h, "sshT_h": sshT_h, "wkv_h": wkv_h})
    nc = build_program()
    res = run_bass_kernel_spmd(nc, in_maps, core_ids=list(range(NCORES)))
    R = res.results
    y_prompt = np.zeros((8, SEQ, D), np.float32)
    y_sample = np.zeros((128, TL_S, D), np.float32)
    nsp = np.zeros((1, 8, R_COLS), np.float32)
    nwp = np.zeros((1, 8, 8, 64, 64), np.float32)
    nss = np.zeros((1, 128, R_COLS), np.float32)
    nws = np.zeros((1, 128, 8, 64, 64), np.float32)
    ncv = np.zeros((1, 128, TL_S, 512), np.float32)
    for i in range(NCORES):
        r = R[i]
        ss = slice(NSEQ_S * i, NSEQ_S * (i + 1))
        yT = np.asarray(r["yT"], np.float32)
        y_prompt[i] = yT[:, :SEQ].T
        y_sample[ss] = yT[:, SEQ:].T.reshape(NSEQ_S, TL_S, D)
        sh = np.asarray(r["shT_o"], np.float32)
        shf = sh.transpose(2, 1, 0).reshape(17, R_COLS)
        nsp[0, i] = shf[0]
        nss[0, ss] = shf[1:]
        wk = np.asarray(r["wkv_o"], np.float32).reshape(2, 64, 17, 4, 64)
        wk = wk.transpose(2, 3, 0, 4, 1).reshape(17, 8, 64, 64)
        nwp[0, i] = wk[0]
        nws[0, ss] = wk[1:]
        cv = np.asarray(r["cvT_o"], np.float32)
        ncv[0, ss] = cv.transpose(2, 1, 0).reshape(NSEQ_S, TL_S, 512)
    return (y_prompt, y_sample, nsp, nwp, nss, nws, ncv)
```

```python
import contextlib
import numpy as np
import concourse.bass as bass
import concourse.mybir as mybir
from concourse.bass_utils import run_bass_kernel_spmd

F32 = mybir.dt.float32
BF16 = mybir.dt.bfloat16
AF = mybir.ActivationFunctionType
ALU = mybir.AluOpType

NCORES = 8
D = 1024
DFF = 2816
FC = 22
SEQ = 2048
NSEQ_S = 16
TL_S = 8
TS = NSEQ_S * TL_S
NTOK = SEQ + TS
TP = 512
R_COLS = 1792
N_MOD = 9
SCAN_SUB = [None]
DBGF = set()
W_C = float(np.exp(-0.5))

ENGS = ("pe", "act", "dve", "pool", "sp")
SAME_ENGINE_SYNC = {"pe": False, "act": True, "dve": True, "pool": True, "sp": False}


class Op:
    __slots__ = ("eng", "fn", "deps", "dma_key", "dma_val", "needs_inc", "seq", "idx")

    def __init__(self, eng, fn, dma_key=None):
        self.eng = eng
        self.fn = fn
        self.deps = []
        self.dma_key = dma_key
        self.dma_val = 0
        self.needs_inc = False
        self.seq = 0


class Sched:
    def __init__(self, nc):
        self.nc = nc
        self.ops = {e: [] for e in ENGS}
        self.last_w = {}
        self.readers = {}
        self.dma_cnt = {}
        self.n = 0

    def op(self, eng, fn, reads=(), writes=(), dma_key=None):
        o = Op(eng, fn, dma_key)
        o.idx = self.n
        self.n += 1
        deps = {}
        for r in reads:
            w = self.last_w.get(r)
            if w is not None:
                deps[id(w)] = w
        for r in writes:
            w = self.last_w.get(r)
            if w is not None:
                deps[id(w)] = w
            for rd in self.readers.get(r, ()):
                deps[id(rd)] = rd
        o.deps = list(deps.values())
        for r in reads:
            lst = self.readers.setdefault(r, [])
            if dma_key is None:
                for k_ in range(len(lst)):
                    if lst[k_].dma_key is None and lst[k_].eng == eng:
                        lst[k_] = o
                        break
                else:
                    lst.append(o)
            else:
                lst.append(o)
        for r in writes:
            self.last_w[r] = o
            self.readers[r] = []
        if dma_key is not None:
            self.dma_cnt[dma_key] = self.dma_cnt.get(dma_key, 0) + 16
            o.dma_val = self.dma_cnt[dma_key]
        self.ops[eng].append(o)
        return o

    def dma(self, eng, out, in_, reads=(), writes=(), key=None):
        assert key is not None
        return self.op(eng, lambda e: e.dma_start(out=out, in_=in_), reads, writes, dma_key=key)

    def emit(self, final_keys=()):
        nc = self.nc
        for e in ENGS:
            for o in self.ops[e]:
                for d in o.deps:
                    if d.dma_key is not None:
                        continue
                    if d.eng == o.eng and not SAME_ENGINE_SYNC[o.eng]:
                        continue
                    d.needs_inc = True
        for e in ENGS:
            c = 0
            for o in self.ops[e]:
                if o.dma_key is None and o.needs_inc:
                    c += 1
                    o.seq = c
        with contextlib.ExitStack() as st:
            esem = {e: st.enter_context(nc.semaphore("s_" + e)) for e in ENGS}
            dsem = {k: st.enter_context(nc.semaphore("d_" + str(k))) for k in self.dma_cnt}
            block = st.enter_context(nc.Block())
            engobj = {"pe": "tensor", "act": "scalar", "dve": "vector", "pool": "gpsimd", "sp": "sync"}

            def make(e):
                def body(eng):
                    waited = {}
                    for o in self.ops[e]:
                        need = {}
                        for d in o.deps:
                            if d.dma_key is not None:
                                key, val, sem = ("d", d.dma_key), d.dma_val, dsem[d.dma_key]
                            else:
                                if d.eng == e and not SAME_ENGINE_SYNC[e]:
                                    continue
                                key, val, sem = ("e", d.eng), d.seq, esem[d.eng]
                            if key not in need or need[key][0] < val:
                                need[key] = (val, sem)
                        for key, (val, sem) in need.items():
                            if waited.get(key, 0) >= val:
                                continue
                            waited[key] = val
                            eng.wait_ge(sem, val)
                        ins = o.fn(eng)
                        if o.dma_key is not None:
                            ins.then_inc(dsem[o.dma_key], 16)
                        elif o.needs_inc:
                            ins.then_inc(esem[e], 1)
                    if e == "sp":
                        for k in final_keys:
                            if k in self.dma_cnt:
                                eng.wait_ge(dsem[k], self.dma_cnt[k])
                return body

            for e in ENGS:
                getattr(block, engobj[e])(make(e))


VEC_SPECS = [("b_ada", 72), ("mu", 14), ("w0", 4), ("a0", 4), ("k_k", 4), ("k_a", 4), ("r_k", 4),
             ("gn_w", 4), ("gn_b", 4), ("ln_g", 4), ("ln_b", 4), ("final_g", 8)]
VOFF = {}
_o = 0
for _n, _k in VEC_SPECS:
    VOFF[_n] = _o
    _o += _k
NV = _o
VD = {"onem_mu": NV, "half_w0": NV + 14, "half_a0": NV + 18, "onem_ka": NV + 22}
NVT = NV + 26


class TileCtx:
    def __init__(self, idx, col0, nseq, tl, C, sample, last_prompt):
        self.idx = idx
        self.col0 = col0
        self.nseq = nseq
        self.tl = tl
        self.T = nseq * tl
        self.C = C
        self.sample = sample
        self.last_prompt = last_prompt


def build_program(stop=None, dbg=False):
    nc = bass.Bass("TRN2", target_bir_lowering=False)

    def din(name, shape, dt=F32):
        return nc.dram_tensor(name, list(shape), dt, kind="ExternalInput").ap()

    def dout(name, shape, dt=F32):
        return nc.dram_tensor(name, list(shape), dt, kind="ExternalOutput").ap()

    def dscr(name, shape, dt=BF16):
        return nc.dram_tensor(name, list(shape), dt, kind="Internal").ap()

    xT_d = din("xT", [D, NTOK])
    cT_d = din("cT", [128, 8, 17])
    vec_d = din("vecs", [128, NV])
    wada_d = din("w_ada", [D, N_MOD * D])
    gu_d = din("gu_h", [2, FC, 128, 2048])
    dn_d = din("dn_h", [2, 8, 128, DFF])
    win_d = din("win_h", [FC, 128, D])
    wout_d = din("wout_h", [8, 128, D])
    lora_d = din("lora_h", [128, 1536])
    wsT_d = din("wsT_h", [128, 8, 128])
    rep_d = din("rep_h", [128, 8, 8])
    bsb_d = din("bsb_h", [128, 4, 128])
    ssh_d = din("sshT_h", [128, 14, NSEQ_S])
    wkv_d = din("wkv_h", [128, NSEQ_S, 4, 64])

    yT_o = dout("yT", [D, NTOK])
    sh_o = dout("shT_o", [128, 14, 17])
    wkv_o = dout("wkv_o", [128, 17, 4, 64])
    cv_o = dout("cvT_o", [128, 4, TS])

    if dbg:
        dbg_y = dout("dbg_y", [128, 4, NTOK])
        dbg_xm = dout("dbg_xm", [128, 14, NTOK])
        dbg_lw = dout("dbg_lw", [128, 4, NTOK])
        dbg_a = dout("dbg_a", [128, 4, NTOK])
        dbg_kkn = dout("dbg_kkn", [128, 4, NTOK])
    gu_s = dscr("gu_s", [2, FC, 128, 2048])
    dn_s = dscr("dn_s", [2, 8, 128, DFF])
    win_s = dscr("win_s", [FC, 128, D])
    wout_s = dscr("wout_s", [8, 128, D])

    st = contextlib.ExitStack()
    with st:
        def sb(name, shape, dt):
            return st.enter_context(nc.sbuf_tensor(name, list(shape), dt))

        xt = sb("xt", [128, 8, TP], F32)
        n_ = sb("n_", [128, 8, TP], BF16)
        big = sb("big", [128, 14 * TP], F32)
        slots = [sb("slot%d" % i, [128, 3072], BF16) for i in range(3)]
        P1 = sb("P1", [128, 4, TP], F32)
        P2 = sb("P2", [128, 4, TP], F32)
        P3 = sb("P3", [128, 4, TP], F32)
        P4 = sb("P4", [128, 4, TP], F32)
        G_ = sb("G_", [128, 4, TP], F32)
        ar = sb("ar", [128, 4, 2 * TP], BF16)
        bt = sb("bt", [128, 4, TP], BF16)
        kt = sb("kt", [128, 4, TP], BF16)
        vb = sb("vb", [128, 4, TP], BF16)
        bkT_sb = sb("bkT_sb", [64, 2, 4, 128], BF16)
        vT_sb = sb("vT_sb", [64, 4, 128], BF16)
        vnT_sb = sb("vnT_sb", [128, 4, 128], BF16)
        Pbk_sb = sb("Pbk_sb", [64, 8, 2, 2, 64], BF16)
        A_sb = [sb("A_sb%d" % i, [64, 8, 64], BF16) for i in range(2)]
        N_sb = [sb("N_sb%d" % i, [64, 8, 64], BF16) for i in range(2)]
        Q_sb = [sb("Q_sb%d" % i, [64, 8, 64], BF16) for i in range(2)]
        A_all = sb("A_all", [64, 4, 8, 64], BF16)
        N_all = sb("N_all", [64, 4, 8, 64], BF16)
        D_sb = [sb("D_sb%d" % i, [64, 8, 64], BF16) for i in range(2)]
        msk4_A = sb("msk4_A", [64, 4, 64], F32)
        msk4_N = sb("msk4_N", [64, 4, 64], BF16)
        msk4_Nf = sb("msk4_Nf", [64, 4, 64], F32)
        Xs = sb("Xs", [64, 4, 128], BF16)
        Us = sb("Us", [64, 4, 128], BF16)
        NMB = 2
        M_pad = [sb("M_pad%d" % i, [128, 4, 128], F32) for i in range(NMB)]
        Mb_pad = [sb("Mb_pad%d" % i, [128, 4, 128], BF16) for i in range(NMB)]
        onesf = sb("onesf", [128, 128], F32)
        ident_f = sb("ident_f", [128, 128], F32)
        ident_bf = sb("ident_bf", [128, 128], BF16)
        onesD_bf = sb("onesD_bf", [128, 128], BF16)
        ones512_f = sb("ones512_f", [128, 128], F32)
        blk64m_f = sb("blk64m_f", [128, 128], F32)
        blk64m_bf = sb("blk64m_bf", [128, 128], BF16)
        blk64o_bf = sb("blk64o_bf", [128, 128], BF16)
        msk_ar = sb("msk_ar", [64, 2, 64], F32)
        m_A = sb("m_A", [64, 64], F32)
        rm64 = sb("rm64", [128, TP], F32)
        rm8 = sb("rm8", [128, TS], F32)
        cmk = sb("cmk", [128, 16, 8], F32)
        cst = sb("cst", [128, 4], F32)
        WsT = sb("WsT", [128, 8, 128], BF16)
        BsT = sb("BsT", [128, 8, 128], BF16)
        rep_f = sb("rep_f", [128, 8, 8], F32)
        bsb = sb("bsb", [128, 4, 128], F32)
        lora_bf = sb("lora_bf", [128, 1536], BF16)
        vec = sb("vec_sb", [128, NVT], F32)
        mod = sb("mod", [128, 72, 17], F32)
        cT = sb("cT_sb", [128, 8, 17], F32)
        cs_bf = sb("cs_bf", [128, 8, 17], BF16)
        carry = sb("carry", [128, 14, 1], F32)
        sshT = sb("sshT", [128, 14, NSEQ_S], F32)
        shout = sb("shout", [128, 14, 17], F32)
        th = [sb("th%d" % i, [128, TP], F32) for i in range(2)]
        sq = [sb("sq%d" % i, [128, TP], BF16) for i in range(2)]
        lin_bf, sg_bf = sq[0], sq[1]
        rstd = sb("rstd", [128, TP], F32)
        t1 = [sb("t1_%d" % i, [128, TP], F32) for i in range(2)]
        tA = sb("tA", [128, TP], F32)
        tB = sb("tB", [128, TP], F32)
        ps = st.enter_context(nc.psum_tensor("ps", [128, 8, 512], F32))

        S = Sched(nc)
        bank_ctr = [0]

        def nb():
            b = bank_ctr[0] % 8
            bank_ctr[0] += 1
            return b

        def PB(b):
            return "ps%d" % b

        def mm(out, lhsT, rhs, start, stop, r, w):
            S.op("pe", lambda e: e.matmul(out, lhsT=lhsT, rhs=rhs, start=start, stop=stop), r, w)

        def tr(out, in_, r, w):
            S.op("pe", lambda e: e.transpose(out=out, in_=in_, identity=ident_bf[:]), list(r) + ["ident_bf"], w)

        def act(out, in_, func, r, w, bias=None, scale=1.0):
            if bias is None:
                S.op("act", lambda e: e.activation(out=out, in_=in_, func=func, scale=scale), r, w)
            else:
                S.op("act", lambda e: e.activation(out=out, in_=in_, func=func, bias=bias, scale=scale), r, w)

        def cp(eng, out, in_, r, w):
            if eng == "act":
                S.op("act", lambda e: e.copy(out=out, in_=in_), r, w)
            else:
                S.op(eng, lambda e: e.tensor_copy(out=out, in_=in_), r, w)

        def tt(eng, out, in0, in1, op, r, w):
            S.op(eng, lambda e: e.tensor_tensor(out=out, in0=in0, in1=in1, op=op), r, w)

        def ts(eng, out, in0, s1, s2, op0, op1, r, w):
            if s2 is None:
                S.op(eng, lambda e: e.tensor_scalar(out=out, in0=in0, scalar1=s1, scalar2=None, op0=op0), r, w)
            else:
                S.op(eng, lambda e: e.tensor_scalar(out=out, in0=in0, scalar1=s1, scalar2=s2, op0=op0, op1=op1), r, w)

        def stt(eng, out, in0, scalar, in1, op0, op1, r, w):
            S.op(eng, lambda e: e.scalar_tensor_tensor(out=out, in0=in0, scalar=scalar, in1=in1, op0=op0, op1=op1), r, w)

        def memset(eng, ap, val, w):
            S.op(eng, lambda e: e.memset(ap, val), (), w)

        def asel(out, in_, pattern, op, base, cm, r, w):
            S.op("pool", lambda e: e.affine_select(out=out, in_=in_, pattern=pattern, compare_op=op, fill=0.0,
                                                   base=base, channel_multiplier=cm), r, w)

        def V(name, c=0, n=1):
            o = VOFF[name] if name in VOFF else VD[name]
            return vec[:, o + c:o + c + n]

        S.dma("sp", vec[:, 0:NV], vec_d, writes=["vec"], key="k_vec")
        S.dma("sp", cT[:], cT_d, writes=["cT"], key="k_cT")
        memset("pool", onesf[:], 1.0, ["onesf"])
        asel(ident_f[:], onesf[:], [[1, 128]], ALU.is_equal, 0, -1, ["onesf"], ["ident_f"])
        cp("pool", ident_bf[:], ident_f[:], ["ident_f"], ["ident_bf"])
        memset("pool", onesD_bf[:], 1.0 / D, ["onesD_bf"])
        memset("pool", ones512_f[:], 1.0 / 512, ["ones512_f"])
        for (t_, v_) in ((blk64m_f, 1.0 / 64), (blk64m_bf, 1.0 / 64), (blk64o_bf, 1.0)):
            memset("pool", t_[:], 0.0, [t_.name])
            memset("pool", t_[0:64, 0:64], v_, [t_.name])
            memset("pool", t_[64:128, 64:128], v_, [t_.name])
        asel(msk_ar[:, 0, :], onesf[0:64, 0:64], [[1, 64]], ALU.is_gt, 0, -1, ["onesf"], ["msk_ar"])
        asel(msk_ar[:, 1, :], onesf[0:64, 0:64], [[1, 64]], ALU.is_ge, 0, -1, ["onesf"], ["msk_ar"])
        asel(m_A[:], onesf[0:64, 0:64], [[-1, 64]], ALU.is_gt, 0, 1, ["onesf"], ["m_A"])
        o64 = onesf[0:64, 0:64]
        vA = msk4_A[:, 0, :].rearrange("p (a b) -> p a b", b=8)
        asel(msk4_A[:, 0, :], o64, [[-1, 64]], ALU.is_gt, 0, 1, ["onesf"], ["msk4_A"])
        asel(vA, vA, [[-8, 8], [0, 8]], ALU.is_ge, 0, 1, ["msk4_A"], ["msk4_A"])
        asel(vA, vA, [[8, 8], [0, 8]], ALU.is_ge, 7, -1, ["msk4_A"], ["msk4_A"])
        vN = msk4_Nf[:, 0, :].rearrange("p (a b) -> p a b", b=8)
        asel(msk4_Nf[:, 0, :], o64, [[1, 64]], ALU.is_gt, 0, -1, ["onesf"], ["msk4_Nf"])
        asel(vN, vN, [[-8, 8], [0, 8]], ALU.is_ge, 0, 1, ["msk4_Nf"], ["msk4_Nf"])
        asel(vN, vN, [[8, 8], [0, 8]], ALU.is_ge, 7, -1, ["msk4_Nf"], ["msk4_Nf"])
        for v_, m_ in ((1, 8), (2, 16), (3, 32)):
            nb_ = 64 // (2 * m_)
            wA = msk4_A[:, v_, :].rearrange("p (a b) -> p a b", b=2 * m_)
            memset("pool", msk4_A[:, v_, :], 1.0, ["msk4_A"])
            asel(wA, wA, [[-2 * m_, nb_], [0, 2 * m_]], ALU.is_ge, -m_, 1, ["msk4_A"], ["msk4_A"])
            asel(wA, wA, [[2 * m_, nb_], [0, 2 * m_]], ALU.is_ge, 2 * m_ - 1, -1, ["msk4_A"], ["msk4_A"])
            asel(wA, wA, [[0, nb_], [-1, 2 * m_]], ALU.is_ge, m_ - 1, 0, ["msk4_A"], ["msk4_A"])
            wN = msk4_Nf[:, v_, :].rearrange("p (a b) -> p a b", b=2 * m_)
            memset("pool", msk4_Nf[:, v_, :], 1.0, ["msk4_Nf"])
            asel(wN, wN, [[-2 * m_, nb_], [0, 2 * m_]], ALU.is_ge, 0, 1, ["msk4_Nf"], ["msk4_Nf"])
            asel(wN, wN, [[2 * m_, nb_], [0, 2 * m_]], ALU.is_ge, m_ - 1, -1, ["msk4_Nf"], ["msk4_Nf"])
            asel(wN, wN, [[0, nb_], [1, 2 * m_]], ALU.is_ge, -m_, 0, ["msk4_Nf"], ["msk4_Nf"])
        cp("pool", msk4_N[:], msk4_Nf[:], ["msk4_Nf"], ["msk4_N"])
        memset("pool", rm64[:], 1.0, ["rm64"])
        memset("pool", rm64[:].rearrange("p (a b) -> p a b", b=64)[:, :, 0:1], 0.0, ["rm64"])
        memset("pool", rm8[:], 1.0, ["rm8"])
        memset("pool", rm8[:].rearrange("p (a b) -> p a b", b=8)[:, :, 0:1], 0.0, ["rm8"])
        memset("pool", cmk[:], 1.0, ["cmk"])
        asel(cmk[:], cmk[:], [[8, 16], [1, 8]], ALU.is_ge, 0, -1, ["cmk"], ["cmk"])
        asel(cmk[:], cmk[:], [[-8, 16], [0, 8]], ALU.is_ge, 0, 1, ["cmk"], ["cmk"])
        memset("pool", cst[:, 0:1], 1e-6, ["cst"])
        memset("pool", cst[:, 1:2], 1e-5, ["cst"])
        memset("pool", cst[:, 2:3], 64e-5, ["cst"])
        memset("pool", cst[:, 3:4], 1e-24, ["cst"])
        for i in range(NMB):
            memset("pool", M_pad[i][:], 0.0, ["M_pad%d_g0" % i, "M_pad%d_g1" % i])
            memset("pool", Mb_pad[i][:], 0.0, ["Mb_pad%d_g0" % i, "Mb_pad%d_g1" % i])
        memset("pool", carry[:], 0.0, ["carry"])
        ts("dve", V("onem_mu", 0, 14), V("mu", 0, 14), -1.0, 1.0, ALU.mult, ALU.add, ["vec"], ["vec"])
        ts("dve", V("half_w0", 0, 4), V("w0", 0, 4), 0.5, None, ALU.mult, None, ["vec"], ["vec"])
        ts("dve", V("half_a0", 0, 4), V("a0", 0, 4), 0.5, None, ALU.mult, None, ["vec"], ["vec"])
        ts("dve", V("onem_ka", 0, 4), V("k_a", 0, 4), -1.0, 1.0, ALU.mult, ALU.add, ["vec"], ["vec"])
        wsT_f = P3[:, 0:2, :].rearrange("p a (b n) -> p (a b) n", n=128)
        S.dma("sp", wsT_f, wsT_d, writes=["P3"], key="k_ws")
        S.dma("sp", rep_f[:], rep_d, writes=["rep_f"], key="k_rep")
        S.dma("sp", bsb[:], bsb_d, writes=["bsb"], key="k_bsb")
        S.dma("sp", sshT[:], ssh_d, writes=["sshT"], key="k_ssh")
        S.dma("pool", lora_bf[:], lora_d, writes=["lora_bf"], key="k_lora")
        asel(wsT_f, wsT_f, [[0, 8], [1, 128]], ALU.is_ge, 0, -1, ["P3"], ["P3"])
        cp("pool", WsT[:], wsT_f, ["P3"], ["WsT"])
        tt("pool", BsT[:].rearrange("p h (s i) -> p h s i", i=8),
           rep_f[:].unsqueeze(2).broadcast_to([128, 8, 16, 8]),
           cmk[:].unsqueeze(1).broadcast_to([128, 8, 16, 8]), ALU.mult, ["rep_f", "cmk"], ["BsT"])

        act(tA[:, 0:136], cT[:].rearrange("p a b -> p (a b)"), AF.Tanh, ["cT"], ["tA"], scale=0.5)
        stt("dve", tA[:, 0:136], tA[:, 0:136], 1.0, cT[:].rearrange("p a b -> p (a b)"), ALU.add, ALU.mult, ["tA", "cT"], ["tA"])
        ts("dve", cs_bf[:].rearrange("p a b -> p (a b)"), tA[:, 0:136], 0.5, None, ALU.mult, None, ["tA"], ["cs_bf"])
        ada_buf = [P1, P2]
        wada_v = wada_d.rearrange("(c p) n -> p c n", p=128)
        for pc in range(18):
            buf = ada_buf[pc % 2]
            bv = buf[:].rearrange("p a b -> p (a b)").bitcast(BF16).rearrange("p (c n) -> p c n", n=512)
            S.dma("pool", bv, wada_v[:, :, pc * 512:(pc + 1) * 512], writes=[buf.name], key="ad%d" % (pc % 2))
            b = nb()
            for jc in range(4):
                for c in range(8):
                    mm(ps[:, b, jc * 17:(jc + 1) * 17], bv[:, c, jc * 128:(jc + 1) * 128], cs_bf[:, c, :],
                       c == 0, c == 7, [buf.name, "cs_bf"], [PB(b)])
            tt("dve", mod[:, pc * 4:(pc + 1) * 4, :], ps[:, b, 0:68].rearrange("p (a b) -> p a b", b=17),
               V("b_ada", pc * 4, 4).unsqueeze(2).broadcast_to([128, 4, 17]), ALU.add, [PB(b), "vec"], ["mod"])
        for m in (1, 4, 7):
            ts("dve", mod[:, m * 8:(m + 1) * 8, :], mod[:, m * 8:(m + 1) * 8, :], 1.0, None, ALU.add, None, ["mod"], ["mod"])
        for m in (2, 8):
            ts("dve", mod[:, m * 8:(m + 1) * 8, :], mod[:, m * 8:(m + 1) * 8, :], 0.25, None, ALU.mult, None, ["mod"], ["mod"])

        cvn = [0]

        def conv(out, in_, tok):
            k = "cv%d" % (cvn[0] % 8)
            cvn[0] += 1
            S.dma("pool", out, in_, reads=[("cvkey", k)], writes=[tok, ("cvkey", k)], key=k)

        def conv_ffn(f):
            for j in range(FC):
                conv(gu_s[f, j], gu_d[f, j], ("gu", f, j))
            for dc in range(8):
                conv(dn_s[f, dc], dn_d[f, dc], ("dn", f, dc))

        conv_ffn(0)
        for l in range(8):
            lo, hi = 3 * l, min(3 * l + 3, FC)
            conv(win_s[lo:hi], win_d[lo:hi], ("win", l))

        def conv_batch_b():
            nB = [0]

            def convb(out, in_, tok):
                S.dma("pool", out, in_, writes=[tok], key="cvB%d" % nB[0])
                nB[0] += 1
            for l in range(3):
                lo, hi = 3 * l, min(3 * l + 3, 8)
                convb(wout_s[lo:hi], wout_d[lo:hi], ("wout", l))
            for j in range(FC):
                convb(gu_s[1, j], gu_d[1, j], ("gu", 1, j))
            for dc in range(8):
                convb(dn_s[1, dc], dn_d[1, dc], ("dn", 1, dc))

        tiles = []
        for i in range(4):
            tiles.append(TileCtx(i, i * TP, 1, TP, 64, False, i == 3))
        tiles.append(TileCtx(4, SEQ, NSEQ_S, TL_S, 8, True, False))
        tiles = [tiles[4]] + tiles[:4]
        if "pesync" in DBGF:
            SAME_ENGINE_SYNC["pe"] = True

        loads = []
        for tcx in tiles:
            loads += [("gu", 0, j) for j in range(FC)] + [("dn", 0, dc) for dc in range(8)]
            loads += [("win", l) for l in range(8)] + [("wout", l) for l in range(3)]
            loads += [("gu", 1, j) for j in range(FC)] + [("dn", 1, dc) for dc in range(8)]
        ld_state = {"issued": 0, "next": 0, "hold": None}

        def issue_load(i):
            kind = loads[i]
            s = slots[i % 3]
            tok = "slot%d" % (i % 3)
            if kind[0] == "gu":
                S.dma("sp", s[:, 0:2048], gu_s[kind[1], kind[2]], reads=[kind], writes=[tok], key="w%d" % (i % 3))
            elif kind[0] == "dn":
                S.dma("sp", s[:, 0:DFF], dn_s[kind[1], kind[2]], reads=[kind], writes=[tok], key="w%d" % (i % 3))
            else:
                src = win_s if kind[0] == "win" else wout_s
                tot = FC if kind[0] == "win" else 8
                lo, hi = 3 * kind[1], min(3 * kind[1] + 3, tot)
                S.dma("sp", s[:, 0:(hi - lo) * D].rearrange("p (j n) -> p j n", n=D),
                      src[lo:hi].rearrange("j p n -> p j n"), reads=[kind], writes=[tok], key="w%d" % (i % 3))

        def next_slot(expect):
            i = ld_state["next"]
            assert loads[i][:len(expect)] == expect, (loads[i], expect)
            base = i if ld_state["hold"] is None else ld_state["hold"]
            while ld_state["issued"] < min(base + 3, len(loads)):
                issue_load(ld_state["issued"])
                ld_state["issued"] += 1
            ld_state["next"] += 1
            return slots[i % 3], "slot%d" % (i % 3)

        def hid_view(tc):
            return big[:, 0:FC * TP // 2].bitcast(BF16).rearrange("p (j t) -> p j t", t=TP)

        def hid_tok(j):
            return ["big%d" % j]

        def xm_tok(jc):
            return ["big%d" % (2 * jc), "big%d" % (2 * jc + 1)]

        xm = big[:].rearrange("p (j t) -> p j t", t=TP)

        def s3(tc, ap2):
            return ap2.rearrange("p (s t) -> p s t", t=tc.tl)

        def modap(tc, m, c):
            if not tc.sample:
                return mod[:, m * 8 + c, 0:1]
            return mod[:, m * 8 + c, 1:17].unsqueeze(2).broadcast_to([128, NSEQ_S, TL_S])

        def rms_stats(tc, xs, nchunks, lhs, toks_in, eps_col):
            T = tc.T
            b = nb()
            for c in range(nchunks):
                k = c % 2
                act(sq[k][:, 0:T], xs(c), AF.Square, toks_in(c), ["sq%d" % k])
                mm(ps[:, b, 0:T], lhs, sq[k][:, 0:T], c == 0, c == nchunks - 1, ["sq%d" % k], [PB(b)])
            act(rstd[:, 0:T], ps[:, b, 0:T], AF.Sqrt, [PB(b), "cst"], ["rstd"], bias=cst[:, eps_col:eps_col + 1])
            S.op("dve", lambda e: e.reciprocal(out=rstd[:, 0:T], in_=rstd[:, 0:T]), ["rstd"], ["rstd"])

        def norm_mod(tc, m_sh, m_sc):
            T = tc.T
            rms_stats(tc, lambda c: xt[:, c, 0:T], 8, onesD_bf[:], lambda c: ["xt%d" % c], 0)
            for c in range(8):
                k = c % 2
                tt("dve", t1[k][:, 0:T], xt[:, c, 0:T], rstd[:, 0:T], ALU.mult, ["xt%d" % c, "rstd"], ["t1_%d" % k])
                if not tc.sample:
                    act(n_[:, c, 0:T], t1[k][:, 0:T], AF.Identity, ["t1_%d" % k, "mod"], ["n%d" % c],
                        bias=modap(tc, m_sh, c), scale=modap(tc, m_sc, c))
                else:
                    tt("dve", s3(tc, t1[k][:, 0:T]), s3(tc, t1[k][:, 0:T]), modap(tc, m_sc, c), ALU.mult, ["t1_%d" % k, "mod"], ["t1_%d" % k])
                    tt("dve", s3(tc, n_[:, c, 0:T]), s3(tc, t1[k][:, 0:T]), modap(tc, m_sh, c), ALU.add, ["t1_%d" % k, "mod"], ["n%d" % c])

        def gate_add(tc, c, psap, m_g, ptoks):
            T = tc.T
            if not tc.sample:
                stt("dve", xt[:, c, 0:T], psap, modap(tc, m_g, c), xt[:, c, 0:T], ALU.mult, ALU.add,
                    list(ptoks) + ["xt%d" % c, "mod"], ["xt%d" % c])
            else:
                tt("dve", s3(tc, tA[:, 0:T]), s3(tc, psap), modap(tc, m_g, c), ALU.mult, list(ptoks) + ["mod"], ["tA"])
                tt("dve", xt[:, c, 0:T], xt[:, c, 0:T], tA[:, 0:T], ALU.add, ["tA", "xt%d" % c], ["xt%d" % c])

        def ffn(tc, f, m_g):
            T = tc.T
            hid = hid_view(tc)
            for j in range(FC):
                sl, stok = next_slot(("gu", f, j))
                w = sl[:, 0:2048].rearrange("p (g c n) -> p g c n", g=2, c=8)
                bg, bu = nb(), nb()
                for g, b in ((0, bg), (1, bu)):
                    for c in range(8):
                        mm(ps[:, b, 0:T], w[:, g, c, :], n_[:, c, 0:T], c == 0, c == 7, [stok, "n%d" % c], [PB(b)])
                k = j % 2
                act(th[k][:, 0:T], ps[:, bg, 0:T], AF.Tanh, [PB(bg)], ["th%d" % k], scale=0.5)
                stt("dve", th[k][:, 0:T], th[k][:, 0:T], 1.0, ps[:, bg, 0:T], ALU.add, ALU.mult, ["th%d" % k, PB(bg)], ["th%d" % k])
                tt("dve", hid[:, j, 0:T], th[k][:, 0:T], ps[:, bu, 0:T], ALU.mult, ["th%d" % k, PB(bu)], hid_tok(j))
            for dc in range(8):
                sl, stok = next_slot(("dn", f, dc))
                w = sl[:, 0:DFF].rearrange("p (j n) -> p j n", n=128)
                b = nb()
                for j in range(FC):
                    mm(ps[:, b, 0:T], w[:, j, :], hid[:, j, 0:T], j == 0, j == FC - 1, [stok] + hid_tok(j), [PB(b)])
                gate_add(tc, dc, ps[:, b, 0:T], m_g, [PB(b)])

        def w_in_phase(tc):
            T = tc.T
            if tc.sample:
                conv_batch_b()
            nsq, tl = tc.nseq, tc.tl
            for l in range(8):
                sl, stok = next_slot(("win", l))
                lo, hi = 3 * l, min(3 * l + 3, FC)
                w = sl[:, 0:(hi - lo) * D].rearrange("p (j c n) -> p j c n", c=8, n=128)
                for jj in range(hi - lo):
                    jc = lo + jj
                    b = nb()
                    for c in range(8):
                        mm(ps[:, b, 0:T], w[:, jj, c, :], n_[:, c, 0:T], c == 0, c == 7, [stok, "n%d" % c], [PB(b)])
                    p3 = s3(tc, ps[:, b, 0:T])
                    if jc < 14:
                        k = jc % 2
                        tk = "t1_%d" % k
                        tmp3 = s3(tc, t1[k][:, 0:T])
                        ts("dve", tmp3[:, :, 1:tl], p3[:, :, 0:tl - 1], V("mu", jc), None, ALU.mult, None, [PB(b), "vec"], [tk])
                        prev0 = sshT[:, jc, :] if tc.sample else carry[:, jc, :]
                        ts("dve", tmp3[:, :, 0:1], prev0.unsqueeze(2), V("mu", jc), None, ALU.mult, None,
                           ["sshT", "carry", "vec"], [tk])
                        stt("dve", xm[:, jc, 0:T], ps[:, b, 0:T], V("onem_mu", jc), t1[k][:, 0:T], ALU.mult, ALU.add,
                            [PB(b), tk, "vec"], xm_tok(jc))
                        if not tc.sample:
                            cp("dve", carry[:, jc, :], ps[:, b, T - 1:T], [PB(b)], ["carry"])
                            if tc.last_prompt and "noshout" not in DBGF:
                                cp("dve", shout[:, jc, 0:1], ps[:, b, T - 1:T], [PB(b)], ["shout"])
                        else:
                            cp("dve", shout[:, jc, 1:17], p3[:, :, tl - 1], [PB(b)], ["shout"])
                        if jc == 13:
                            for _ in prep_gen(tc, 0):
                                pass
                    elif jc < 18:
                        cp("act", P1[:, jc - 14, 0:T], ps[:, b, 0:T], [PB(b)], ["P1"])
                    else:
                        cp("act", P2[:, jc - 18, 0:T], ps[:, b, 0:T], [PB(b)], ["P2"])

        SBK = 128
        pbf = sb("pbf", [128, 4, SBK], BF16)
        gCs = sb("gCs", [128, 4, 16], F32)

        def bc4(name, w=SBK):
            return V(name, 0, 4).unsqueeze(2).broadcast_to([128, 4, w])

        def v4(t2):
            return t2[:, 0:4 * SBK].rearrange("p (c n) -> p c n", n=SBK)

        def gmlp_gen(tc, qb):
            cols = slice(qb * SBK, (qb + 1) * SBK)
            pu, pv = P1, P2
            pvc, pvt = v4(t1[0]), "t1_0"
            tmix, tmt = v4(t1[1]), "t1_1"
            rstd_g = t1[1][:, 0:SBK]
            sqb = th[1][:, 0:256].bitcast(BF16).rearrange("p (c n) -> p c n", n=SBK)
            vn_bf = sq[1][:, 0:4 * SBK].rearrange("p (c n) -> p c n", n=SBK)
            b = nb()
            for c in range(4):
                mm(ps[:, b, 0:SBK], ones512_f[:], pv[:, c, cols], c == 0, c == 3, ["P2", "ones512_f"], [PB(b)])
            tt("dve", pvc, pv[:, :, cols], ps[:, b, 0:SBK].unsqueeze(1).broadcast_to([128, 4, SBK]), ALU.subtract, ["P2", PB(b)], [pvt])
            act(sqb, pvc, AF.Square, [pvt, "th1"], ["th1a"])
            yield
            b2 = nb()
            for c in range(4):
                mm(ps[:, b2, 0:SBK], ones512_bf[:], sqb[:, c, :], c == 0, c == 3, ["th1a", "ones512_bf"], [PB(b2)])
            act(rstd_g, ps[:, b2, 0:SBK], AF.Sqrt, [PB(b2), "cst"], [tmt], bias=cst[:, 1:2])
            S.op("dve", lambda e: e.reciprocal(out=rstd_g, in_=rstd_g), [tmt], [tmt])
            tt("pool", pvc, pvc, rstd_g.unsqueeze(1).broadcast_to([128, 4, SBK]), ALU.mult, [pvt, tmt], [pvt])
            yield
            for c in range(4):
                act(pvc[:, c, :], pvc[:, c, :], AF.Identity, [pvt, "vec"], [pvt], bias=V("ln_b", c), scale=V("ln_g", c))
            cp("act", vn_bf, pvc, [pvt], ["sq1"])
            if tc.sample:
                S.dma("sp", cv_o, pvc, reads=[pvt], key="cvo")
            yield
            bt_ = nb()
            pT = ps[:, bt_, 0:256].bitcast(BF16).rearrange("p (c n) -> p c n", n=128)
            for c in range(4):
                tr(pT[:, c, :], vn_bf[:, c, :], ["sq1"], [PB(bt_)])
            cp("act", vnT_sb[:], pT, [PB(bt_)], ["vnT_sb"])
            yield
            bm = nb()
            Wm = BsT if tc.sample else WsT
            Wtok = "BsT" if tc.sample else "WsT"
            for c in range(4):
                for hh in range(2):
                    mm(ps[hh * 64:(hh + 1) * 64, bm, c * 128:(c + 1) * 128], vnT_sb[:, c, hh * 64:(hh + 1) * 64],
                       Wm[:, 2 * c + hh, :], True, True, ["vnT_sb", Wtok], [PB(bm)])
            mixp = ps[:, bm, :].rearrange("p (c n) -> p c n", n=128)
            if tc.sample:
                bias_ap = bsb[:, :, 0:8].unsqueeze(2).broadcast_to([128, 4, 16, 8])
                tt("dve", tmix.rearrange("p c (s i) -> p c s i", i=8),
                   mixp.rearrange("p c (s i) -> p c s i", i=8), bias_ap, ALU.add, [PB(bm), "bsb"], [tmt])
            else:
                tt("dve", tmix, mixp, bsb[:], ALU.add, [PB(bm), "bsb"], [tmt])
            tt("pool", n_[:, 4:8, cols], tmix, pu[:, :, cols], ALU.mult, [tmt, "P1"], ["n4", "n5", "n6", "n7"])
            yield

        def prep_gen(tc, sbi):
            T, C = tc.T, tc.C
            cols = slice(sbi * SBK, (sbi + 1) * SBK)
            SB = "_%d" % sbi
            r4, k4, v4_ = xm[:, 0:4, cols], xm[:, 4:8, cols], xm[:, 8:12, cols]
            rtok = [t for c in range(4) for t in xm_tok(c)]
            ktok = [t for c in range(4) for t in xm_tok(4 + c)]
            vtok = [t for c in range(4) for t in xm_tok(8 + c)]
            wdad = xm[:, 12, cols]
            gd = xm[:, 13, cols]
            L, Aa, K, Gi = P4[:, :, 0:128], P4[:, :, 128:256], P4[:, :, 256:384], P4[:, :, 384:512]
            tX, tY = v4(tA), v4(tB)
            lin_bf, sg_bf = sq[0][:, 0:128], sq[0][:, 128:256]
            sqs = [sq[0][:, 256:384], sq[0][:, 384:512]]
            act(lin_bf[0:64, :], wdad[0:64, :], AF.Tanh, xm_tok(12) + ["sq0"], ["sq0a"])
            cp("act", lin_bf[64:128, :], wdad[64:128, :], xm_tok(12), ["sq0a"])
            act(tY[:, 0, :], gd, AF.Tanh, xm_tok(13), ["tB"], scale=0.5)
            ts("pool", sg_bf, tY[:, 0, :], 0.5, 0.5, ALU.mult, ALU.add, ["tB", "sq0"], ["sq0b"])
            yield
            bw, ba, bg = nb(), nb(), nb()
            for c in range(4):
                mm(ps[:, bw, c * 128:(c + 1) * 128], lora_bf[:, c * 128:(c + 1) * 128], lin_bf, True, True, ["lora_bf", "sq0a"], [PB(bw)])
            for c in range(4):
                mm(ps[:, ba, c * 128:(c + 1) * 128], lora_bf[:, 512 + c * 128:512 + (c + 1) * 128], lin_bf, True, True, ["lora_bf", "sq0a"], [PB(ba)])
            for c in range(4):
                mm(ps[:, bg, c * 128:(c + 1) * 128], lora_bf[:, 1024 + c * 128:1024 + (c + 1) * 128], sg_bf, True, True, ["lora_bf", "sq0b"], [PB(bg)])
            for c in range(4):
                act(tX[:, c, :], ps[:, bw, c * 128:(c + 1) * 128], AF.Tanh, [PB(bw), "vec"], ["tA"], bias=V("half_w0", c), scale=0.5)
            ts("pool", L, tX, -0.5 * W_C, -0.5 * W_C, ALU.mult, ALU.add, ["tA"], ["P4a"])
            for c in range(4):
                act(tY[:, c, :], ps[:, ba, c * 128:(c + 1) * 128], AF.Tanh, [PB(ba), "vec"], ["tB"], bias=V("half_a0", c), scale=0.5)
            ts("pool", Aa, tY, 0.5, 0.5, ALU.mult, ALU.add, ["tB"], ["P4b"])
            cp("act", G_[:, :, cols], ps[:, bg, :].rearrange("p (c n) -> p c n", n=128), [PB(bg)], ["G_" + SB])
            yield
            tt("pool", K, k4, bc4("k_k"), ALU.mult, ktok + ["vec"], ["P4c"])
            bk = nb()
            for c in range(4):
                act(sqs[c % 2], K[:, c, :], AF.Square, ["P4c", "sq0"], ["sq0%s" % "cd"[c % 2]])
                mm(ps[:, bk, c * 128:(c + 1) * 128], blk64o_bf[:], sqs[c % 2], True, True, ["sq0%s" % "cd"[c % 2], "blk64o_bf"], [PB(bk)])
            act(tA[:, 0:512], ps[:, bk, :], AF.Sqrt, [PB(bk), "cst"], ["tA"], bias=cst[:, 3:4])
            S.op("dve", lambda e: e.reciprocal(out=tA[:, 0:512], in_=tA[:, 0:512]), ["tA"], ["tA"])
            tt("pool", K, K, tX, ALU.mult, ["P4c", "tA"], ["P4c"])
            yield
            tt("pool", tY, Aa, bc4("k_a"), ALU.mult, ["P4b", "vec"], ["tB"])
            tt("pool", tY, tY, bc4("onem_ka"), ALU.add, ["tB", "vec"], ["tB"])
            tt("dve", k4, k4, tY, ALU.mult, ktok + ["tB"], ktok)
            yield
            rm = rm8 if tc.sample else rm64
            rmt = "rm8" if tc.sample else "rm64"
            for c in range(4):
                S.op("dve", lambda e, c=c: e.tensor_tensor_scan(out=tX[:, c, :], data0=rm[:, 0:SBK], data1=L[:, c, :],
                                                                 initial=0.0, op0=ALU.mult, op1=ALU.add),
                     [rmt, "P4a"], ["tA"])
            act(Gi, tX, AF.Exp, ["tA"], ["P4d"])
            tt("pool", tY, tX, L, ALU.subtract, ["tA", "P4a"], ["tB"])
            act(tY, tY, AF.Exp, ["tB"], ["tB"])
            act(tX, tX, AF.Exp, ["tA"], ["tA"], scale=-1.0)
            nqs = SBK // C
            q0 = sbi * nqs
            cp("pool", gCs[:, :, q0:q0 + nqs], Gi.rearrange("p c (q t) -> p c q t", t=C)[:, :, :, C - 1],
               ["P4d"], ["gCs" + SB])
            yield
            arv = ar[:, :, 0:2 * T].rearrange("p c (q x t) -> p c q x t", x=2, t=C)
            q4 = lambda ap3: ap3.rearrange("p c (q t) -> p c q t", t=C)
            for c in range(4):
                stt("dve", arv[:, c, q0:q0 + nqs, 0, :], K[:, c, :].rearrange("p (q t) -> p q t", t=C), -1.0,
                    tY[:, c, :].rearrange("p (q t) -> p q t", t=C), ALU.mult, ALU.mult, ["P4c", "tB"], ["ar" + SB])
                tt("pool", arv[:, c, q0:q0 + nqs, 1, :], xm[:, c, cols].rearrange("p (q t) -> p q t", t=C),
                   Gi[:, c, :].rearrange("p (q t) -> p q t", t=C), ALU.mult, xm_tok(c) + ["P4d"], ["ar" + SB])
            yield
            tt("pool", tY, K, Aa, ALU.mult, ["P4c", "P4b"], ["tB"])
            tt("dve", bt[:, :, cols], tY, tX, ALU.mult, ["tB", "tA"], ["bt" + SB])
            tt("pool", kt[:, :, cols], k4, tX, ALU.mult, ktok + ["tA"], ["kt" + SB])
            cp("act", vb[:, :, cols], v4_, vtok, ["vb" + SB])
            yield

        def post_gen(tc, sbi):
            cols = slice(sbi * SBK, (sbi + 1) * SBK)
            SB = "_%d" % sbi
            y = P3[:, :, cols]
            yt = "P3" + SB
            pa = v4(th[0])
            sqb = th[1][:, 256:512].bitcast(BF16).rearrange("p (c n) -> p c n", n=SBK)
            rs = v4(rstd)
            rtok = [t for c in range(4) for t in xm_tok(c)]
            ktok = [t for c in range(4) for t in xm_tok(4 + c)]
            vtok = [t for c in range(4) for t in xm_tok(8 + c)]
            b = nb()
            for c in range(4):
                mm(ps[:, b, c * 128:(c + 1) * 128], blk64m_f[:], P3[:, c, cols], True, True, [yt, "blk64m_f"], [PB(b)])
            tt("dve", y, y, ps[:, b, :].rearrange("p (c n) -> p c n", n=128), ALU.subtract, [yt, PB(b)], [yt])
            act(sqb, y, AF.Square, [yt, "th1"], ["th1d"])
            yield
            b2 = nb()
            for c in range(4):
                mm(ps[:, b2, c * 128:(c + 1) * 128], blk64m_bf[:], sqb[:, c, :], True, True, ["th1d", "blk64m_bf"], [PB(b2)])
            act(rstd[:, 0:512], ps[:, b2, :], AF.Sqrt, [PB(b2), "cst"], ["rstd"], bias=cst[:, 2:3])
            S.op("dve", lambda e: e.reciprocal(out=rstd[:, 0:512], in_=rstd[:, 0:512]), ["rstd"], ["rstd"])
            tt("pool", y, y, rs, ALU.mult, [yt, "rstd"], [yt])
            yield
            for c in range(4):
                act(P3[:, c, cols], P3[:, c, cols], AF.Identity, [yt, "vec"], [yt], bias=V("gn_b", c), scale=V("gn_w", c))
            tt("pool", pa, xm[:, 0:4, cols], bc4("r_k"), ALU.mult, rtok + ["vec"], ["th0"])
            tt("pool", pbf[:], pa, xm[:, 4:8, cols], ALU.mult, ["th0"] + ktok, ["pbf"])
            yield
            b3 = nb()
            for c in range(4):
                mm(ps[:, b3, c * 128:(c + 1) * 128], blk64o_bf[:], pbf[:, c, :], True, True, ["pbf", "blk64o_bf"], [PB(b3)])
            tt("dve", pa, ps[:, b3, :].rearrange("p (c n) -> p c n", n=128), xm[:, 8:12, cols], ALU.mult, [PB(b3)] + vtok, ["th0"])
            tt("pool", y, y, pa, ALU.add, [yt, "th0"], [yt])
            tt("pool", n_[:, 0:4, cols], y, G_[:, :, cols], ALU.mult, [yt, "G_" + SB], ["n0", "n1", "n2", "n3"])
            yield

        def scan_gen(tc, sbi):
            T, C = tc.T, tc.C
            nq = T // C
            y_sb = P3
            SB = "_%d" % sbi
            arv = ar[:, :, 0:2 * T].rearrange("p c (q x t) -> p c q x t", x=2, t=C)
            nv = 4 if C == 64 else 1

            def chunk_gen(q, hg, mi, pb=0):
                G = "_g%d_s%d" % (hg, pb // 32)
                PS = slice(pb, pb + C)
                FS = slice(pb, pb + C)
                Mp, Mb = M_pad[mi], Mb_pad[mi]
                Mt, Mbt = "M_pad%d_g%d" % (mi, hg), "Mb_pad%d_g%d" % (mi, hg)
                cs_ = slice(q * C, (q + 1) * C)
                c0 = 2 * hg
                h0 = 4 * hg
                cl = (c0, c0 + 1)
                b1 = nb()
                pT = ps[PS, b1, 0:256].bitcast(BF16).rearrange("p (x c n) -> p x c n", x=2, n=128)
                pV = ps[PS, b1, 256:384].bitcast(BF16).rearrange("p (c n) -> p c n", n=128)
                for x, src, stok in ((0, bt, "bt" + SB), (1, kt, "kt" + SB)):
                    for ci, c in enumerate(cl):
                        tr(pT[:, x, ci, :], src[:, c, cs_], [stok], [PB(b1)])
                for ci, c in enumerate(cl):
                    tr(pV[:, ci, :], vb[:, c, cs_], ["vb" + SB], [PB(b1)])
                cp("act", bkT_sb[PS, :, c0:c0 + 2, :], pT, [PB(b1)], ["bkT_sb" + G])
                cp("act", vT_sb[PS, c0:c0 + 2, :], pV, [PB(b1)], ["vT_sb" + G])
                yield
                for hh in range(2):
                    hs = slice(hh * 64, (hh + 1) * 64)
                    b = nb()
                    pv5 = ps[PS, b, 0:2 * 4 * C].rearrange("p (h x y t) -> p h x y t", x=2, y=2, t=C)
                    for ci, c in enumerate(cl):
                        rhs = arv[hs, c, q].rearrange("p x t -> p (x t)")
                        for x, src, stok in ((0, bt, "bt" + SB), (1, kt, "kt" + SB)):
                            mm(pv5[:, ci, x].rearrange("p y t -> p (y t)"), src[hs, c, cs_], rhs, True, True, [stok, "ar" + SB], [PB(b)])
                    hsel = slice(h0 + hh, h0 + hh + 3, 2)
                    if C == 64:
                        tt("dve", Pbk_sb[PS, hsel, :, :, 0:C], pv5,
                           msk_ar[PS, :, FS].unsqueeze(1).unsqueeze(1).broadcast_to([C, 2, 2, 2, C]), ALU.mult,
                           [PB(b), "msk_ar"], ["Pbk_sb" + G])
                    else:
                        for x in range(2):
                            tt("dve", Pbk_sb[PS, hsel, x, :, 0:C], pv5[:, :, x],
                               msk_ar[PS, :, FS].unsqueeze(1).broadcast_to([C, 2, 2, C]), ALU.mult,
                               [PB(b), "msk_ar"], ["Pbk_sb" + G])
                    yield
                for hh in range(2):
                    hs = slice(hh * 64, (hh + 1) * 64)
                    b = nb()
                    pA = ps[PS, b, 0:2 * C].rearrange("p (h t) -> p h t", t=C)
                    for ci, c in enumerate(cl):
                        mm(pA[:, ci, :], arv[hs, c, q, 0, :], bt[hs, c, cs_], True, True, ["ar" + SB, "bt" + SB], [PB(b)])
                    tt("dve", A_all[PS, 0:nv, h0 + hh:h0 + hh + 3:2, 0:C], pA.unsqueeze(1).broadcast_to([C, nv, 2, C]),
                       msk4_A[PS, 0:nv, FS].unsqueeze(2).broadcast_to([C, nv, 2, C]), ALU.mult,
                       [PB(b), "msk4_A"], ["A_all" + G])
                    yield
                hr = slice(h0, h0 + 4)
                tt("pool", N_all[PS, 0:nv, hr, 0:C], Pbk_sb[PS, hr, 0, 0, 0:C].unsqueeze(1).broadcast_to([C, nv, 4, C]),
                   msk4_N[PS, 0:nv, FS].unsqueeze(2).broadcast_to([C, nv, 4, C]), ALU.mult,
                   ["Pbk_sb" + G, "msk4_N"], ["N_all" + G])
                idb = ident_bf[PS, FS].unsqueeze(1).broadcast_to([C, 4, C])
                tt("pool", Q_sb[0][PS, hr, 0:C], N_all[PS, 0, hr, 0:C], idb, ALU.add, ["N_all" + G, "ident_bf"], ["Q_sb0" + G])
                if nv > 1:
                    tt("pool", D_sb[0][PS, hr, 0:C], A_all[PS, 0, hr, 0:C], idb, ALU.add, ["A_all" + G, "ident_bf"], ["D_sb0" + G])
                yield

                def hmm(psv, lhs, ltok, rhs, rtok, bnk):
                    for hl in range(4):
                        mm(psv[:, hl, :], lhs[PS, h0 + hl, 0:C], rhs[PS, h0 + hl, 0:C], True, True, [ltok + G, rtok + G], [PB(bnk)])

                def pbank():
                    bnk = nb()
                    return bnk, ps[PS, bnk, 0:4 * C].rearrange("p (h t) -> p h t", t=C)

                Acur, Atok = A_all[:, 0], "A_all"
                Ncur, Ntok = N_all[:, 0], "N_all"
                cur = 0
                for lv in range(2):
                    nxt = 1 - cur
                    an, nn = A_sb[lv % 2], N_sb[lv % 2]
                    ant, nnt = "A_sb%d" % (lv % 2), "N_sb%d" % (lv % 2)
                    bA, pA2 = pbank()
                    hmm(pA2, Ncur, Ntok, Acur, Atok, bA)
                    needN = (nv > 1) or lv == 0
                    if needN:
                        bN, pN2 = pbank()
                        hmm(pN2, Acur, Atok, Ncur, Ntok, bN)
                    cp("act", an[PS, hr, 0:C], pA2, [PB(bA)], [ant + G])
                    if needN:
                        cp("dve", nn[PS, hr, 0:C], pN2, [PB(bN)], [nnt + G])
                    yield
                    bQ, pQ = pbank()
                    hmm(pQ, an, ant, Q_sb[cur], "Q_sb%d" % cur, bQ)
                    if nv > 1:
                        bD, pD = pbank()
                        hmm(pD, nn, nnt, D_sb[cur], "D_sb%d" % cur, bD)
                    tt("dve", Q_sb[nxt][PS, hr, 0:C], pQ, Q_sb[cur][PS, hr, 0:C], ALU.add, [PB(bQ), "Q_sb%d" % cur + G], ["Q_sb%d" % nxt + G])
                    if nv > 1:
                        tt("dve", D_sb[nxt][PS, hr, 0:C], pD, D_sb[cur][PS, hr, 0:C], ALU.add, [PB(bD), "D_sb%d" % cur + G], ["D_sb%d" % nxt + G])
                    yield
                    Acur, Atok, Ncur, Ntok = an, ant, nn, nnt
                    cur = nxt
                for v in range(1, nv):
                    nxt = 1 - cur
                    lastm = (v == nv - 1)
                    Qc, Qct, Dc, Dct = Q_sb[cur], "Q_sb%d" % cur, D_sb[cur], "D_sb%d" % cur
                    b2_, pX2 = pbank()
                    hmm(pX2, A_all[:, v], "A_all", Qc, Qct, b2_)
                    if not lastm:
                        b1_, pX1 = pbank()
                        hmm(pX1, N_all[:, v], "N_all", Dc, Dct, b1_)
                    cp("act", N_sb[0][PS, hr, 0:C], pX2, [PB(b2_)], ["N_sb0" + G])
                    if not lastm:
                        cp("dve", A_sb[0][PS, hr, 0:C], pX1, [PB(b1_)], ["A_sb0" + G])
                    yield
                    b3_, pY2 = pbank()
                    hmm(pY2, Dc, Dct, N_sb[0], "N_sb0", b3_)
                    if not lastm:
                        b4_, pY1 = pbank()
                        hmm(pY1, Qc, Qct, A_sb[0], "A_sb0", b4_)
                    tt("dve", Q_sb[nxt][PS, hr, 0:C], pY2, Qc[PS, hr, 0:C], ALU.add, [PB(b3_), Qct + G], ["Q_sb%d" % nxt + G])
                    if not lastm:
                        tt("dve", D_sb[nxt][PS, hr, 0:C], pY1, Dc[PS, hr, 0:C], ALU.add, [PB(b4_), Dct + G], ["D_sb%d" % nxt + G])
                    yield
                    cur = nxt
                Qf, Qt = Q_sb[cur], "Q_sb%d" % cur + G
                bX = nb()
                pX = ps[PS, bX, 0:256].rearrange("p (c n) -> p c n", n=128)
                for ci, c in enumerate(cl):
                    mm(pX[:, ci, :], arv[:, c, q, 0, :], Mb[:, c, :], True, False, ["ar" + SB, Mbt], [PB(bX)])
                    for hh in range(2):
                        hs = slice(hh * 64, (hh + 1) * 64)
                        mm(pX[:, ci, hs], Pbk_sb[PS, 2 * c + hh, 1, 0, 0:C], vT_sb[PS, c, hs], False, hh == 1,
                           ["Pbk_sb" + G, "vT_sb" + G], [PB(bX)])
                cp("act", Xs[PS, c0:c0 + 2, :], pX, [PB(bX)], ["Xs" + G])
                yield
                bU = nb()
                pU = ps[PS, bU, 0:256].rearrange("p (c n) -> p c n", n=128)
                for ci, c in enumerate(cl):
                    for hh in range(2):
                        hs = slice(hh * 64, (hh + 1) * 64)
                        mm(pU[:, ci, hs], Qf[PS, 2 * c + hh, 0:C], Xs[PS, c, hs], True, True, [Qt, "Xs" + G], [PB(bU)])
                cp("dve", Us[PS, c0:c0 + 2, :], pU, [PB(bU)], ["Us" + G])
                yield
                bY = nb()
                pY = ps[:, bY, 0:2 * C].rearrange("p (c t) -> p c t", t=C)
                for ci, c in enumerate(cl):
                    for hh in range(2):
                        h = 2 * c + hh
                        hs = slice(hh * 64, (hh + 1) * 64)
                        mm(pY[hs, ci, :], Mb[:, c, hs], arv[:, c, q, 1, :], True, False, [Mbt, "ar" + SB], [PB(bY)])
                        mm(pY[hs, ci, :], Us[PS, c, hs], Pbk_sb[PS, h, 0, 1, 0:C], False, False, ["Us" + G, "Pbk_sb" + G], [PB(bY)])
                        mm(pY[hs, ci, :], vT_sb[PS, c, hs], Pbk_sb[PS, h, 1, 1, 0:C], False, True, ["vT_sb" + G, "Pbk_sb" + G], [PB(bY)])
                bM = nb()
                pM = ps[:, bM, 0:256].rearrange("p (c n) -> p c n", n=128)
                for ci, c in enumerate(cl):
                    mm(pM[:, ci, :], bkT_sb[PS, 0, c, :], Us[PS, c, :], True, False, ["bkT_sb" + G, "Us" + G], [PB(bM)])
                    mm(pM[:, ci, :], bkT_sb[PS, 1, c, :], vT_sb[PS, c, :], False, True, ["bkT_sb" + G, "vT_sb" + G], [PB(bM)])
                cp("act", y_sb[:, c0:c0 + 2, cs_], pY, [PB(bY)], ["P3" + SB])
                tt("dve", Mp[:, c0:c0 + 2, :], Mp[:, c0:c0 + 2, :], pM, ALU.add, [Mt, PB(bM)], [Mt])
                gC = gCs[:, c0:c0 + 2, q:q + 1].broadcast_to([128, 2, 128])
                tt("dve", Mp[:, c0:c0 + 2, :], Mp[:, c0:c0 + 2, :], gC, ALU.mult, [Mt, "gCs" + SB], [Mt])
                last = tc.sample or (tc.last_prompt and q == nq - 1)
                if not last:
                    for hh in range(2):
                        hs = slice(hh * 64, (hh + 1) * 64)
                        cp("act", Mb[hs, c0:c0 + 2, hs], Mp[hs, c0:c0 + 2, hs], [Mt], [Mbt])
                else:
                    seq = (1 + q) if tc.sample else 0
                    for hh in range(2):
                        hs = slice(hh * 64, (hh + 1) * 64)
                        S.dma("sp", wkv_o[hs, seq, c0:c0 + 2], Mp[hs, c0:c0 + 2, hs], reads=[Mt], key="wk%d_%d" % (mi, hg))
                yield

            nqs = SBK // C
            step = 2 if tc.sample else 1
            for q0_ in range(sbi * nqs, (sbi + 1) * nqs, step):
                gens = []
                for q in range(q0_, q0_ + step):
                    mi = (q % NMB) if tc.sample else 0
                    pb = 32 * (q - q0_)
                    if tc.sample:
                        Mp, Mb = M_pad[mi], Mb_pad[mi]
                        mts = ["M_pad%d_g0" % mi, "M_pad%d_g1" % mi]
                        mbts = ["Mb_pad%d_g0" % mi, "Mb_pad%d_g1" % mi]
                        for hh in range(2):
                            S.dma("sp", Mp[hh * 64:(hh + 1) * 64, :, hh * 64:(hh + 1) * 64], wkv_d[hh * 64:(hh + 1) * 64, q],
                                  writes=mts, key="sl%d" % mi)
                        for hh in range(2):
                            cp("act", Mb[hh * 64:(hh + 1) * 64, :, hh * 64:(hh + 1) * 64],
                               Mp[hh * 64:(hh + 1) * 64, :, hh * 64:(hh + 1) * 64], mts, mbts)
                    gens += [chunk_gen(q, 0, mi, pb), chunk_gen(q, 1, mi, pb)]
                while gens:
                    for g in list(gens):
                        try:
                            next(g)
                        except StopIteration:
                            gens.remove(g)
                    yield

        def mixer_phase(tc):
            nsb = tc.T // SBK
            if (not tc.sample) and tc.idx == 0:
                memset("pool", M_pad[0][:], 0.0, ["M_pad0_g0", "M_pad0_g1"])
                memset("pool", Mb_pad[0][:], 0.0, ["Mb_pad0_g0", "Mb_pad0_g1"])

            def run(main, aux, ratio=3):
                aux = list(aux)
                alive = True
                while alive or aux:
                    if main is not None and alive:
                        for _ in range(ratio):
                            try:
                                next(main)
                            except StopIteration:
                                alive = False
                                break
                    else:
                        alive = False
                    for g in list(aux):
                        try:
                            next(g)
                        except StopIteration:
                            aux.remove(g)

            for k in range(nsb):
                aux = []
                if k + 1 < nsb:
                    aux.append(prep_gen(tc, k + 1))
                aux.append(gmlp_gen(tc, k))
                if k >= 1:
                    aux.append(post_gen(tc, k - 1))
                run(scan_gen(tc, k), aux)
            run(None, [post_gen(tc, nsb - 1)])

        def w_out_phase(tc):
            T = tc.T
            wv = []
            ld_state["hold"] = ld_state["next"]
            for l in range(3):
                sl, stok = next_slot(("wout", l))
                lo, hi = 3 * l, min(3 * l + 3, 8)
                w = sl[:, 0:(hi - lo) * D].rearrange("p (j n) -> p j n", n=D)
                for jj in range(hi - lo):
                    wv.append((w[:, jj, :], stok))
            for dc in range(8):
                b = nb()
                for kc in range(8):
                    mm(ps[:, b, 0:T], wv[kc][0][:, dc * 128:(dc + 1) * 128], n_[:, kc, 0:T], kc == 0, kc == 7,
                       [wv[kc][1], "n%d" % kc], [PB(b)])
                gate_add(tc, dc, ps[:, b, 0:T], 5, [PB(b)])
            ld_state["hold"] = None

        def final_norm(tc):
            T = tc.T
            rms_stats(tc, lambda c: xt[:, c, 0:T], 8, onesD_bf[:], lambda c: ["xt%d" % c], 0)
            yv = yT_o.rearrange("(c p) t -> p c t", p=128)
            for c in range(8):
                k = c % 2
                stt("dve", th[k][:, 0:T], xt[:, c, 0:T], V("final_g", c), rstd[:, 0:T], ALU.mult, ALU.mult,
                    ["xt%d" % c, "rstd", "vec"], ["th%d" % k])
                S.dma("act", yv[:, c, tc.col0:tc.col0 + T], th[k][:, 0:T], reads=["th%d" % k], key="yo%d" % k)

        ones512_bf = sb("ones512_bf", [128, 128], BF16)
        memset("pool", ones512_bf[:], 1.0 / 512, ["ones512_bf"])

        xv = xT_d.rearrange("(c p) t -> p c t", p=128)
        phases = [lambda tc: norm_mod(tc, 0, 1), lambda tc: ffn(tc, 0, 2), lambda tc: norm_mod(tc, 3, 4), w_in_phase,
                  mixer_phase, w_out_phase, lambda tc: norm_mod(tc, 6, 7),
                  lambda tc: ffn(tc, 1, 8), final_norm]
        pcount = 0
        for tc in (tiles if stop is None or stop >= 0 else []):
            T = tc.T
            S.dma("act", xt[:, :, 0:T], xv[:, :, tc.col0:tc.col0 + T], writes=["xt%d" % c for c in range(8)], key="xin")
            for ph in phases:
                if stop is not None and pcount >= stop:
                    break
                ph(tc)
                pcount += 1
            if stop is not None and pcount >= stop:
                break
        S.dma("sp", sh_o, shout[:], reads=["shout"], key="sho")
        fk = ["yo0", "yo1", "sho", "cvo", "dbg"] + ["wk%d_%d" % (i, g) for i in range(NMB) for g in range(2)]
        S.emit(final_keys=fk)
    return nc


def _chunkvec(v):
    v = np.asarray(v, np.float32).reshape(-1)
    return np.ascontiguousarray(v.reshape(-1, 128).T)


def kernel(x_prompt, x_sample, state_shift, state_wkv, c_prompt, c_sample, w_ada, b_ada, ffn1_gu,
           ffn1_dn, w_in, mu_shift, w0, w_lora_up, a0, a_lora_up, g_lora_up, k_k, k_a, r_k, gn_w,
           gn_b, ln_v_g, ln_v_b, w_s, b_s, w_out, ffn2_gu, ffn2_dn, final_g):
    f = lambda a: np.asarray(a, np.float32)
    x_prompt, x_sample, state_shift, state_wkv = f(x_prompt), f(x_sample), f(state_shift), f(state_wkv)
    c_prompt, c_sample = f(c_prompt), f(c_sample)
    vecs = np.concatenate([_chunkvec(v) for v in (b_ada[0], mu_shift[0], w0[0], a0[0], k_k[0], k_a[0], f(r_k[0]).reshape(-1),
                                                   gn_w[0], gn_b[0], ln_v_g[0], ln_v_b[0], final_g)], axis=1)
    assert vecs.shape == (128, NV)

    def gu_layout(w):
        w = f(w).reshape(8, 128, 2, FC, 128)
        return np.ascontiguousarray(w.transpose(3, 1, 2, 0, 4)).reshape(FC, 128, 2048)

    def dn_layout(w):
        w = f(w).reshape(FC, 128, 8, 128)
        return np.ascontiguousarray(w.transpose(2, 1, 0, 3)).reshape(8, 128, DFF)

    gu_h = np.stack([gu_layout(ffn1_gu[0]), gu_layout(ffn2_gu[0])])
    dn_h = np.stack([dn_layout(ffn1_dn[0]), dn_layout(ffn2_dn[0])])
    win_h = np.ascontiguousarray(f(w_in[0]).reshape(8, 128, FC, 128).transpose(2, 1, 0, 3)).reshape(FC, 128, D)
    wout_h = np.ascontiguousarray(f(w_out[0]).reshape(8, 128, D))
    lora_h = np.zeros((128, 1536), np.float32)
    lora_h[0:64, 0:512] = f(w_lora_up[0])
    lora_h[64:128, 512:1024] = f(a_lora_up[0])
    lora_h[:, 1024:1536] = f(g_lora_up[0])
    ws = f(w_s[0])
    wsT_h = np.ascontiguousarray(ws.transpose(2, 0, 1))
    rep_h = np.ascontiguousarray(np.tile(ws[:, 0:8, 0:8].transpose(2, 0, 1), (16, 1, 1)))
    bs = f(b_s[0])
    bsb_h = np.ascontiguousarray(np.repeat(bs.reshape(4, 2, 1, 128), 64, axis=2).transpose(1, 2, 0, 3)).reshape(128, 4, 128)
    wada = np.ascontiguousarray(f(w_ada[0]))

    in_maps = []
    for i in range(NCORES):
        ss = slice(NSEQ_S * i, NSEQ_S * (i + 1))
        xT = np.ascontiguousarray(np.concatenate([x_prompt[i].T, x_sample[ss].reshape(TS, D).T], axis=1))
        call = np.concatenate([c_prompt[i:i + 1], c_sample[ss]], axis=0)
        cT_h = np.ascontiguousarray(call.reshape(17, 8, 128).transpose(2, 1, 0))
        sshT_h = np.ascontiguousarray(state_shift[0, ss].reshape(NSEQ_S, 14, 128).transpose(2, 1, 0))
        wk = state_wkv[0, ss].reshape(NSEQ_S, 4, 2, 64, 64)
        wkv_h = np.ascontiguousarray(wk.transpose(2, 4, 0, 1, 3)).reshape(128, NSEQ_S, 4, 64)
        in_maps.append({"xT": xT, "cT": cT_h, "vecs": vecs, "w_ada": wada, "gu_h": gu_h, "dn_h": dn_h,
                        "win_h": win_h, "wout_h": wout_h, "lora_h": lora_h, "wsT_h": wsT_h, "rep_h": rep_h,
                        "bsb_h": bsb_h, "sshT_h": sshT_h, "wkv_h": wkv_h})
    nc = build_program()
    res = run_bass_kernel_spmd(nc, in_maps, core_ids=list(range(NCORES)))
    R = res.results
    y_prompt = np.zeros((8, SEQ, D), np.float32)
    y_sample = np.zeros((128, TL_S, D), np.float32)
    nsp = np.zeros((1, 8, R_COLS), np.float32)
    nwp = np.zeros((1, 8, 8, 64, 64), np.float32)
    nss = np.zeros((1, 128, R_COLS), np.float32)
    nws = np.zeros((1, 128, 8, 64, 64), np.float32)
    ncv = np.zeros((1, 128, TL_S, 512), np.float32)
    for i in range(NCORES):
        r = R[i]
        ss = slice(NSEQ_S * i, NSEQ_S * (i + 1))
        yT = np.asarray(r["yT"], np.float32)
        y_prompt[i] = yT[:, :SEQ].T
        y_sample[ss] = yT[:, SEQ:].T.reshape(NSEQ_S, TL_S, D)
        sh = np.asarray(r["shT_o"], np.float32)
        shf = sh.transpose(2, 1, 0).reshape(17, R_COLS)
        nsp[0, i] = shf[0]
        nss[0, ss] = shf[1:]
        wk = np.asarray(r["wkv_o"], np.float32).reshape(2, 64, 17, 4, 64)
        wk = wk.transpose(2, 3, 0, 4, 1).reshape(17, 8, 64, 64)
        nwp[0, i] = wk[0]
        nws[0, ss] = wk[1:]
        cv = np.asarray(r["cvT_o"], np.float32)
        ncv[0, ss] = cv.transpose(2, 1, 0).reshape(NSEQ_S, TL_S, 512)
    return (y_prompt, y_sample, nsp, nwp, nss, nws, ncv)
```
